# Optimizing a Trainium2 kernel written in Bass

```python
import math
import jax, jax.numpy as jnp
from jax import lax
import numpy as np

D_MODEL = 4096
BATCH = 2
SEQ = 8192
DEPTH = 1

HEAD_DIM = 128
D_MIX = D_MODEL
H_A = D_MIX // (2 * HEAD_DIM)
H_B = D_MIX // (2 * HEAD_DIM)
D_A = H_A * HEAD_DIM
D_B = H_B * HEAD_DIM
D_FF = 4 * D_MODEL
PLE_DIM = 256
Q_BLOCK = 128
CHUNK = 64
N_IN = 3 * D_A + H_A + 4 * D_B
RMS_EPS = 1e-6

kernel_name = "fox_hgrn2_parallel_hybrid_layer"


def rms_norm(x, g, eps=RMS_EPS):
    xf = x.astype(jnp.float32)
    y = xf * lax.rsqrt(jnp.mean(xf * xf, axis=-1, keepdims=True) + eps)
    return (y * g.astype(jnp.float32)).astype(x.dtype)


def split_columns(proj):
    sizes = (D_A, D_A, D_A, H_A, D_B, D_B, D_B, D_B)
    outs, start = [], 0
    for s in sizes:
        outs.append(proj[..., start:start + s])
        start += s
    return outs


def fox_attention(q, k, v, f_logit):
    B, S, H, Dh = q.shape
    nb = S // Q_BLOCK
    scale = 1.0 / math.sqrt(Dh)
    F = jnp.cumsum(jax.nn.log_sigmoid(f_logit.astype(jnp.float32)), axis=1).transpose(0, 2, 1)
    qh = q.transpose(0, 2, 1, 3)
    kh = k.transpose(0, 2, 1, 3)
    vh = v.transpose(0, 2, 1, 3)
    q_blocks = qh.reshape(B, H, nb, Q_BLOCK, Dh).transpose(2, 0, 1, 3, 4)
    F_blocks = F.reshape(B, H, nb, Q_BLOCK).transpose(2, 0, 1, 3)
    pos_k = jnp.arange(S)

    def one_block(args):
        blk, qi, Fi = args
        pos_q = blk * Q_BLOCK + jnp.arange(Q_BLOCK)
        s = jnp.einsum('bhqd,bhkd->bhqk', qi, kh, preferred_element_type=jnp.float32) * scale
        s = s + Fi[..., :, None] - F[..., None, :]
        s = jnp.where(pos_k[None, :] <= pos_q[:, None], s, -jnp.inf)
        pr = jax.nn.softmax(s, axis=-1)
        return jnp.einsum('bhqk,bhkd->bhqd', pr.astype(vh.dtype), vh)

    out = lax.map(one_block, (jnp.arange(nb), q_blocks, F_blocks))
    return out.transpose(1, 0, 3, 2, 4).reshape(B, S, H, Dh)


def hgrn2_recurrence(q, k, v, logf):
    B, S, H, Dk = q.shape
    Dv = v.shape[-1]
    nc = S // CHUNK

    def to_chunks(t):
        return t.reshape(B, nc, CHUNK, H, t.shape[-1]).transpose(1, 0, 3, 2, 4)

    causal = jnp.tril(jnp.ones((CHUNK, CHUNK), dtype=bool))

    def step(state, inp):
        qi, ki, vi, gi = inp
        b = jnp.cumsum(gi, axis=-2)
        o_inter = jnp.einsum('bhtk,bhkv->bhtv', qi * jnp.exp(b), state)
        diff = b[:, :, :, None, :] - b[:, :, None, :, :]
        decay = jnp.exp(jnp.where(causal[:, :, None], diff, -jnp.inf))
        A = jnp.einsum('bhtk,bhsk,bhtsk->bhts', qi, ki, decay)
        o_intra = jnp.einsum('bhts,bhsv->bhtv', A, vi)
        b_last = b[:, :, -1:, :]
        new_state = jnp.exp(b_last[:, :, 0, :])[..., None] * state + \
            jnp.einsum('bhsk,bhsv->bhkv', ki * jnp.exp(b_last - b), vi)
        return new_state, o_inter + o_intra

    s0 = jnp.zeros((B, H, Dk, Dv), jnp.float32)
    _, o = lax.scan(step, s0, (to_chunks(q), to_chunks(k), to_chunks(v), to_chunks(logf)))
    return o.transpose(1, 0, 3, 2, 4).reshape(B, S, H, Dv)


def setup_inputs(seed: int = 0) -> dict:
    key = jax.random.key(seed)
    ks = jax.random.split(key, 17)
    f32 = jnp.float32

    def nrm(k, shape, scale):
        return jax.random.normal(k, shape, f32) * scale

    return {
        "x": nrm(ks[0], (BATCH, SEQ, D_MODEL), 1.0),
        "p": nrm(ks[1], (DEPTH, BATCH, SEQ, PLE_DIM), 1.0),
        "norm_mix_g": 1.0 + nrm(ks[2], (DEPTH, D_MODEL), 0.02),
        "w_in": nrm(ks[3], (DEPTH, D_MODEL, N_IN), D_MODEL ** -0.5),
        "fox_f_bias": 1.0 + nrm(ks[4], (DEPTH, H_A), 0.1),
        "fox_q_norm_g": 1.0 + nrm(ks[5], (DEPTH, HEAD_DIM), 0.02),
        "fox_k_norm_g": 1.0 + nrm(ks[6], (DEPTH, HEAD_DIM), 0.02),
        "hgrn_lb_logits": nrm(ks[7], (DEPTH + 1, D_B), 0.1),
        "hgrn_norm_g": 1.0 + nrm(ks[8], (DEPTH, HEAD_DIM), 0.02),
        "w_out": nrm(ks[9], (DEPTH, D_MIX, D_MODEL), D_MIX ** -0.5),
        "norm_mlp_g": 1.0 + nrm(ks[10], (DEPTH, D_MODEL), 0.02),
        "w_up": nrm(ks[11], (DEPTH, D_MODEL, D_FF), D_MODEL ** -0.5),
        "w_down": nrm(ks[12], (DEPTH, D_FF, D_MODEL), D_FF ** -0.5),
        "ple_norm_g": 1.0 + nrm(ks[13], (DEPTH, D_MODEL), 0.02),
        "w_ple_gate": nrm(ks[14], (DEPTH, D_MODEL, D_MODEL), D_MODEL ** -0.5),
        "w_ple_proj": nrm(ks[15], (DEPTH, PLE_DIM, D_MODEL), PLE_DIM ** -0.5),
        "ple_post_g": 1.0 + nrm(ks[16], (DEPTH, D_MODEL), 0.02),
    }


def reference(x, p, norm_mix_g, w_in, fox_f_bias, fox_q_norm_g, fox_k_norm_g, hgrn_lb_logits,
              hgrn_norm_g, w_out, norm_mlp_g, w_up, w_down, ple_norm_g, w_ple_gate,
              w_ple_proj, ple_post_g):
    B, S, _ = x.shape
    h = x
    lower_bounds = jnp.cumsum(jax.nn.softmax(hgrn_lb_logits.astype(jnp.float32), axis=0), axis=0)
    for i in range(DEPTH):
        u = rms_norm(h, norm_mix_g[i])
        proj = jnp.einsum('bsd,dn->bsn', u, w_in[i])
        q_a, k_a, v_a, f_a, q_b, f_b, i_b, g_b = split_columns(proj)

        qa = rms_norm(q_a.reshape(B, S, H_A, HEAD_DIM), fox_q_norm_g[i])
        ka = rms_norm(k_a.reshape(B, S, H_A, HEAD_DIM), fox_k_norm_g[i])
        va = v_a.reshape(B, S, H_A, HEAD_DIM)
        out_a = fox_attention(qa, ka, va, f_a + fox_f_bias[i]).reshape(B, S, D_A)

        lb = lower_bounds[i].reshape(H_B, HEAD_DIM)
        z = f_b.astype(jnp.float32).reshape(B, S, H_B, HEAD_DIM)
        logf = jnp.log(lb + (1.0 - lb) * jax.nn.sigmoid(z))
        kb = (1.0 - lb) * jax.nn.sigmoid(-z)
        qb = jax.nn.silu(q_b.astype(jnp.float32)).reshape(B, S, H_B, HEAD_DIM)
        vb = i_b.astype(jnp.float32).reshape(B, S, H_B, HEAD_DIM)
        ob = hgrn2_recurrence(qb, kb, vb, logf)
        ob = rms_norm(ob, hgrn_norm_g[i]).astype(h.dtype).reshape(B, S, D_B)
        out_b = ob * jax.nn.silu(g_b)

        mixed = jnp.concatenate([out_a.astype(h.dtype), out_b], axis=-1)
        h = h + jnp.einsum('bsm,md->bsd', mixed, w_out[i])

        um = rms_norm(h, norm_mlp_g[i])
        hid = jnp.square(jax.nn.relu(jnp.einsum('bsd,df->bsf', um, w_up[i])))
        h = h + jnp.einsum('bsf,fd->bsd', hid, w_down[i])

        gate = jax.nn.sigmoid(jnp.einsum('bsd,de->bse', rms_norm(h, ple_norm_g[i]), w_ple_gate[i]))
        e = rms_norm(jnp.einsum('bsq,qd->bsd', p[i].astype(h.dtype), w_ple_proj[i]), ple_post_g[i])
        h = h + gate * e
    return h
```

```python
import math
from contextlib import ExitStack

import numpy as np
import ml_dtypes
import concourse.bass as bass
import concourse.mybir as mybir
from concourse.bass_utils import run_bass_kernel_spmd

F32 = mybir.dt.float32
BF16 = mybir.dt.bfloat16
AF = mybir.ActivationFunctionType
ALU = mybir.AluOpType

D = 4096
HD = 128
DFF = 16384
PLE = 256
EPS = 1e-6
NEG = -1.0e30
HG_OFFSET = 16


class Sched:
    ENGS = ("pe", "act", "dve", "pool", "sp")

    def __init__(self, nc, nsem=4, ndsem=8):
        self.nc = nc
        self.ops = []
        self.deps = []
        self.last_writer = {}
        self.readers = {}
        self.nsem = nsem
        self.ndsem = ndsem
        self.last_c = {}
        self.last_d = {e: [] for e in self.ENGS}
        self.excl = set()

    def add(self, eng, fn, reads=(), writes=(), kind="c", after=()):
        i = len(self.ops)
        d = set(after)
        if self.excl:
            ex = [s for s in reads if s in self.excl]
            if ex:
                reads = [s for s in reads if s not in self.excl]
                writes = list(writes) + ex
        for s in reads:
            w = self.last_writer.get(s)
            if w is not None:
                d.add(w)
        for s in writes:
            w = self.last_writer.get(s)
            if w is not None:
                d.add(w)
            d.update(self.readers.get(s, ()))
        for s in reads:
            self.readers.setdefault(s, []).append(i)
        for s in writes:
            self.last_writer[s] = i
            self.readers[s] = []
        d.discard(i)
        self.ops.append((eng, fn, kind))
        self.deps.append(d)
        if kind == "d":
            self.last_d[eng] = (self.last_d[eng] + [i])[-self.ndsem:]
        elif kind != "cc":
            self.last_c[eng] = i
        return i

    def dma(self, eng, out, in_, reads=(), writes=(), after=()):
        return self.add(eng, lambda e: e.dma_start(out=out, in_=in_), reads, writes, kind="d", after=after)

    def barrier(self):
        L = set(self.last_c.values())
        for e in self.ENGS:
            L.update(self.last_d[e])
        for e in self.ENGS:
            i = self.add(e, None, (), (), kind="n")
            self.deps[i] = set(x for x in L if x != i)
        self.last_writer = {}
        self.readers = {}

    def emit(self, stack):
        nc = self.nc
        ops, deps = self.ops, self.deps
        n = len(ops)
        has_dep = [False] * n
        for i in range(n):
            for d in deps[i]:
                if ops[d][0] == "pe" and ops[i][0] == "pe" and ops[d][2] == "c" and ops[i][2] == "c":
                    continue
                has_dep[d] = True
        csem = {e: [stack.enter_context(nc.semaphore(f"c_{e}_{k}")) for k in range(self.nsem)] for e in self.ENGS}
        dsem = {e: [stack.enter_context(nc.semaphore(f"d_{e}_{k}")) for k in range(self.ndsem)] for e in self.ENGS}
        sig = [None] * n
        ccount = {e: 0 for e in self.ENGS}
        dcount = {e: 0 for e in self.ENGS}
        throttle = [None] * n
        for i, (eng, fn, kind) in enumerate(ops):
            if kind == "d":
                j = dcount[eng]
                dcount[eng] += 1
                slot, rnd = j % self.ndsem, j // self.ndsem
                sig[i] = (dsem[eng][slot], 16 * (rnd + 1), "d", eng, j)
                if rnd > 0:
                    throttle[i] = (dsem[eng][slot], 16 * rnd)
            elif kind == "cc":
                ccsem = stack.enter_context(nc.semaphore(f"cc_{i}"))
                sig[i] = (ccsem, 1, "x", eng, 0)
            elif kind == "c" and has_dep[i]:
                k = ccount[eng]
                ccount[eng] += 1
                sig[i] = (csem[eng][k % self.nsem], k // self.nsem + 1, "c", eng, k)
        per_eng = {e: [] for e in self.ENGS}
        for i, (eng, fn, kind) in enumerate(ops):
            per_eng[eng].append(i)
        self.stats = {e: len(per_eng[e]) for e in self.ENGS}
        final_d = dict(dcount)
        cc_final = []

        def make_body(eng):
            idxs = per_eng[eng]

            def body(e):
                waited = {}
                known = {}

                def wait(sem, val):
                    key = id(sem)
                    if waited.get(key, 0) >= val:
                        return
                    e.wait_ge(sem, val)
                    waited[key] = val

                for i in idxs:
                    _, fn, kind = ops[i]
                    if throttle[i] is not None:
                        wait(*throttle[i])
                    cmax = {}
                    dmax = {}
                    for d in deps[i]:
                        s = sig[d]
                        if s is None:
                            continue
                        sem, val, skind, peng, k = s
                        if skind == "c":
                            if peng == "pe" and eng == "pe" and kind == "c":
                                continue
                            if k > cmax.get(peng, (-1,))[0]:
                                cmax[peng] = (k, sem, val)
                        else:
                            key = id(sem)
                            if val > dmax.get(key, (0,))[0]:
                                dmax[key] = (val, sem)
                    for peng, (k, sem, val) in cmax.items():
                        if known.get(peng, -1) >= k:
                            continue
                        wait(sem, val)
                        known[peng] = k
                    for key, (val, sem) in dmax.items():
                        wait(sem, val)
                    if fn is None:
                        continue
                    ins = fn(e)
                    s = sig[i]
                    if s is not None:
                        ins.then_inc(s[0], 16 if s[2] == "d" else 1)
                    if kind == "cc":
                        cc_final.append(s)
                if eng == "sp":
                    for q in self.ENGS:
                        cnt = final_d[q]
                        for slot in range(min(cnt, self.ndsem)):
                            uses = (cnt - slot + self.ndsem - 1) // self.ndsem
                            wait(dsem[q][slot], 16 * uses)

            return body

        block = stack.enter_context(nc.Block())
        block.tensor(make_body("pe"))
        block.scalar(make_body("act"))
        block.vector(make_body("dve"))
        block.gpsimd(make_body("pool"))
        block.sync(make_body("sp"))


def build_nc(S, dbg=()):
    QT = S // 4
    NBA = S // 512
    NBQ = QT // 512
    NKT = S // 128
    nc = bass.Bass("TRN2", target_bir_lowering=False)
    dt_in = lambda name, shape, dt=F32: nc.dram_tensor(name, list(shape), dt, kind="ExternalInput").ap()
    dt_sc = lambda name, shape, dt: nc.dram_tensor(name, list(shape), dt, kind="Internal").ap()

    xb = dt_in("xb", [S, D])
    xq = dt_in("xq", [QT, D])
    pq = dt_in("pq", [QT, PLE])
    w_in = dt_in("w_in", [D, 3588])
    w_out = dt_in("w_out", [D, D])
    w_up = dt_in("w_up", [D, DFF])
    w_down = dt_in("w_down", [DFF, D])
    w_gate = dt_in("w_gate", [D, D])
    w_pp = dt_in("w_pp", [PLE, D])
    g_rows = dt_in("g_rows", [4, D])
    cols = dt_in("cols", [128, 16])
    fbias = dt_in("fbias", [4, 1])
    ident_in = dt_in("ident", [128, 128], BF16)
    cmask_in = dt_in("cmask", [128, 4 * 512])
    hmask_in = dt_in("hmask", [128, 128])
    rmask_in = dt_in("rmask", [128, 512])
    sel4_in = dt_in("sel4", [4, 4 * 128])
    id4_in = dt_in("id4", [4, 4])
    out = nc.dram_tensor("out", [QT, D], F32, kind="ExternalOutput").ap()

    uT = dt_sc("uT", [S // 512, 128, 32, 512], BF16)
    qa_raw = dt_sc("qa_raw", [4, 128, S], F32)
    ka_raw = dt_sc("ka_raw", [4, 128, S], F32)
    qT = dt_sc("qT", [4, 128, S], BF16)
    kT = dt_sc("kT", [4, 128, S], BF16)
    va = dt_sc("va", [4, 128, S // 128, 128], BF16)
    fa_raw = dt_sc("fa_raw", [4, S], F32)
    qbT = dt_sc("qbT", [4, 128, S], F32)
    sgT = dt_sc("sgT", [4, 128, S], F32)
    gbT = dt_sc("gbT", [4, 128, S], F32)
    vb = dt_sc("vb", [4, 128, S // 128, 128], BF16)
    rs_a = dt_sc("rs_a", [4 * 2048, QT], BF16)
    rs_b = dt_sc("rs_b", [4 * 2048, QT], BF16)
    mT_a = dt_sc("mT_a", [2048, QT], BF16)
    mT_b = dt_sc("mT_b", [2048, QT], BF16)
    h1 = dt_sc("h1", [QT, D], F32)
    umT = dt_sc("umT", [QT // 512, 128, 32, 512], BF16)
    hidT = dt_sc("hidT", [QT // 512, 128, 128, 512], BF16)
    h2 = dt_sc("h2", [QT, D], F32)
    upT = dt_sc("upT", [QT // 512, 128, 32, 512], BF16)
    eS = dt_sc("eS", [QT, D], F32)
    dbg_out = {}
    dbg_src = {}
    for name in dbg:
        src = locals()[name]
        dbg_src[name] = src
        dbg_out[name] = nc.dram_tensor("dbg_" + name, list(src.shape), src.dtype, kind="ExternalOutput").ap()

    with ExitStack() as top:
        Sc = Sched(nc)
        add, dma = Sc.add, Sc.dma

        def mm(o, lhsT, rhs, start, stop, r, w):
            add("pe", lambda e: e.matmul(o, lhsT=lhsT, rhs=rhs, start=start, stop=stop), r, w)

        def tr(o, in_, ident, r, w):
            add("pe", lambda e: e.transpose(out=o, in_=in_, identity=ident), r, w)

        def act(o, in_, func, r, w, **kw):
            add("act", lambda e: e.activation(out=o, in_=in_, func=func, **kw), r, w)

        def dve(method, r, w, **kw):
            add("dve", lambda e: getattr(e, method)(**kw), r, w)

        uid = [0]

        def sbt(st, name, shape, dt):
            uid[0] += 1
            return st.enter_context(nc.sbuf_tensor(f"{name}_{uid[0]}", list(shape), dt))

        def pst(st, name, shape, dt):
            uid[0] += 1
            Sc.excl.add(name)
            return st.enter_context(nc.psum_tensor(f"{name}_{uid[0]}", list(shape), dt))
        IDN = sbt(top, "IDN", [128, 128], BF16)
        COLS = sbt(top, "COLS", [128, 16], F32)
        CD = sbt(top, "CD", [128, 24], F32)
        ONESF = sbt(top, "ONESF", [128, 128], F32)
        ONESB = sbt(top, "ONESB", [128, 128], BF16)
        zst = ExitStack()
        ZT = sbt(zst, "ZT", [128, 2048], BF16)
        dma("sp", IDN[:], ident_in, (), ["IDN"])
        dma("sp", COLS[:], cols, (), ["COLS"])
        add("dve", lambda e: e.memset(ONESF[:], 1.0 / 128), (), ["ONESF"])
        add("dve", lambda e: e.memset(ONESB[:], 1.0), (), ["ONESB"])
        add("dve", lambda e: e.memset(ZT[:], 0.0), (), ["ZT"])
        add("dve", lambda e: e.tensor_scalar_mul(out=CD[:, 0:1], in0=COLS[:, 0:1], scalar1=1.0 / math.sqrt(HD)), ["COLS"], ["CD0"])
        add("dve", lambda e: e.tensor_sub(out=CD[:, 13:17], in0=COLS[:, 3:7], in1=COLS[:, 7:11]), ["COLS"], ["CD13"])
        act(CD[:, 1:5], CD[:, 13:17], AF.Sigmoid, ["CD13"], ["CD1"])
        act(CD[:, 5:9], CD[:, 13:17], AF.Sigmoid, ["CD13"], ["CD5"], scale=-1.0)
        add("dve", lambda e: e.tensor_scalar_mul(out=CD[:, 9:13], in0=CD[:, 5:9], scalar1=-1.0), ["CD5"], ["CD9"])
        zw = min(QT, 2048)
        for rs_ in (rs_a, rs_b):
            for i in range(4 * 2048 // 128):
                for j in range(QT // zw):
                    dma("sp", rs_[i * 128:(i + 1) * 128, j * zw:(j + 1) * zw], ZT[:, 0:zw], ["ZT"], [])
        Sc.barrier()
        zst.close()

        def norm_transpose(src, dstT, M, grow, tag):
            with ExitStack() as st:
                X = [sbt(st, f"X{i}", [128, D], F32) for i in range(4)]
                GB = sbt(st, "GB", [128, D], F32)
                JK = sbt(st, "JK", [128, D], BF16)
                U = [sbt(st, f"U{i}", [128, D], BF16) for i in range(4)]
                UT = [sbt(st, f"UT{i}", [128, 32, 512], BF16) for i in range(2)]
                SSq = [sbt(st, f"SS{i}", [128, 4], F32) for i in range(4)]
                PB = [pst(st, f"PB{i}", [128, 1024], BF16) for i in range(4)]
                dma("sp", GB[:], g_rows[grow:grow + 1, :].partition_broadcast(128), (), ["GB"])
                nt = M // 128
                for t_ in range(min(3, nt)):
                    dma("sp", X[t_][:], src[t_ * 128:(t_ + 1) * 128, :], (), [f"X{t_}"])

                def stats(t):
                    i = t % 4
                    act(JK[:], X[i][:], AF.Square, [f"X{i}"], ["JK", f"SSa{i}"], accum_out=SSq[i][:, 0:1])
                    act(SSq[i][:, 1:2], SSq[i][:, 0:1], AF.Sqrt, [f"SSa{i}"], [f"SSb{i}"], scale=1.0 / D, bias=EPS)

                def recip(t):
                    i = t % 4
                    dve("reciprocal", [f"SSb{i}"], [f"SSc{i}"], out=SSq[i][:, 2:3], in_=SSq[i][:, 1:2])

                stats(0)
                recip(0)
                for t in range(nt):
                    i = t % 4
                    g, tt = t // 4, t % 4
                    gi = g % 2
                    if t + 3 < nt:
                        dma("sp", X[(t + 3) % 4][:], src[(t + 3) * 128:(t + 4) * 128, :], (), [f"X{(t + 3) % 4}"])
                    if t + 1 < nt:
                        stats(t + 1)
                    dve("scalar_tensor_tensor", [f"X{i}", f"SSc{i}", "GB"], [f"U{i}"], out=U[i][:], in0=X[i][:],
                        scalar=SSq[i][:, 2:3], in1=GB[:], op0=ALU.mult, op1=ALU.mult)
                    if t + 1 < nt:
                        recip(t + 1)
                    for q in range(4):
                        pb = (t * 4 + q) % 4
                        for j in range(8):
                            kc = q * 8 + j
                            tr(PB[pb][:, j * 128:(j + 1) * 128], U[i][:, kc * 128:(kc + 1) * 128], IDN[:],
                               [f"U{i}", "IDN"], [f"PB{pb}"])
                        o = UT[gi][:, q * 8:(q + 1) * 8, tt * 128:(tt + 1) * 128]
                        src_ps = PB[pb][:].rearrange("p (j m) -> p j m", j=8)
                        if q % 2 == 0:
                            add("act", lambda e, o=o, s_=src_ps: e.copy(out=o, in_=s_), [f"PB{pb}"], [f"UT{gi}"])
                        else:
                            add("dve", lambda e, o=o, s_=src_ps: e.tensor_copy(out=o, in_=s_), [f"PB{pb}"], [f"UT{gi}"])
                    if tt == 3:
                        dma("pool", dstT[g], UT[gi][:], [f"UT{gi}"], [])
            Sc.barrier()

        def gemm_pass(aT, M, w, panels, extra_alloc=None, KC=32, side=None, a_after=(), nps=6, side_from=0,
                      a_view=None, post_load=None):
            with ExitStack() as st:
                nblk = M // 512
                resident = nblk <= 4
                wmax = max(p_[1] for p_ in panels)
                WP = [sbt(st, f"WP{i}", [128, KC, (wmax + 7) // 8 * 8], BF16) for i in range(2)]
                AB = [sbt(st, f"AB{i}", [128, KC, 512], BF16) for i in range(nblk if resident else 3)]
                PS = [pst(st, f"PS{i}", [128, 512], F32) for i in range(nps)]
                ctx = extra_alloc(st) if extra_alloc else None
                sgen = side(st) if side else None
                psi = 0
                seq = [(pi, bi) for pi in range(len(panels)) for bi in range(nblk)]

                def load_w(pi):
                    c0, ncol, _, _ = panels[pi]
                    wi = pi % 2
                    ids = []
                    for kg in range(KC // 8):
                        ids.append(dma("pool", WP[wi][:, kg * 8:(kg + 1) * 8, 0:ncol],
                                       w[kg * 1024:(kg + 1) * 1024, c0:c0 + ncol].rearrange("(k p) n -> p k n", p=128),
                                       (), [f"WP{wi}_{kg}"]))
                    return ids

                def load_a(n):
                    pi, bi = seq[n]
                    ai = bi if resident else n % 3
                    src = a_view(bi) if a_view else aT[bi]
                    ids = []
                    for kg in range(KC // 8):
                        ids.append(dma("sp", AB[ai][:, kg * 8:(kg + 1) * 8, :], src[:, kg * 8:(kg + 1) * 8, :], (), [f"AB{ai}_{kg}"],
                                       after=a_after))
                    return ids

                def side_step(pi):
                    if sgen is not None and pi >= side_from:
                        next(sgen, None)

                init_ids = load_w(0)
                if resident:
                    for b_ in range(nblk):
                        init_ids += load_a(b_)
                else:
                    init_ids += load_a(0)
                    init_ids += load_a(1)
                if post_load:
                    post_load(init_ids)
                if hasattr(panels[0][3], "pre"):
                    panels[0][3].pre(ctx, 0, 0, 0)
                for n, (pi, bi) in enumerate(seq):
                    c0, ncol, orient, epi = panels[pi]
                    wi = pi % 2
                    ai = bi if resident else n % 3
                    if bi == 0 and pi + 1 < len(panels):
                        load_w(pi + 1)
                    if not resident and n + 2 < len(seq):
                        load_a(n + 2)
                    nsub = (ncol + 127) // 128 if orient == "F" else 4
                    for sub in range(nsub):
                        ps = PS[psi % nps]
                        pname = f"PS{psi % nps}"
                        psi += 1
                        if orient == "F":
                            cw = min(128, ncol - sub * 128)
                            for kc in range(KC):
                                mm(ps[0:cw, :], WP[wi][:, kc, sub * 128:sub * 128 + cw], AB[ai][:, kc, :], kc == 0, kc == KC - 1,
                                   [f"WP{wi}_{kc // 8}", f"AB{ai}_{kc // 8}"], [pname])
                        else:
                            for kc in range(KC):
                                mm(ps[:, 0:ncol], AB[ai][:, kc, sub * 128:(sub + 1) * 128], WP[wi][:, kc, 0:ncol], kc == 0, kc == KC - 1,
                                   [f"WP{wi}_{kc // 8}", f"AB{ai}_{kc // 8}"], [pname])
                        nxt = None
                        if sub + 1 < nsub:
                            nxt = (pi, sub + 1, bi)
                        elif n + 1 < len(seq):
                            nxt = (seq[n + 1][0], 0, seq[n + 1][1])
                        if nxt is not None and hasattr(panels[nxt[0]][3], "pre"):
                            panels[nxt[0]][3].pre(ctx, *nxt)
                        epi(ctx, ps, pname, pi, sub, bi)
                        side_step(pi)
                if sgen is not None:
                    for _ in sgen:
                        pass
            Sc.barrier()

        norm_transpose(xb, uT, S, 0, "A")

        def alloc_B(st):
            c = {}
            c["OF"] = [sbt(st, f"OF{i}", [128, 512], F32) for i in range(3)]
            c["OB"] = [sbt(st, f"OB{i}", [128, 512], BF16) for i in range(2)]
            c["n"] = 0
            return c

        def epi_F(dst, func, tag=None):
            def epi(c, ps, pname, pi, ch, bi):
                i = c["n"] % 3
                c["n"] += 1
                o = c["OF"][i]
                act(o[:], ps[:], func, [pname], [f"OF{i}"])
                dma("sp", dst[ch, :, bi * 512:(bi + 1) * 512], o[:], [f"OF{i}"], [f"{tag}_{ch}_{bi}"] if tag else [])
            return epi

        def epi_T16(dst):
            def epi(c, ps, pname, pi, tt, bi):
                i = c["n"] % 2
                c["n"] += 1
                o = c["OB"][i]
                dve("tensor_copy", [pname], [f"OB{i}"], out=o[:], in_=ps[:])
                kt = bi * 4 + tt
                dma("sp", dst[:, :, kt, :].rearrange("h p d -> p h d"), o[:].rearrange("p (h d) -> p h d", h=4), [f"OB{i}"], [])
            return epi

        def epi_fa(c, ps, pname, pi, ch, bi):
            i = c["n"] % 3
            c["n"] += 1
            o = c["OF"][i]
            act(o[0:4, :], ps[0:4, :], AF.Copy, [pname], [f"OF{i}"])
            dma("sp", fa_raw[:, bi * 512:(bi + 1) * 512], o[0:4, :], [f"OF{i}"], [])

        epi_qa = epi_F(qa_raw, AF.Copy, "qa")

        def epi_qa_fa(c, ps, pname, pi, ch, bi):
            (epi_fa if ch == 4 else epi_qa)(c, ps, pname, pi, ch, bi)

        panels_B = [
            (0, 516, "F", epi_qa_fa),
            (516, 512, "F", epi_F(ka_raw, AF.Copy, "ka")),
            (1028, 512, "T", epi_T16(va)),
            (2564, 512, "T", epi_T16(vb)),
            (1540, 512, "F", epi_F(qbT, AF.Silu)),
            (3076, 512, "F", epi_F(gbT, AF.Silu)),
            (2052, 512, "F", epi_F(sgT, AF.Sigmoid)),
        ]
        def c_side(st):
            RAW = [sbt(st, f"RAW{i}", [128, 512], F32) for i in range(3)]
            SQ = [sbt(st, f"SQ{i}", [128, 512], F32) for i in range(3)]
            SD = [sbt(st, f"SD{i}", [128, 512], F32) for i in range(3)]
            O16 = [sbt(st, f"O16{i}", [128, 512], BF16) for i in range(3)]
            PM = [pst(st, f"PM{i}", [128, 512], F32) for i in range(2)]
            items = [(src, dst, gcol, tag, h, bi) for (src, dst, gcol, tag) in
                     ((qa_raw, qT, CD[:, 0:1], "qa"), (ka_raw, kT, COLS[:, 1:2], "ka")) for h in range(4) for bi in range(NBA)]
            N = len(items)

            def sA(n):
                src, dst, gcol, tag, h, bi = items[n]
                i = n % 3
                dma("act", RAW[i][:], src[h, :, bi * 512:(bi + 1) * 512], [f"{tag}_{h}_{bi}"], [f"RAW{i}"])
                act(SQ[i][:], RAW[i][:], AF.Square, [f"RAW{i}"], [f"SQ{i}"])

            def sB(n):
                i, j = n % 3, n % 2
                mm(PM[j][:], ONESF[:], SQ[i][:], True, True, [f"SQ{i}"], [f"PM{j}"])

            def sC(n):
                src, dst, gcol, tag, h, bi = items[n]
                i, j = n % 3, n % 2
                act(SD[i][:], PM[j][:], AF.Sqrt, [f"PM{j}"], [f"SD{i}"], bias=EPS, scale=1.0)
                dve("reciprocal", [f"SD{i}"], [f"SD{i}"], out=SD[i][:], in_=SD[i][:])
                dve("scalar_tensor_tensor", [f"RAW{i}", f"SD{i}"], [f"O16{i}"], out=O16[i][:],
                    in0=RAW[i][:], scalar=gcol, in1=SD[i][:], op0=ALU.mult, op1=ALU.mult)
                dma("pool", dst[h, :, bi * 512:(bi + 1) * 512], O16[i][:], [f"O16{i}"], [])

            for k in range(N + 2):
                if k < N:
                    sA(k)
                if 0 <= k - 1 < N:
                    sB(k - 1)
                if 0 <= k - 2 < N:
                    sC(k - 2)
                yield

        gemm_pass(uT, S, w_in, panels_B, alloc_B, side=c_side, nps=5, side_from=2)

        with ExitStack() as st:
            HM = sbt(st, "HM", [128, 128], F32)
            RM = sbt(st, "RM", [128, 512], F32)
            per = lambda name, shape, dt: [sbt(st, f"{name}{i}", shape, dt) for i in range(4)]
            QB, SG, GG = per("QB", [128, 512], F32), per("SG", [128, 512], F32), per("GG", [128, 512], F32)
            VB = per("VB", [128, 4, 128], BF16)
            LOGF, KBt, Bt, EB, ENB, KE = (per(nm, [128, 512], F32) for nm in ("LOGF", "KBt", "Bt", "EB", "ENB", "KE"))
            QE16, KE16, KD16 = (per(nm, [128, 512], BF16) for nm in ("QE16", "KE16", "KD16"))
            KDT = per("KDT", [128, 4, 128], BF16)
            SQh, SDh, Yh = (per(nm, [128, 512], F32) for nm in ("SQh", "SDh", "Yh"))
            OSh = per("OSh", [128, 4, 512], BF16)
            ST = [[sbt(st, f"ST{h}_{j}", [128, 128], F32) for j in range(2)] for h in range(4)]
            S16 = [[sbt(st, f"S16{h}_{j}", [128, 128], BF16) for j in range(2)] for h in range(4)]
            ATM = per("ATM", [128, 128], BF16)
            PAT = pst(st, "PAT", [128, 512], F32)
            PUs = [pst(st, f"PU{i}", [128, 512], F32) for i in range(2)]
            PMS = pst(st, "PKM", [128, 512], F32)
            PKD = PMS[:].bitcast(BF16)
            POT = [pst(st, f"POT{i}", [128, 512], F32) for i in range(4)]
            dma("sp", HM[:], hmask_in, (), ["HM"])
            dma("sp", RM[:], rmask_in, (), ["RM"])
            for h in range(4):
                add("dve", lambda e, h=h: e.memset(ST[h][0][:], 0.0), (), [f"ST{h}_0"])
                add("dve", lambda e, h=h: e.memset(S16[h][0][:], 0.0), (), [f"S16{h}_0"])

            def group_gen(heads):
                cur = {h: 0 for h in heads}
                for bi in range(NBA):
                    sl = slice(bi * 512, (bi + 1) * 512)
                    for h in heads:
                        i = h
                        dma("sp", QB[i][:], qbT[h, :, sl], (), [f"QB{i}"])
                        dma("sp", SG[i][:], sgT[h, :, sl], (), [f"SG{i}"])
                        dma("sp", GG[i][:], gbT[h, :, sl], (), [f"GG{i}"])
                        dma("sp", VB[i][:], vb[h, :, bi * 4:(bi + 1) * 4, :], (), [f"VB{i}"])
                    yield
                    for h in heads:
                        i = h
                        act(LOGF[i][:], SG[i][:], AF.Ln, [f"SG{i}"], [f"LOGF{i}"], scale=CD[:, 5 + h:6 + h], bias=CD[:, 1 + h:2 + h])
                        act(KBt[i][:], SG[i][:], AF.Identity, [f"SG{i}"], [f"KBt{i}"], scale=CD[:, 9 + h:10 + h], bias=CD[:, 5 + h:6 + h])
                        yield
                        add("dve", lambda e, i=i: e.tensor_tensor_scan(out=Bt[i][:], data0=RM[:], data1=LOGF[i][:], initial=0.0,
                                                                       op0=ALU.mult, op1=ALU.add), ["RM", f"LOGF{i}"], [f"Bt{i}"])
                        act(EB[i][:], Bt[i][:], AF.Exp, [f"Bt{i}"], [f"EB{i}"])
                        act(ENB[i][:], Bt[i][:], AF.Exp, [f"Bt{i}"], [f"ENB{i}"], scale=-1.0)
                        yield
                        dve("tensor_tensor", [f"QB{i}", f"EB{i}"], [f"QE16{i}"], out=QE16[i][:], in0=QB[i][:], in1=EB[i][:], op=ALU.mult)
                        dve("tensor_tensor", [f"KBt{i}", f"ENB{i}"], [f"KE{i}"], out=KE[i][:], in0=KBt[i][:], in1=ENB[i][:], op=ALU.mult)
                        add("act", lambda e, i=i: e.copy(out=KE16[i][:], in_=KE[i][:]), [f"KE{i}"], [f"KE16{i}"])
                        yield
                        for c in range(8):
                            cs = slice(c * 64, (c + 1) * 64)
                            dve("tensor_scalar_mul", [f"KE{i}", f"EB{i}"], [f"KD16{i}_{c}"], out=KD16[i][:, cs], in0=KE[i][:, cs],
                                scalar1=EB[i][:, c * 64 + 63:c * 64 + 64])
                            if c == 3:
                                yield
                        yield
                        hp = h % 2
                        for tt in range(4):
                            tr(PKD[:, hp * 512 + tt * 128:hp * 512 + (tt + 1) * 128], KD16[i][:, tt * 128:(tt + 1) * 128], IDN[:],
                               [f"KD16{i}_{2 * tt}", f"KD16{i}_{2 * tt + 1}", "IDN"], ["PKM"])
                        add("act", lambda e, i=i, hp=hp: e.copy(out=KDT[i][:], in_=PKD[:, hp * 512:(hp + 1) * 512].rearrange("p (t d) -> p t d", t=4)),
                            ["PKM"], [f"KDT{i}"])
                        yield
                    for tt in range(4):
                        ts_ = slice(tt * 128, (tt + 1) * 128)
                        for h in heads:
                            i = h
                            pas = slice(h * 128, (h + 1) * 128)
                            mm(PAT[:, pas], KE16[i][:, ts_], QE16[i][:, ts_], True, True, [f"KE16{i}", f"QE16{i}"], ["PAT"])
                            dve("tensor_tensor", ["PAT", "HM"], [f"ATM{h}"], out=ATM[h][:], in0=PAT[:, pas], in1=HM[:], op=ALU.mult)
                        yield
                        for c in range(2):
                            tok = tt * 128 + c * 64
                            tk = slice(tok, tok + 64)
                            for h in heads:
                                i = h
                                pus = slice(h * 128, (h + 1) * 128)
                                cu = cur[h]
                                nx = 1 - cu
                                mm(POT[h][:, tk], VB[i][:, tt, :], ATM[h][:, c * 64:(c + 1) * 64], True, False,
                                   [f"VB{i}", f"ATM{h}"], [f"POT{h}"])
                                mm(POT[h][:, tk], S16[h][cu][:], QE16[i][:, tk], False, True, [f"S16{h}_{cu}", f"QE16{i}"], [f"POT{h}"])
                                mm(PUs[h % 2][:, pus], KDT[i][c * 64:(c + 1) * 64, tt, :], VB[i][c * 64:(c + 1) * 64, tt, :], True, True,
                                   [f"KDT{i}", f"VB{i}"], [f"PU{h % 2}"])
                                dve("scalar_tensor_tensor", [f"ST{h}_{cu}", f"EB{i}", f"PU{h % 2}"], [f"ST{h}_{nx}"], out=ST[h][nx][:],
                                    in0=ST[h][cu][:], scalar=EB[i][:, tok + 63:tok + 64], in1=PUs[h % 2][:, pus], op0=ALU.mult, op1=ALU.add)
                                add("act", lambda e, h=h, nx=nx: e.copy(out=S16[h][nx][:], in_=ST[h][nx][:]), [f"ST{h}_{nx}"], [f"S16{h}_{nx}"])
                                cur[h] = nx
                            yield
                    for h in heads:
                        i = h
                        act(SQh[i][:], POT[h][:], AF.Square, [f"POT{h}"], [f"SQh{i}"])
                        mm(PMS[:], ONESF[:], SQh[i][:], True, True, ["ONESF", f"SQh{i}"], ["PKM"])
                        act(SDh[i][:], PMS[:], AF.Sqrt, ["PKM"], [f"SDh{i}"], bias=EPS, scale=1.0)
                        yield
                        dve("reciprocal", [f"SDh{i}"], [f"SDh{i}"], out=SDh[i][:], in_=SDh[i][:])
                        dve("scalar_tensor_tensor", [f"POT{h}", f"SDh{i}"], [f"Yh{i}"], out=Yh[i][:], in0=POT[h][:],
                            scalar=COLS[:, 2:3], in1=SDh[i][:], op0=ALU.mult, op1=ALU.mult)
                        dve("tensor_tensor", [f"Yh{i}", f"GG{i}"], [f"Yh{i}"], out=Yh[i][:], in0=Yh[i][:], in1=GG[i][:], op=ALU.mult)
                        yield
                        for k in range(4):
                            add("pool", lambda e, i=i, k=k: e.tensor_scalar_mul(out=OSh[i][:, k, :], in0=Yh[i][:], scalar1=COLS[:, 11 + k:12 + k]),
                                [f"Yh{i}"], [f"OSh{i}_{k}"])
                        quarter, off = (bi * 512) // QT, (bi * 512) % QT
                        for k in range(4):
                            r0 = quarter * 2048 + k * 512 + h * 128
                            dma("sp", rs_b[r0:r0 + 128, off:off + 512], OSh[i][:, k, :], [f"OSh{i}_{k}"], [])
                        yield

            gA, gB = group_gen((0, 1)), group_gen((2, 3))
            for _ in range(HG_OFFSET):
                next(gA, None)
            doneA = doneB = False
            while not (doneA and doneB):
                if not doneA:
                    try:
                        next(gA)
                    except StopIteration:
                        doneA = True
                if not doneB:
                    try:
                        next(gB)
                    except StopIteration:
                        doneB = True
        Sc.barrier()

        RG = [[0, 1, 2, 3], [4, 5, 6, 7]]

        with ExitStack() as st:
            FROW = sbt(st, "FROW", [4, S], F32)
            NEGF = sbt(st, "NEGF", [128, NKT * 4], F32)
            SEL4 = sbt(st, "SEL4", [4, 512], F32)
            PF = pst(st, "PF", [128, 512], F32)
            with ExitStack() as st2:
                FTMP = sbt(st2, "FTMP", [4, S], F32)
                ONE4 = sbt(st2, "ONE4", [4, S], F32)
                NB4 = sbt(st2, "NB4", [4, 2], F32)
                ID4 = sbt(st2, "ID4", [4, 4], F32)
                dma("sp", FROW[:], fa_raw, (), ["FROW"])
                dma("sp", NB4[:, 0:1], fbias, (), ["NB4"])
                dma("sp", ID4[:], id4_in, (), ["ID4"])
                add("dve", lambda e: e.memset(ONE4[:], 1.0), (), ["ONE4"])
                add("dve", lambda e: e.tensor_scalar_mul(out=NB4[:, 1:2], in0=NB4[:, 0:1], scalar1=-1.0), ["NB4"], ["NB4b"])
                act(FTMP[:], FROW[:], AF.Exp, ["FROW", "NB4b"], ["FTMP"], bias=NB4[:, 1:2], scale=-1.0)
                act(FTMP[:], FTMP[:], AF.Ln, ["FTMP"], ["FTMP"], bias=1.0, scale=1.0)
                add("dve", lambda e: e.tensor_tensor_scan(out=FROW[:], data0=ONE4[:], data1=FTMP[:], initial=0.0,
                                                          op0=ALU.mult, op1=ALU.subtract), ["ONE4", "FTMP"], ["FROW"])
                for kt0 in range(0, NKT, 128):
                    nk = min(NKT, kt0 + 128) - kt0
                    for kt in range(kt0, kt0 + nk):
                        mm(PF[:, (kt - kt0) * 4:(kt - kt0) * 4 + 4], FROW[:, kt * 128:(kt + 1) * 128], ID4[:], True, True,
                           ["FROW", "ID4"], ["PF"])
                    add("act", lambda e, kt0=kt0, nk=nk: e.mul(out=NEGF[:, kt0 * 4:(kt0 + nk) * 4], in_=PF[:, 0:nk * 4], mul=-1.0),
                        ["PF"], ["NEGF"])
            Sc.barrier()
            CM = sbt(st, "CM", [128, 4, 512], F32)
            KTs = [sbt(st, f"KT{i}", [128, S], BF16) for i in range(2)]
            QTs = [sbt(st, f"QT{i}", [128, S], BF16) for i in range(2)]
            VT = [sbt(st, f"VT{i}", [128, NKT, 128], BF16) for i in range(2)]
            FQ = [sbt(st, f"FQ{i}", [128, 5, 512], F32) for i in range(2)]
            TMP = [sbt(st, f"TMP{i}", [128, 512], F32) for i in range(4)]
            PT = [sbt(st, f"PT{i}", [128, 512], BF16) for i in range(4)]
            RL = sbt(st, "RL", [128, 512], F32)
            OA = [sbt(st, f"OA{i}", [128, 512], F32) for i in range(2)]
            OS = [sbt(st, f"OS{i}", [128, 4, 512], BF16) for i in range(2)]
            PA = [pst(st, f"PA{i}", [128, 512], F32) for i in range(4)]
            PO = [pst(st, f"PO{i}", [128, 512], F32) for i in range(2)]
            PL = [pst(st, f"PL{i}", [128, 512], F32) for i in range(1)]
            dma("sp", SEL4[:], sel4_in, (), ["SEL4"])
            dma("sp", CM[:], cmask_in.rearrange("p (j n) -> p j n", j=4), (), ["CM"])

            LA = 3
            blocks = [(h, qb) for h in range(4) for qb in range(NBA)]
            tiles = []
            for n_, (h, qb) in enumerate(blocks):
                nkt = 4 * (qb + 1)
                for kt in range(nkt):
                    tiles.append((n_, h, qb, kt, nkt))

            def load_head(h):
                hi = h % 2
                return [dma("sp", KTs[hi][:], kT[h], (), [f"KT{hi}"]),
                        dma("sp", QTs[hi][:], qT[h], (), [f"QT{hi}"]),
                        dma("sp", VT[hi][:], va[h], (), [f"VT{hi}"])]

            def prologue(n_):
                h, qb = blocks[n_]
                qi = n_ % 2
                qs = slice(qb * 512, (qb + 1) * 512)
                mm(PF[:], SEL4[:, h * 128:(h + 1) * 128], FROW[:, qs], True, True, ["SEL4", "FROW"], ["PF"])
                add("act", lambda e: e.copy(out=FQ[qi][:, 0, :], in_=PF[:]), ["PF"], [f"FQ{qi}"])
                for d in range(4):
                    add("pool", lambda e, d=d: e.tensor_add(out=FQ[qi][:, 1 + d, :], in0=CM[:, d, :], in1=FQ[qi][:, 0, :]),
                        [f"FQ{qi}", "CM"], [f"FQm{qi}_{d}"])

            def front(i):
                n_, h, qb, kt, nkt = tiles[i]
                hi, qi, a = h % 2, n_ % 2, i % 4
                d = kt - 4 * qb
                mm(PA[a][:], KTs[hi][:, kt * 128:(kt + 1) * 128], QTs[hi][:, qb * 512:(qb + 1) * 512], True, True,
                   [f"KT{hi}", f"QT{hi}"], [f"PA{a}"])
                if d >= 0:
                    fq, fslot = FQ[qi][:, 1 + d, :], f"FQm{qi}_{d}"
                else:
                    fq, fslot = FQ[qi][:, 0, :], f"FQ{qi}"
                dve("tensor_tensor", [f"PA{a}", fslot], [f"TMP{a}"], out=TMP[a][:], in0=PA[a][:], in1=fq, op=ALU.add)
                act(PT[a][:], TMP[a][:], AF.Exp, [f"TMP{a}", "NEGF"], [f"PT{a}"],
                    bias=NEGF[:, kt * 4 + h:kt * 4 + h + 1], scale=1.0)

            def back(i):
                n_, h, qb, kt, nkt = tiles[i]
                hi, qi, a = h % 2, n_ % 2, i % 4
                mm(PO[qi][:], VT[hi][:, kt, :], PT[a][:], kt == 0, kt == nkt - 1, [f"VT{hi}", f"PT{a}"], [f"PO{qi}"])
                mm(PL[0][:], ONESB[:], PT[a][:], kt == 0, kt == nkt - 1, ["ONESB", f"PT{a}"], ["PL0"])
                if kt == nkt - 1:
                    dve("reciprocal", ["PL0"], ["RL"], out=RL[:], in_=PL[0][:])
                    dve("tensor_tensor", [f"PO{qi}", "RL"], [f"OA{qi}"], out=OA[qi][:], in0=PO[qi][:], in1=RL[:], op=ALU.mult)
                    for k in range(4):
                        add("pool", lambda e, k=k: e.tensor_scalar_mul(out=OS[qi][:, k, :], in0=OA[qi][:], scalar1=COLS[:, 11 + k:12 + k]),
                            [f"OA{qi}"], [f"OS{qi}_{k}"])
                    quarter, off = (qb * 512) // QT, (qb * 512) % QT
                    for k in range(4):
                        r0 = quarter * 2048 + k * 512 + h * 128
                        dma("sp", rs_a[r0:r0 + 128, off:off + 512], OS[qi][:, k, :], [f"OS{qi}_{k}"], [])

            hl = load_head(0) + load_head(1)
            rsb = add("pool", lambda e: e.collective_compute("ReduceScatter", ALU.add, replica_groups=RG, ins=[rs_b], outs=[mT_b], dma_qos="P3"),
                      (), (), kind="cc", after=hl)
            prologue(0)
            for i in range(len(tiles) + LA):
                if i < len(tiles):
                    n_, h, qb, kt, nkt = tiles[i]
                    if kt == 0 and n_ + 1 < len(blocks):
                        prologue(n_ + 1)
                    front(i)
                if i - LA >= 0:
                    back(i - LA)
                    n_, h, qb, kt, nkt = tiles[i - LA]
                    if qb == NBA - 1 and kt == nkt - 1 and h + 2 < 4:
                        load_head(h + 2)
        Sc.barrier()

        def alloc_res(st):
            c = {"XR": [sbt(st, f"XR{i}", [128, 512], F32) for i in range(2)],
                 "ER": [sbt(st, f"ER{i}", [128, 512], F32) for i in range(2)],
                 "OF": [sbt(st, f"OF{i}", [128, 512], F32) for i in range(2)],
                 "OB": [sbt(st, f"OB{i}", [128, 512], BF16) for i in range(2)], "n": 0}
            return c

        def epi_res(prev, dst):
            def pre(c, pi, tt, bi):
                i = c.setdefault("pn", 0) % 2
                c["pn"] += 1
                r0, c0 = bi * 512 + tt * 128, pi * 512
                dma("sp", c["XR"][i][:], prev[r0:r0 + 128, c0:c0 + 512], (), [f"XR{i}"])

            def epi(c, ps, pname, pi, tt, bi):
                i = c["n"] % 2
                c["n"] += 1
                r0, c0 = bi * 512 + tt * 128, pi * 512
                dve("tensor_tensor", [pname, f"XR{i}"], [f"OF{i}"], out=c["OF"][i][:], in0=ps[:], in1=c["XR"][i][:], op=ALU.add)
                dma("sp", dst[r0:r0 + 128, c0:c0 + 512], c["OF"][i][:], [f"OF{i}"], [])
            epi.pre = pre
            return epi

        def ple_side(st):
            WPP = sbt(st, "WPP", [128, 2, D], BF16)
            GB = sbt(st, "GBp", [128, D], F32)
            PQ = [sbt(st, f"PQ{i}", [128, PLE], F32) for i in range(2)]
            P16 = [sbt(st, f"P16{i}", [128, PLE], BF16) for i in range(2)]
            PTt = [sbt(st, f"PTt{i}", [128, 2, 128], BF16) for i in range(2)]
            EPs = [sbt(st, f"EP{i}", [128, D], F32) for i in range(2)]
            JK = sbt(st, "JKp", [128, D], BF16)
            SSq = [sbt(st, f"SSp{i}", [128, 4], F32) for i in range(2)]
            PSs = [pst(st, f"PSp{i}", [128, 512], F32) for i in range(2)]
            PBp = pst(st, "PBp", [128, 1024], BF16)
            dma("pool", WPP[:], w_pp.rearrange("(k p) n -> p k n", p=128), (), ["WPP"])
            dma("sp", GB[:], g_rows[3:4, :].partition_broadcast(128), (), ["GBp"])
            yield
            for t in range(QT // 128):
                i = t % 2
                EP = EPs[i]
                EPn = f"EP{i}"
                dma("act", PQ[i][:], pq[t * 128:(t + 1) * 128, :], (), [f"PQ{i}"])
                dve("tensor_copy", [f"PQ{i}"], [f"P16{i}"], out=P16[i][:], in_=PQ[i][:])
                yield
                for kc in range(2):
                    tr(PBp[:, kc * 128:(kc + 1) * 128], P16[i][:, kc * 128:(kc + 1) * 128], IDN[:], [f"P16{i}"], ["PBp"])
                add("act", lambda e, i=i: e.copy(out=PTt[i][:], in_=PBp[:, 0:256].rearrange("p (k m) -> p k m", k=2)), ["PBp"], [f"PTt{i}"])
                yield
                for nb in range(8):
                    p_ = nb % 2
                    for kc in range(2):
                        mm(PSs[p_][:], PTt[i][:, kc, :], WPP[:, kc, nb * 512:(nb + 1) * 512], kc == 0, kc == 1,
                           [f"PTt{i}", "WPP"], [f"PSp{p_}"])
                    add("act", lambda e, nb=nb, p_=p_, EP=EP: e.copy(out=EP[:, nb * 512:(nb + 1) * 512], in_=PSs[p_][:]),
                        [f"PSp{p_}"], [EPn])
                    if nb % 2 == 1:
                        yield
                act(JK[:], EP[:], AF.Square, [EPn], ["JKp", f"SSa{i}"], accum_out=SSq[i][:, 0:1])
                act(SSq[i][:, 1:2], SSq[i][:, 0:1], AF.Sqrt, [f"SSa{i}"], [f"SSb{i}"], scale=1.0 / D, bias=EPS)
                dve("reciprocal", [f"SSb{i}"], [f"SSc{i}"], out=SSq[i][:, 2:3], in_=SSq[i][:, 1:2])
                dve("scalar_tensor_tensor", [EPn, f"SSc{i}", "GBp"], [EPn], out=EP[:], in0=EP[:],
                    scalar=SSq[i][:, 2:3], in1=GB[:], op0=ALU.mult, op1=ALU.mult)
                dma("pool", eS[t * 128:(t + 1) * 128, :], EP[:], [EPn], [])
                yield

        mTb3 = mT_b.rearrange("(k p) m -> k p m", p=128)
        mTa3 = mT_a.rearrange("(k p) m -> k p m", p=128)
        rs_ops = {}

        def issue_rsa(init_ids):
            rs_ops["a"] = add("pool", lambda e: e.collective_compute("ReduceScatter", ALU.add, replica_groups=RG, ins=[rs_a],
                                                                      outs=[mT_a], dma_qos="P3"), (), (), kind="cc", after=init_ids)

        gemm_pass(mTb3, QT, w_out[2048:4096, :],
                  [(pi * 512, 512, "T", epi_res(xq, h1)) for pi in range(8)], alloc_res, KC=16, side=ple_side, a_after=[rsb], nps=5,
                  a_view=lambda bi: mTb3[:, :, bi * 512:(bi + 1) * 512].rearrange("k p m -> p k m"), post_load=issue_rsa)
        gemm_pass(mTa3, QT, w_out[0:2048, :],
                  [(pi * 512, 512, "T", epi_res(h1, h1)) for pi in range(8)], alloc_res, KC=16, a_after=[rs_ops["a"]],
                  a_view=lambda bi: mTa3[:, :, bi * 512:(bi + 1) * 512].rearrange("k p m -> p k m"))

        norm_transpose(h1, umT, QT, 1, "H")

        def epi_up(c, ps, pname, pi, ch, bi):
            i = c["n"] % 2
            c["n"] += 1
            act(c["OF"][i][:], ps[:], AF.Relu, [pname], [f"OF{i}"])
            dve("tensor_tensor", [f"OF{i}"], [f"OB{i}"], out=c["OB"][i][:], in0=c["OF"][i][:], in1=c["OF"][i][:], op=ALU.mult)
            dma("sp", hidT[bi, :, pi * 4 + ch, :], c["OB"][i][:], [f"OB{i}"], [])

        gemm_pass(umT, QT, w_up, [(pi * 512, 512, "F", epi_up) for pi in range(32)], alloc_res)
        for q in range(4):
            gemm_pass(hidT, QT, w_down[q * D:(q + 1) * D, :],
                      [(pi * 512, 512, "T", epi_res(h1 if q == 0 else h2, h2)) for pi in range(8)], alloc_res,
                      a_view=lambda bi, q=q: hidT[bi, :, q * 32:(q + 1) * 32, :])

        norm_transpose(h2, upT, QT, 2, "K")

        def epi_gate(c, ps, pname, pi, tt, bi):
            i = c["n"] % 2
            c["n"] += 1
            r0, c0 = bi * 512 + tt * 128, pi * 512
            dma("sp", c["XR"][i][:], h2[r0:r0 + 128, c0:c0 + 512], (), [f"XR{i}"])
            dma("sp", c["ER"][i][:], eS[r0:r0 + 128, c0:c0 + 512], (), [f"ER{i}"])
            act(c["OF"][i][:], ps[:], AF.Sigmoid, [pname], [f"OF{i}"])
            dve("tensor_tensor", [f"OF{i}", f"ER{i}"], [f"OF{i}"], out=c["OF"][i][:], in0=c["OF"][i][:], in1=c["ER"][i][:], op=ALU.mult)
            dve("tensor_tensor", [f"OF{i}", f"XR{i}"], [f"OF{i}"], out=c["OF"][i][:], in0=c["OF"][i][:], in1=c["XR"][i][:], op=ALU.add)
            dma("sp", out[r0:r0 + 128, c0:c0 + 512], c["OF"][i][:], [f"OF{i}"], [])

        gemm_pass(upT, QT, w_gate, [(pi * 512, 512, "T", epi_gate) for pi in range(8)], alloc_res)

        for name, dst in dbg_out.items():
            src = dbg_src[name]
            dma("sp", dst, src, (), [])
        Sc.emit(top)
        print("ops per engine:", Sc.stats, flush=True)
    return nc


def host_consts():
    ident = np.eye(128, dtype=np.float32).astype(ml_dtypes.bfloat16)
    p = np.arange(128)[:, None]
    j = np.arange(512)[None, :]
    cm = np.zeros((128, 4, 512), np.float32)
    for d in range(4):
        cm[:, d, :] = np.where(j >= 128 * d + p, 0.0, NEG)
    s = np.arange(128)[:, None]
    t = np.arange(128)[None, :]
    hm = ((s // 64 == t // 64) & (s <= t)).astype(np.float32)
    rm = np.ones((128, 512), np.float32)
    rm[:, ::64] = 0.0
    sel4 = np.zeros((4, 4, 128), np.float32)
    for h in range(4):
        sel4[h, h, :] = 1.0
    return {"ident": ident, "cmask": cm.reshape(128, 2048), "hmask": hm, "rmask": rm,
            "sel4": sel4.reshape(4, 512), "id4": np.eye(4, dtype=np.float32)}


def make_in_maps(inp, S):
    QT = S // 4
    f = lambda a: np.ascontiguousarray(np.asarray(a, dtype=np.float32))
    x, p = f(inp["x"]), f(inp["p"])
    w_in = f(inp["w_in"])[0]
    consts = host_consts()
    g_rows = np.stack([f(inp["norm_mix_g"])[0], f(inp["norm_mlp_g"])[0], f(inp["ple_norm_g"])[0], f(inp["ple_post_g"])[0]])
    lbl = f(inp["hgrn_lb_logits"])
    shared = {"w_out": f(inp["w_out"])[0], "w_up": f(inp["w_up"])[0], "w_down": f(inp["w_down"])[0],
              "w_gate": f(inp["w_ple_gate"])[0], "w_pp": f(inp["w_ple_proj"])[0], "g_rows": g_rows}
    shared.update(consts)
    w_in_r = []
    for r in range(4):
        sl = lambda o: w_in[:, o + r * 512:o + (r + 1) * 512]
        w_in_r.append(np.ascontiguousarray(np.concatenate(
            [sl(0), w_in[:, 6144 + 4 * r:6144 + 4 * r + 4], sl(2048), sl(4096), sl(6160), sl(8208), sl(10256), sl(12304)], axis=1)))
    maps = []
    for c in range(8):
        b, r = c // 4, c % 4
        cols = np.zeros((128, 16), np.float32)
        cols[:, 0] = f(inp["fox_q_norm_g"])[0]
        cols[:, 1] = f(inp["fox_k_norm_g"])[0]
        cols[:, 2] = f(inp["hgrn_norm_g"])[0]
        cols[:, 3:7] = lbl[0].reshape(16, 128)[4 * r:4 * r + 4].T
        cols[:, 7:11] = lbl[1].reshape(16, 128)[4 * r:4 * r + 4].T
        cols[:, 11 + r] = 1.0
        m = dict(shared)
        m.update({"xb": x[b], "xq": np.ascontiguousarray(x[b, r * QT:(r + 1) * QT]),
                  "pq": np.ascontiguousarray(p[0, b, r * QT:(r + 1) * QT]), "w_in": w_in_r[r], "cols": cols,
                  "fbias": np.ascontiguousarray(f(inp["fox_f_bias"])[0, 4 * r:4 * r + 4].reshape(4, 1))})
        maps.append(m)
    return maps


_NC_CACHE = {}


def kernel(**inputs):
    S = int(np.asarray(inputs["x"]).shape[1])
    QT = S // 4
    if S not in _NC_CACHE:
        _NC_CACHE[S] = build_nc(S)
    nc = _NC_CACHE[S]
    maps = make_in_maps(inputs, S)
    res = run_bass_kernel_spmd(nc, maps, core_ids=list(range(8)))
    outp = np.empty((2, S, D), np.float32)
    for c in range(8):
        b, r = c // 4, c % 4
        outp[b, r * QT:(r + 1) * QT] = res.results[c]["out"]
    return outp
```

```python
import math
from contextlib import ExitStack

import numpy as np
import ml_dtypes
import concourse.bass as bass
import concourse.mybir as mybir
from concourse.bass_utils import run_bass_kernel_spmd

F32 = mybir.dt.float32
BF16 = mybir.dt.bfloat16
AF = mybir.ActivationFunctionType
ALU = mybir.AluOpType

D = 4096
HD = 128
DFF = 16384
PLE = 256
EPS = 1e-6
NEG = -1.0e30
HG_OFFSET = 16


class Sched:
    ENGS = ("pe", "act", "dve", "pool", "sp")

    def __init__(self, nc, nsem=4, ndsem=8):
        self.nc = nc
        self.ops = []
        self.deps = []
        self.last_writer = {}
        self.readers = {}
        self.nsem = nsem
        self.ndsem = ndsem
        self.last_c = {}
        self.last_d = {e: [] for e in self.ENGS}
        self.excl = set()

    def add(self, eng, fn, reads=(), writes=(), kind="c", after=()):
        i = len(self.ops)
        d = set(after)
        if self.excl:
            ex = [s for s in reads if s in self.excl]
            if ex:
                reads = [s for s in reads if s not in self.excl]
                writes = list(writes) + ex
        for s in reads:
            w = self.last_writer.get(s)
            if w is not None:
                d.add(w)
        for s in writes:
            w = self.last_writer.get(s)
            if w is not None:
                d.add(w)
            d.update(self.readers.get(s, ()))
        for s in reads:
            self.readers.setdefault(s, []).append(i)
        for s in writes:
            self.last_writer[s] = i
            self.readers[s] = []
        d.discard(i)
        self.ops.append((eng, fn, kind))
        self.deps.append(d)
        if kind == "d":
            self.last_d[eng] = (self.last_d[eng] + [i])[-self.ndsem:]
        elif kind != "cc":
            self.last_c[eng] = i
        return i

    def dma(self, eng, out, in_, reads=(), writes=(), after=()):
        return self.add(eng, lambda e: e.dma_start(out=out, in_=in_), reads, writes, kind="d", after=after)

    def barrier(self):
        L = set(self.last_c.values())
        for e in self.ENGS:
            L.update(self.last_d[e])
        for e in self.ENGS:
            i = self.add(e, None, (), (), kind="n")
            self.deps[i] = set(x for x in L if x != i)
        self.last_writer = {}
        self.readers = {}

    def emit(self, stack):
        nc = self.nc
        ops, deps = self.ops, self.deps
        n = len(ops)
        has_dep = [False] * n
        for i in range(n):
            for d in deps[i]:
                if ops[d][0] == "pe" and ops[i][0] == "pe" and ops[d][2] == "c" and ops[i][2] == "c":
                    continue
                has_dep[d] = True
        csem = {e: [stack.enter_context(nc.semaphore(f"c_{e}_{k}")) for k in range(self.nsem)] for e in self.ENGS}
        dsem = {e: [stack.enter_context(nc.semaphore(f"d_{e}_{k}")) for k in range(self.ndsem)] for e in self.ENGS}
        sig = [None] * n
        ccount = {e: 0 for e in self.ENGS}
        dcount = {e: 0 for e in self.ENGS}
        throttle = [None] * n
        for i, (eng, fn, kind) in enumerate(ops):
            if kind == "d":
                j = dcount[eng]
                dcount[eng] += 1
                slot, rnd = j % self.ndsem, j // self.ndsem
                sig[i] = (dsem[eng][slot], 16 * (rnd + 1), "d", eng, j)
                if rnd > 0:
                    throttle[i] = (dsem[eng][slot], 16 * rnd)
            elif kind == "cc":
                ccsem = stack.enter_context(nc.semaphore(f"cc_{i}"))
                sig[i] = (ccsem, 1, "x", eng, 0)
            elif kind == "c" and has_dep[i]:
                k = ccount[eng]
                ccount[eng] += 1
                sig[i] = (csem[eng][k % self.nsem], k // self.nsem + 1, "c", eng, k)
        per_eng = {e: [] for e in self.ENGS}
        for i, (eng, fn, kind) in enumerate(ops):
            per_eng[eng].append(i)
        self.stats = {e: len(per_eng[e]) for e in self.ENGS}
        final_d = dict(dcount)
        cc_final = []

        def make_body(eng):
            idxs = per_eng[eng]

            def body(e):
                waited = {}
                known = {}

                def wait(sem, val):
                    key = id(sem)
                    if waited.get(key, 0) >= val:
                        return
                    e.wait_ge(sem, val)
                    waited[key] = val

                for i in idxs:
                    _, fn, kind = ops[i]
                    if throttle[i] is not None:
                        wait(*throttle[i])
                    cmax = {}
                    dmax = {}
                    for d in deps[i]:
                        s = sig[d]
                        if s is None:
                            continue
                        sem, val, skind, peng, k = s
                        if skind == "c":
                            if peng == "pe" and eng == "pe" and kind == "c":
                                continue
                            if k > cmax.get(peng, (-1,))[0]:
                                cmax[peng] = (k, sem, val)
                        else:
                            key = id(sem)
                            if val > dmax.get(key, (0,))[0]:
                                dmax[key] = (val, sem)
                    for peng, (k, sem, val) in cmax.items():
                        if known.get(peng, -1) >= k:
                            continue
                        wait(sem, val)
                        known[peng] = k
                    for key, (val, sem) in dmax.items():
                        wait(sem, val)
                    if fn is None:
                        continue
                    ins = fn(e)
                    s = sig[i]
                    if s is not None:
                        ins.then_inc(s[0], 16 if s[2] == "d" else 1)
                    if kind == "cc":
                        cc_final.append(s)
                if eng == "sp":
                    for q in self.ENGS:
                        cnt = final_d[q]
                        for slot in range(min(cnt, self.ndsem)):
                            uses = (cnt - slot + self.ndsem - 1) // self.ndsem
                            wait(dsem[q][slot], 16 * uses)

            return body

        block = stack.enter_context(nc.Block())
        block.tensor(make_body("pe"))
        block.scalar(make_body("act"))
        block.vector(make_body("dve"))
        block.gpsimd(make_body("pool"))
        block.sync(make_body("sp"))


def build_nc(S, dbg=()):
    QT = S // 4
    NBA = S // 512
    NBQ = QT // 512
    NKT = S // 128
    nc = bass.Bass("TRN2", target_bir_lowering=False)
    dt_in = lambda name, shape, dt=F32: nc.dram_tensor(name, list(shape), dt, kind="ExternalInput").ap()
    dt_sc = lambda name, shape, dt: nc.dram_tensor(name, list(shape), dt, kind="Internal").ap()

    xb = dt_in("xb", [S, D])
    xq = dt_in("xq", [QT, D])
    pq = dt_in("pq", [QT, PLE])
    w_in = dt_in("w_in", [D, 3588])
    w_out = dt_in("w_out", [D, D])
    w_up = dt_in("w_up", [D, DFF])
    w_down = dt_in("w_down", [DFF, D])
    w_gate = dt_in("w_gate", [D, D])
    w_pp = dt_in("w_pp", [PLE, D])
    g_rows = dt_in("g_rows", [4, D])
    cols = dt_in("cols", [128, 16])
    fbias = dt_in("fbias", [4, 1])
    ident_in = dt_in("ident", [128, 128], BF16)
    cmask_in = dt_in("cmask", [128, 4 * 512])
    hmask_in = dt_in("hmask", [128, 128])
    rmask_in = dt_in("rmask", [128, 512])
    sel4_in = dt_in("sel4", [4, 4 * 128])
    id4_in = dt_in("id4", [4, 4])
    out = nc.dram_tensor("out", [QT, D], F32, kind="ExternalOutput").ap()

    uT = dt_sc("uT", [S // 512, 128, 32, 512], BF16)
    qa_raw = dt_sc("qa_raw", [4, 128, S], F32)
    ka_raw = dt_sc("ka_raw", [4, 128, S], F32)
    qT = dt_sc("qT", [4, 128, S], BF16)
    kT = dt_sc("kT", [4, 128, S], BF16)
    va = dt_sc("va", [4, 128, S // 128, 128], BF16)
    fa_raw = dt_sc("fa_raw", [4, S], F32)
    qbT = dt_sc("qbT", [4, 128, S], F32)
    sgT = dt_sc("sgT", [4, 128, S], F32)
    gbT = dt_sc("gbT", [4, 128, S], F32)
    vb = dt_sc("vb", [4, 128, S // 128, 128], BF16)
    rs_a = dt_sc("rs_a", [4 * 2048, QT], BF16)
    rs_b = dt_sc("rs_b", [4 * 2048, QT], BF16)
    mT_a = dt_sc("mT_a", [2048, QT], BF16)
    mT_b = dt_sc("mT_b", [2048, QT], BF16)
    h1 = dt_sc("h1", [QT, D], F32)
    umT = dt_sc("umT", [QT // 512, 128, 32, 512], BF16)
    hidT = dt_sc("hidT", [QT // 512, 128, 128, 512], BF16)
    h2 = dt_sc("h2", [QT, D], F32)
    upT = dt_sc("upT", [QT // 512, 128, 32, 512], BF16)
    eS = dt_sc("eS", [QT, D], F32)
    dbg_out = {}
    dbg_src = {}
    for name in dbg:
        src = locals()[name]
        dbg_src[name] = src
        dbg_out[name] = nc.dram_tensor("dbg_" + name, list(src.shape), src.dtype, kind="ExternalOutput").ap()

    with ExitStack() as top:
        Sc = Sched(nc)
        add, dma = Sc.add, Sc.dma

        def mm(o, lhsT, rhs, start, stop, r, w):
            add("pe", lambda e: e.matmul(o, lhsT=lhsT, rhs=rhs, start=start, stop=stop), r, w)

        def tr(o, in_, ident, r, w):
            add("pe", lambda e: e.transpose(out=o, in_=in_, identity=ident), r, w)

        def act(o, in_, func, r, w, **kw):
            add("act", lambda e: e.activation(out=o, in_=in_, func=func, **kw), r, w)

        def dve(method, r, w, **kw):
            add("dve", lambda e: getattr(e, method)(**kw), r, w)

        uid = [0]

        def sbt(st, name, shape, dt):
            uid[0] += 1
            return st.enter_context(nc.sbuf_tensor(f"{name}_{uid[0]}", list(shape), dt))

        def pst(st, name, shape, dt):
            uid[0] += 1
            Sc.excl.add(name)
            return st.enter_context(nc.psum_tensor(f"{name}_{uid[0]}", list(shape), dt))
        IDN = sbt(top, "IDN", [128, 128], BF16)
        COLS = sbt(top, "COLS", [128, 16], F32)
        CD = sbt(top, "CD", [128, 24], F32)
        ONESF = sbt(top, "ONESF", [128, 128], F32)
        ONESB = sbt(top, "ONESB", [128, 128], BF16)
        zst = ExitStack()
        ZT = sbt(zst, "ZT", [128, 2048], BF16)
        dma("sp", IDN[:], ident_in, (), ["IDN"])
        dma("sp", COLS[:], cols, (), ["COLS"])
        add("dve", lambda e: e.memset(ONESF[:], 1.0 / 128), (), ["ONESF"])
        add("dve", lambda e: e.memset(ONESB[:], 1.0), (), ["ONESB"])
        add("dve", lambda e: e.memset(ZT[:], 0.0), (), ["ZT"])
        add("dve", lambda e: e.tensor_scalar_mul(out=CD[:, 0:1], in0=COLS[:, 0:1], scalar1=1.0 / math.sqrt(HD)), ["COLS"], ["CD0"])
        add("dve", lambda e: e.tensor_sub(out=CD[:, 13:17], in0=COLS[:, 3:7], in1=COLS[:, 7:11]), ["COLS"], ["CD13"])
        act(CD[:, 1:5], CD[:, 13:17], AF.Sigmoid, ["CD13"], ["CD1"])
        act(CD[:, 5:9], CD[:, 13:17], AF.Sigmoid, ["CD13"], ["CD5"], scale=-1.0)
        add("dve", lambda e: e.tensor_scalar_mul(out=CD[:, 9:13], in0=CD[:, 5:9], scalar1=-1.0), ["CD5"], ["CD9"])
        zw = min(QT, 2048)
        for rs_ in (rs_a, rs_b):
            for i in range(4 * 2048 // 128):
                for j in range(QT // zw):
                    dma("sp", rs_[i * 128:(i + 1) * 128, j * zw:(j + 1) * zw], ZT[:, 0:zw], ["ZT"], [])
        Sc.barrier()
        zst.close()

        def norm_transpose(src, dstT, M, grow, tag):
            with ExitStack() as st:
                X = [sbt(st, f"X{i}", [128, D], F32) for i in range(4)]
                GB = sbt(st, "GB", [128, D], F32)
                JK = sbt(st, "JK", [128, D], BF16)
                U = [sbt(st, f"U{i}", [128, D], BF16) for i in range(4)]
                UT = [sbt(st, f"UT{i}", [128, 32, 512], BF16) for i in range(2)]
                SSq = [sbt(st, f"SS{i}", [128, 4], F32) for i in range(4)]
                PB = [pst(st, f"PB{i}", [128, 1024], BF16) for i in range(4)]
                dma("sp", GB[:], g_rows[grow:grow + 1, :].partition_broadcast(128), (), ["GB"])
                nt = M // 128
                for t_ in range(min(3, nt)):
                    dma("sp", X[t_][:], src[t_ * 128:(t_ + 1) * 128, :], (), [f"X{t_}"])

                def stats(t):
                    i = t % 4
                    act(JK[:], X[i][:], AF.Square, [f"X{i}"], ["JK", f"SSa{i}"], accum_out=SSq[i][:, 0:1])
                    act(SSq[i][:, 1:2], SSq[i][:, 0:1], AF.Sqrt, [f"SSa{i}"], [f"SSb{i}"], scale=1.0 / D, bias=EPS)

                def recip(t):
                    i = t % 4
                    dve("reciprocal", [f"SSb{i}"], [f"SSc{i}"], out=SSq[i][:, 2:3], in_=SSq[i][:, 1:2])

                stats(0)
                recip(0)
                for t in range(nt):
                    i = t % 4
                    g, tt = t // 4, t % 4
                    gi = g % 2
                    if t + 3 < nt:
                        dma("sp", X[(t + 3) % 4][:], src[(t + 3) * 128:(t + 4) * 128, :], (), [f"X{(t + 3) % 4}"])
                    if t + 1 < nt:
                        stats(t + 1)
                    dve("scalar_tensor_tensor", [f"X{i}", f"SSc{i}", "GB"], [f"U{i}"], out=U[i][:], in0=X[i][:],
                        scalar=SSq[i][:, 2:3], in1=GB[:], op0=ALU.mult, op1=ALU.mult)
                    if t + 1 < nt:
                        recip(t + 1)
                    for q in range(4):
                        pb = (t * 4 + q) % 4
                        for j in range(8):
                            kc = q * 8 + j
                            tr(PB[pb][:, j * 128:(j + 1) * 128], U[i][:, kc * 128:(kc + 1) * 128], IDN[:],
                               [f"U{i}", "IDN"], [f"PB{pb}"])
                        o = UT[gi][:, q * 8:(q + 1) * 8, tt * 128:(tt + 1) * 128]
                        src_ps = PB[pb][:].rearrange("p (j m) -> p j m", j=8)
                        if q % 2 == 0:
                            add("act", lambda e, o=o, s_=src_ps: e.copy(out=o, in_=s_), [f"PB{pb}"], [f"UT{gi}"])
                        else:
                            add("dve", lambda e, o=o, s_=src_ps: e.tensor_copy(out=o, in_=s_), [f"PB{pb}"], [f"UT{gi}"])
                    if tt == 3:
                        dma("pool", dstT[g], UT[gi][:], [f"UT{gi}"], [])
            Sc.barrier()

        def gemm_pass(aT, M, w, panels, extra_alloc=None, KC=32, side=None, a_after=(), nps=6, side_from=0,
                      a_view=None, post_load=None):
            with ExitStack() as st:
                nblk = M // 512
                resident = nblk <= 4
                wmax = max(p_[1] for p_ in panels)
                WP = [sbt(st, f"WP{i}", [128, KC, (wmax + 7) // 8 * 8], BF16) for i in range(2)]
                AB = [sbt(st, f"AB{i}", [128, KC, 512], BF16) for i in range(nblk if resident else 3)]
                PS = [pst(st, f"PS{i}", [128, 512], F32) for i in range(nps)]
                ctx = extra_alloc(st) if extra_alloc else None
                sgen = side(st) if side else None
                psi = 0
                seq = [(pi, bi) for pi in range(len(panels)) for bi in range(nblk)]

                def load_w(pi):
                    c0, ncol, _, _ = panels[pi]
                    wi = pi % 2
                    ids = []
                    for kg in range(KC // 8):
                        ids.append(dma("pool", WP[wi][:, kg * 8:(kg + 1) * 8, 0:ncol],
                                       w[kg * 1024:(kg + 1) * 1024, c0:c0 + ncol].rearrange("(k p) n -> p k n", p=128),
                                       (), [f"WP{wi}_{kg}"]))
                    return ids

                def load_a(n):
                    pi, bi = seq[n]
                    ai = bi if resident else n % 3
                    src = a_view(bi) if a_view else aT[bi]
                    ids = []
                    for kg in range(KC // 8):
                        ids.append(dma("sp", AB[ai][:, kg * 8:(kg + 1) * 8, :], src[:, kg * 8:(kg + 1) * 8, :], (), [f"AB{ai}_{kg}"],
                                       after=a_after))
                    return ids

                def side_step(pi):
                    if sgen is not None and pi >= side_from:
                        next(sgen, None)

                init_ids = load_w(0)
                if resident:
                    for b_ in range(nblk):
                        init_ids += load_a(b_)
                else:
                    init_ids += load_a(0)
                    init_ids += load_a(1)
                if post_load:
                    post_load(init_ids)
                if hasattr(panels[0][3], "pre"):
                    panels[0][3].pre(ctx, 0, 0, 0)
                for n, (pi, bi) in enumerate(seq):
                    c0, ncol, orient, epi = panels[pi]
                    wi = pi % 2
                    ai = bi if resident else n % 3
                    if bi == 0 and pi + 1 < len(panels):
                        load_w(pi + 1)
                    if not resident and n + 2 < len(seq):
                        load_a(n + 2)
                    nsub = (ncol + 127) // 128 if orient == "F" else 4
                    for sub in range(nsub):
                        ps = PS[psi % nps]
                        pname = f"PS{psi % nps}"
                        psi += 1
                        if orient == "F":
                            cw = min(128, ncol - sub * 128)
                            for kc in range(KC):
                                mm(ps[0:cw, :], WP[wi][:, kc, sub * 128:sub * 128 + cw], AB[ai][:, kc, :], kc == 0, kc == KC - 1,
                                   [f"WP{wi}_{kc // 8}", f"AB{ai}_{kc // 8}"], [pname])
                        else:
                            for kc in range(KC):
                                mm(ps[:, 0:ncol], AB[ai][:, kc, sub * 128:(sub + 1) * 128], WP[wi][:, kc, 0:ncol], kc == 0, kc == KC - 1,
                                   [f"WP{wi}_{kc // 8}", f"AB{ai}_{kc // 8}"], [pname])
                        nxt = None
                        if sub + 1 < nsub:
                            nxt = (pi, sub + 1, bi)
                        elif n + 1 < len(seq):
                            nxt = (seq[n + 1][0], 0, seq[n + 1][1])
                        if nxt is not None and hasattr(panels[nxt[0]][3], "pre"):
                            panels[nxt[0]][3].pre(ctx, *nxt)
                        epi(ctx, ps, pname, pi, sub, bi)
                        side_step(pi)
                if sgen is not None:
                    for _ in sgen:
                        pass
            Sc.barrier()

        norm_transpose(xb, uT, S, 0, "A")

        def alloc_B(st):
            c = {}
            c["OF"] = [sbt(st, f"OF{i}", [128, 512], F32) for i in range(3)]
            c["OB"] = [sbt(st, f"OB{i}", [128, 512], BF16) for i in range(2)]
            c["n"] = 0
            return c

        def epi_F(dst, func, tag=None):
            def epi(c, ps, pname, pi, ch, bi):
                i = c["n"] % 3
                c["n"] += 1
                o = c["OF"][i]
                act(o[:], ps[:], func, [pname], [f"OF{i}"])
                dma("sp", dst[ch, :, bi * 512:(bi + 1) * 512], o[:], [f"OF{i}"], [f"{tag}_{ch}_{bi}"] if tag else [])
            return epi

        def epi_T16(dst):
            def epi(c, ps, pname, pi, tt, bi):
                i = c["n"] % 2
                c["n"] += 1
                o = c["OB"][i]
                dve("tensor_copy", [pname], [f"OB{i}"], out=o[:], in_=ps[:])
                kt = bi * 4 + tt
                dma("sp", dst[:, :, kt, :].rearrange("h p d -> p h d"), o[:].rearrange("p (h d) -> p h d", h=4), [f"OB{i}"], [])
            return epi

        def epi_fa(c, ps, pname, pi, ch, bi):
            i = c["n"] % 3
            c["n"] += 1
            o = c["OF"][i]
            act(o[0:4, :], ps[0:4, :], AF.Copy, [pname], [f"OF{i}"])
            dma("sp", fa_raw[:, bi * 512:(bi + 1) * 512], o[0:4, :], [f"OF{i}"], [])

        epi_qa = epi_F(qa_raw, AF.Copy, "qa")

        def epi_qa_fa(c, ps, pname, pi, ch, bi):
            (epi_fa if ch == 4 else epi_qa)(c, ps, pname, pi, ch, bi)

        panels_B = [
            (0, 516, "F", epi_qa_fa),
            (516, 512, "F", epi_F(ka_raw, AF.Copy, "ka")),
            (1028, 512, "T", epi_T16(va)),
            (2564, 512, "T", epi_T16(vb)),
            (1540, 512, "F", epi_F(qbT, AF.Silu)),
            (3076, 512, "F", epi_F(gbT, AF.Silu)),
            (2052, 512, "F", epi_F(sgT, AF.Sigmoid)),
        ]
        def c_side(st):
            RAW = [sbt(st, f"RAW{i}", [128, 512], F32) for i in range(3)]
            SQ = [sbt(st, f"SQ{i}", [128, 512], F32) for i in range(3)]
            SD = [sbt(st, f"SD{i}", [128, 512], F32) for i in range(3)]
            O16 = [sbt(st, f"O16{i}", [128, 512], BF16) for i in range(3)]
            PM = [pst(st, f"PM{i}", [128, 512], F32) for i in range(2)]
            items = [(src, dst, gcol, tag, h, bi) for (src, dst, gcol, tag) in
                     ((qa_raw, qT, CD[:, 0:1], "qa"), (ka_raw, kT, COLS[:, 1:2], "ka")) for h in range(4) for bi in range(NBA)]
            N = len(items)

            def sA(n):
                src, dst, gcol, tag, h, bi = items[n]
                i = n % 3
                dma("act", RAW[i][:], src[h, :, bi * 512:(bi + 1) * 512], [f"{tag}_{h}_{bi}"], [f"RAW{i}"])
                act(SQ[i][:], RAW[i][:], AF.Square, [f"RAW{i}"], [f"SQ{i}"])

            def sB(n):
                i, j = n % 3, n % 2
                mm(PM[j][:], ONESF[:], SQ[i][:], True, True, [f"SQ{i}"], [f"PM{j}"])

            def sC(n):
                src, dst, gcol, tag, h, bi = items[n]
                i, j = n % 3, n % 2
                act(SD[i][:], PM[j][:], AF.Sqrt, [f"PM{j}"], [f"SD{i}"], bias=EPS, scale=1.0)
                dve("reciprocal", [f"SD{i}"], [f"SD{i}"], out=SD[i][:], in_=SD[i][:])
                dve("scalar_tensor_tensor", [f"RAW{i}", f"SD{i}"], [f"O16{i}"], out=O16[i][:],
                    in0=RAW[i][:], scalar=gcol, in1=SD[i][:], op0=ALU.mult, op1=ALU.mult)
                dma("pool", dst[h, :, bi * 512:(bi + 1) * 512], O16[i][:], [f"O16{i}"], [])

            for k in range(N + 2):
                if k < N:
                    sA(k)
                if 0 <= k - 1 < N:
                    sB(k - 1)
                if 0 <= k - 2 < N:
                    sC(k - 2)
                yield

        gemm_pass(uT, S, w_in, panels_B, alloc_B, side=c_side, nps=5, side_from=2)

        with ExitStack() as st:
            HM = sbt(st, "HM", [128, 128], F32)
            RM = sbt(st, "RM", [128, 512], F32)
            per = lambda name, shape, dt: [sbt(st, f"{name}{i}", shape, dt) for i in range(4)]
            QB, SG, GG = per("QB", [128, 512], F32), per("SG", [128, 512], F32), per("GG", [128, 512], F32)
            VB = per("VB", [128, 4, 128], BF16)
            LOGF, KBt, Bt, EB, ENB, KE = (per(nm, [128, 512], F32) for nm in ("LOGF", "KBt", "Bt", "EB", "ENB", "KE"))
            QE16, KE16, KD16 = (per(nm, [128, 512], BF16) for nm in ("QE16", "KE16", "KD16"))
            KDT = per("KDT", [128, 4, 128], BF16)
            SQh, SDh, Yh = (per(nm, [128, 512], F32) for nm in ("SQh", "SDh", "Yh"))
            OSh = per("OSh", [128, 4, 512], BF16)
            ST = [[sbt(st, f"ST{h}_{j}", [128, 128], F32) for j in range(2)] for h in range(4)]
            S16 = [[sbt(st, f"S16{h}_{j}", [128, 128], BF16) for j in range(2)] for h in range(4)]
            ATM = per("ATM", [128, 128], BF16)
            PAT = pst(st, "PAT", [128, 512], F32)
            PUs = [pst(st, f"PU{i}", [128, 512], F32) for i in range(2)]
            PMS = pst(st, "PKM", [128, 512], F32)
            PKD = PMS[:].bitcast(BF16)
            POT = [pst(st, f"POT{i}", [128, 512], F32) for i in range(4)]
            dma("sp", HM[:], hmask_in, (), ["HM"])
            dma("sp", RM[:], rmask_in, (), ["RM"])
            for h in range(4):
                add("dve", lambda e, h=h: e.memset(ST[h][0][:], 0.0), (), [f"ST{h}_0"])
                add("dve", lambda e, h=h: e.memset(S16[h][0][:], 0.0), (), [f"S16{h}_0"])

            def group_gen(heads):
                cur = {h: 0 for h in heads}
                for bi in range(NBA):
                    sl = slice(bi * 512, (bi + 1) * 512)
                    for h in heads:
                        i = h
                        dma("sp", QB[i][:], qbT[h, :, sl], (), [f"QB{i}"])
                        dma("sp", SG[i][:], sgT[h, :, sl], (), [f"SG{i}"])
                        dma("sp", GG[i][:], gbT[h, :, sl], (), [f"GG{i}"])
                        dma("sp", VB[i][:], vb[h, :, bi * 4:(bi + 1) * 4, :], (), [f"VB{i}"])
                    yield
                    for h in heads:
                        i = h
                        act(LOGF[i][:], SG[i][:], AF.Ln, [f"SG{i}"], [f"LOGF{i}"], scale=CD[:, 5 + h:6 + h], bias=CD[:, 1 + h:2 + h])
                        act(KBt[i][:], SG[i][:], AF.Identity, [f"SG{i}"], [f"KBt{i}"], scale=CD[:, 9 + h:10 + h], bias=CD[:, 5 + h:6 + h])
                        yield
                        add("dve", lambda e, i=i: e.tensor_tensor_scan(out=Bt[i][:], data0=RM[:], data1=LOGF[i][:], initial=0.0,
                                                                       op0=ALU.mult, op1=ALU.add), ["RM", f"LOGF{i}"], [f"Bt{i}"])
                        act(EB[i][:], Bt[i][:], AF.Exp, [f"Bt{i}"], [f"EB{i}"])
                        act(ENB[i][:], Bt[i][:], AF.Exp, [f"Bt{i}"], [f"ENB{i}"], scale=-1.0)
                        yield
                        dve("tensor_tensor", [f"QB{i}", f"EB{i}"], [f"QE16{i}"], out=QE16[i][:], in0=QB[i][:], in1=EB[i][:], op=ALU.mult)
                        dve("tensor_tensor", [f"KBt{i}", f"ENB{i}"], [f"KE{i}"], out=KE[i][:], in0=KBt[i][:], in1=ENB[i][:], op=ALU.mult)
                        add("act", lambda e, i=i: e.copy(out=KE16[i][:], in_=KE[i][:]), [f"KE{i}"], [f"KE16{i}"])
                        yield
                        for c in range(8):
                            cs = slice(c * 64, (c + 1) * 64)
                            dve("tensor_scalar_mul", [f"KE{i}", f"EB{i}"], [f"KD16{i}_{c}"], out=KD16[i][:, cs], in0=KE[i][:, cs],
                                scalar1=EB[i][:, c * 64 + 63:c * 64 + 64])
                            if c == 3:
                                yield
                        yield
                        hp = h % 2
                        for tt in range(4):
                            tr(PKD[:, hp * 512 + tt * 128:hp * 512 + (tt + 1) * 128], KD16[i][:, tt * 128:(tt + 1) * 128], IDN[:],
                               [f"KD16{i}_{2 * tt}", f"KD16{i}_{2 * tt + 1}", "IDN"], ["PKM"])
                        add("act", lambda e, i=i, hp=hp: e.copy(out=KDT[i][:], in_=PKD[:, hp * 512:(hp + 1) * 512].rearrange("p (t d) -> p t d", t=4)),
                            ["PKM"], [f"KDT{i}"])
                        yield
                    for tt in range(4):
                        ts_ = slice(tt * 128, (tt + 1) * 128)
                        for h in heads:
                            i = h
                            pas = slice(h * 128, (h + 1) * 128)
                            mm(PAT[:, pas], KE16[i][:, ts_], QE16[i][:, ts_], True, True, [f"KE16{i}", f"QE16{i}"], ["PAT"])
                            dve("tensor_tensor", ["PAT", "HM"], [f"ATM{h}"], out=ATM[h][:], in0=PAT[:, pas], in1=HM[:], op=ALU.mult)
                        yield
                        for c in range(2):
                            tok = tt * 128 + c * 64
                            tk = slice(tok, tok + 64)
                            for h in heads:
                                i = h
                                pus = slice(h * 128, (h + 1) * 128)
                                cu = cur[h]
                                nx = 1 - cu
                                mm(POT[h][:, tk], VB[i][:, tt, :], ATM[h][:, c * 64:(c + 1) * 64], True, False,
                                   [f"VB{i}", f"ATM{h}"], [f"POT{h}"])
                                mm(POT[h][:, tk], S16[h][cu][:], QE16[i][:, tk], False, True, [f"S16{h}_{cu}", f"QE16{i}"], [f"POT{h}"])
                                mm(PUs[h % 2][:, pus], KDT[i][c * 64:(c + 1) * 64, tt, :], VB[i][c * 64:(c + 1) * 64, tt, :], True, True,
                                   [f"KDT{i}", f"VB{i}"], [f"PU{h % 2}"])
                                dve("scalar_tensor_tensor", [f"ST{h}_{cu}", f"EB{i}", f"PU{h % 2}"], [f"ST{h}_{nx}"], out=ST[h][nx][:],
                                    in0=ST[h][cu][:], scalar=EB[i][:, tok + 63:tok + 64], in1=PUs[h % 2][:, pus], op0=ALU.mult, op1=ALU.add)
                                add("act", lambda e, h=h, nx=nx: e.copy(out=S16[h][nx][:], in_=ST[h][nx][:]), [f"ST{h}_{nx}"], [f"S16{h}_{nx}"])
                                cur[h] = nx
                            yield
                    for h in heads:
                        i = h
                        act(SQh[i][:], POT[h][:], AF.Square, [f"POT{h}"], [f"SQh{i}"])
                        mm(PMS[:], ONESF[:], SQh[i][:], True, True, ["ONESF", f"SQh{i}"], ["PKM"])
                        act(SDh[i][:], PMS[:], AF.Sqrt, ["PKM"], [f"SDh{i}"], bias=EPS, scale=1.0)
                        yield
                        dve("reciprocal", [f"SDh{i}"], [f"SDh{i}"], out=SDh[i][:], in_=SDh[i][:])
                        dve("scalar_tensor_tensor", [f"POT{h}", f"SDh{i}"], [f"Yh{i}"], out=Yh[i][:], in0=POT[h][:],
                            scalar=COLS[:, 2:3], in1=SDh[i][:], op0=ALU.mult, op1=ALU.mult)
                        dve("tensor_tensor", [f"Yh{i}", f"GG{i}"], [f"Yh{i}"], out=Yh[i][:], in0=Yh[i][:], in1=GG[i][:], op=ALU.mult)
                        yield
                        for k in range(4):
                            act(OSh[i][:, k, :], Yh[i][:], AF.Identity, [f"Yh{i}"], [f"OSh{i}"], scale=COLS[:, 11 + k:12 + k])
                        quarter, off = (bi * 512) // QT, (bi * 512) % QT
                        for k in range(4):
                            r0 = quarter * 2048 + k * 512 + h * 128
                            dma("sp", rs_b[r0:r0 + 128, off:off + 512], OSh[i][:, k, :], [f"OSh{i}"], [])
                        yield

            gA, gB = group_gen((0, 1)), group_gen((2, 3))
            for _ in range(HG_OFFSET):
                next(gA, None)
            doneA = doneB = False
            while not (doneA and doneB):
                if not doneA:
                    try:
                        next(gA)
                    except StopIteration:
                        doneA = True
                if not doneB:
                    try:
                        next(gB)
                    except StopIteration:
                        doneB = True
        Sc.barrier()

        RG = [[0, 1, 2, 3], [4, 5, 6, 7]]

        with ExitStack() as st:
            FROW = sbt(st, "FROW", [4, S], F32)
            FB16 = sbt(st, "FB16", [4, S], BF16)
            NEGF = sbt(st, "NEGF", [128, NKT * 4], F32)
            SEL4 = sbt(st, "SEL4", [4, 512], F32)
            SEL4B = sbt(st, "SEL4B", [4, 512], BF16)
            ONES1 = sbt(st, "ONES1", [128, 128], F32)
            PF = pst(st, "PF", [128, 512], F32)
            with ExitStack() as st2:
                FTMP = sbt(st2, "FTMP", [4, S], F32)
                ONE4 = sbt(st2, "ONE4", [4, S], F32)
                NB4 = sbt(st2, "NB4", [4, 2], F32)
                ID4 = sbt(st2, "ID4", [4, 4], F32)
                dma("sp", FROW[:], fa_raw, (), ["FROW"])
                dma("sp", NB4[:, 0:1], fbias, (), ["NB4"])
                dma("sp", ID4[:], id4_in, (), ["ID4"])
                add("dve", lambda e: e.memset(ONE4[:], 1.0), (), ["ONE4"])
                add("dve", lambda e: e.tensor_scalar_mul(out=NB4[:, 1:2], in0=NB4[:, 0:1], scalar1=-1.0), ["NB4"], ["NB4b"])
                act(FTMP[:], FROW[:], AF.Exp, ["FROW", "NB4b"], ["FTMP"], bias=NB4[:, 1:2], scale=-1.0)
                act(FTMP[:], FTMP[:], AF.Ln, ["FTMP"], ["FTMP"], bias=1.0, scale=1.0)
                add("dve", lambda e: e.tensor_tensor_scan(out=FROW[:], data0=ONE4[:], data1=FTMP[:], initial=0.0,
                                                          op0=ALU.mult, op1=ALU.subtract), ["ONE4", "FTMP"], ["FROW"])
                for kt0 in range(0, NKT, 128):
                    nk = min(NKT, kt0 + 128) - kt0
                    for kt in range(kt0, kt0 + nk):
                        mm(PF[:, (kt - kt0) * 4:(kt - kt0) * 4 + 4], FROW[:, kt * 128:(kt + 1) * 128], ID4[:], True, True,
                           ["FROW", "ID4"], ["PF"])
                    add("act", lambda e, kt0=kt0, nk=nk: e.mul(out=NEGF[:, kt0 * 4:(kt0 + nk) * 4], in_=PF[:, 0:nk * 4], mul=-1.0),
                        ["PF"], ["NEGF"])
                add("act", lambda e: e.copy(out=FB16[:], in_=FROW[:]), ["FROW"], ["FB16"])
                dma("sp", SEL4[:], sel4_in, (), ["SEL4"])
                add("act", lambda e: e.copy(out=SEL4B[:], in_=SEL4[:]), ["SEL4"], ["SEL4B"])
                add("dve", lambda e: e.memset(ONES1[:], 1.0), (), ["ONES1"])
            Sc.barrier()
            CM = sbt(st, "CM", [128, 4, 512], F32)
            KTs = [sbt(st, f"KT{i}", [128, S], BF16) for i in range(2)]
            QTs = [sbt(st, f"QT{i}", [128, S], BF16) for i in range(2)]
            VT = [sbt(st, f"VT{i}", [128, NKT, 128], BF16) for i in range(2)]
            LACC = [sbt(st, f"LACC{i}", [128, 512], F32) for i in range(2)]
            TMP = [sbt(st, f"TMP{i}", [128, 512], F32) for i in range(4)]
            PT = [sbt(st, f"PT{i}", [128, 512], BF16) for i in range(4)]
            RL = sbt(st, "RL", [128, 512], F32)
            OA = [sbt(st, f"OA{i}", [128, 512], F32) for i in range(2)]
            OS = [sbt(st, f"OS{i}", [128, 4, 512], BF16) for i in range(2)]
            PA = [pst(st, f"PA{i}", [128, 512], F32) for i in range(4)]
            PO = [pst(st, f"PO{i}", [128, 512], F32) for i in range(2)]
            PL = [pst(st, f"PL{i}", [128, 512], F32) for i in range(1)]
            dma("sp", CM[:], cmask_in.rearrange("p (j n) -> p j n", j=4), (), ["CM"])

            LA = 3
            blocks = [(h, qb) for h in range(4) for qb in range(NBA)]
            tiles = []
            for n_, (h, qb) in enumerate(blocks):
                nkt = 4 * (qb + 1)
                for kt in range(nkt):
                    tiles.append((n_, h, qb, kt, nkt))

            def load_head(h):
                hi = h % 2
                return [dma("sp", KTs[hi][:], kT[h], (), [f"KT{hi}"]),
                        dma("sp", QTs[hi][:], qT[h], (), [f"QT{hi}"]),
                        dma("sp", VT[hi][:], va[h], (), [f"VT{hi}"])]

            def front(i):
                n_, h, qb, kt, nkt = tiles[i]
                hi, qi, a = h % 2, n_ % 2, i % 4
                d = kt - 4 * qb
                qs = slice(qb * 512, (qb + 1) * 512)
                mm(PA[a][:], KTs[hi][:, kt * 128:(kt + 1) * 128], QTs[hi][:, qs], True, False,
                   [f"KT{hi}", f"QT{hi}"], [f"PA{a}"])
                mm(PA[a][:], SEL4B[:, h * 128:(h + 1) * 128], FB16[:, qs], False, True, [], [f"PA{a}"])
                if d >= 0:
                    dve("tensor_tensor", [f"PA{a}"], [f"TMP{a}"], out=TMP[a][:], in0=PA[a][:], in1=CM[:, d, :], op=ALU.add)
                    act(PT[a][:], TMP[a][:], AF.Exp, [f"TMP{a}"], [f"PT{a}"], bias=NEGF[:, kt * 4 + h:kt * 4 + h + 1], scale=1.0)
                else:
                    act(PT[a][:], PA[a][:], AF.Exp, [f"PA{a}"], [f"PT{a}"], bias=NEGF[:, kt * 4 + h:kt * 4 + h + 1], scale=1.0)

            def back(i):
                n_, h, qb, kt, nkt = tiles[i]
                hi, qi, a = h % 2, n_ % 2, i % 4
                mm(PO[qi][:], VT[hi][:, kt, :], PT[a][:], kt == 0, kt == nkt - 1, [f"VT{hi}", f"PT{a}"], [f"PO{qi}"])
                if kt == 0:
                    dve("tensor_copy", [f"PT{a}"], [f"LACC{qi}"], out=LACC[qi][:], in_=PT[a][:])
                else:
                    dve("tensor_tensor", [f"PT{a}", f"LACC{qi}"], [f"LACC{qi}"], out=LACC[qi][:], in0=LACC[qi][:], in1=PT[a][:], op=ALU.add)
                if kt == nkt - 1:
                    mm(PL[0][:], ONES1[:], LACC[qi][:], True, True, [f"LACC{qi}"], ["PL0"])
                    dve("reciprocal", ["PL0"], ["RL"], out=RL[:], in_=PL[0][:])
                    dve("tensor_tensor", [f"PO{qi}", "RL"], [f"OA{qi}"], out=OA[qi][:], in0=PO[qi][:], in1=RL[:], op=ALU.mult)
                    for k in range(4):
                        act(OS[qi][:, k, :], OA[qi][:], AF.Identity, [f"OA{qi}", "COLS"], [f"OS{qi}"], scale=COLS[:, 11 + k:12 + k])
                    quarter, off = (qb * 512) // QT, (qb * 512) % QT
                    for k in range(4):
                        r0 = quarter * 2048 + k * 512 + h * 128
                        dma("sp", rs_a[r0:r0 + 128, off:off + 512], OS[qi][:, k, :], [f"OS{qi}"], [])

            hl = load_head(0) + load_head(1)
            rsb = add("pool", lambda e: e.collective_compute("ReduceScatter", ALU.add, replica_groups=RG, ins=[rs_b], outs=[mT_b], dma_qos="P3"),
                      (), (), kind="cc", after=hl)
            for i in range(len(tiles) + LA):
                if i < len(tiles):
                    front(i)
                if i - LA >= 0:
                    back(i - LA)
                    n_, h, qb, kt, nkt = tiles[i - LA]
                    if qb == NBA - 1 and kt == nkt - 1 and h + 2 < 4:
                        load_head(h + 2)
        Sc.barrier()

        def alloc_res(st):
            c = {"XR": [sbt(st, f"XR{i}", [128, 512], F32) for i in range(2)],
                 "ER": [sbt(st, f"ER{i}", [128, 512], F32) for i in range(2)],
                 "OF": [sbt(st, f"OF{i}", [128, 512], F32) for i in range(2)],
                 "OB": [sbt(st, f"OB{i}", [128, 512], BF16) for i in range(2)], "n": 0}
            return c

        def epi_res(prev, dst):
            def pre(c, pi, tt, bi):
                i = c.setdefault("pn", 0) % 2
                c["pn"] += 1
                r0, c0 = bi * 512 + tt * 128, pi * 512
                dma("sp", c["XR"][i][:], prev[r0:r0 + 128, c0:c0 + 512], (), [f"XR{i}"])

            def epi(c, ps, pname, pi, tt, bi):
                i = c["n"] % 2
                c["n"] += 1
                r0, c0 = bi * 512 + tt * 128, pi * 512
                dve("tensor_tensor", [pname, f"XR{i}"], [f"OF{i}"], out=c["OF"][i][:], in0=ps[:], in1=c["XR"][i][:], op=ALU.add)
                dma("sp", dst[r0:r0 + 128, c0:c0 + 512], c["OF"][i][:], [f"OF{i}"], [])
            epi.pre = pre
            return epi

        def ple_side(st):
            WPP = sbt(st, "WPP", [128, 2, D], BF16)
            GB = sbt(st, "GBp", [128, D], F32)
            PQ = [sbt(st, f"PQ{i}", [128, PLE], F32) for i in range(2)]
            P16 = [sbt(st, f"P16{i}", [128, PLE], BF16) for i in range(2)]
            PTt = [sbt(st, f"PTt{i}", [128, 2, 128], BF16) for i in range(2)]
            EPs = [sbt(st, f"EP{i}", [128, D], F32) for i in range(2)]
            JK = sbt(st, "JKp", [128, D], BF16)
            SSq = [sbt(st, f"SSp{i}", [128, 4], F32) for i in range(2)]
            PSs = [pst(st, f"PSp{i}", [128, 512], F32) for i in range(2)]
            PBp = pst(st, "PBp", [128, 1024], BF16)
            dma("pool", WPP[:], w_pp.rearrange("(k p) n -> p k n", p=128), (), ["WPP"])
            dma("sp", GB[:], g_rows[3:4, :].partition_broadcast(128), (), ["GBp"])
            yield
            for t in range(QT // 128):
                i = t % 2
                EP = EPs[i]
                EPn = f"EP{i}"
                dma("act", PQ[i][:], pq[t * 128:(t + 1) * 128, :], (), [f"PQ{i}"])
                dve("tensor_copy", [f"PQ{i}"], [f"P16{i}"], out=P16[i][:], in_=PQ[i][:])
                yield
                for kc in range(2):
                    tr(PBp[:, kc * 128:(kc + 1) * 128], P16[i][:, kc * 128:(kc + 1) * 128], IDN[:], [f"P16{i}"], ["PBp"])
                add("act", lambda e, i=i: e.copy(out=PTt[i][:], in_=PBp[:, 0:256].rearrange("p (k m) -> p k m", k=2)), ["PBp"], [f"PTt{i}"])
                yield
                for nb in range(8):
                    p_ = nb % 2
                    for kc in range(2):
                        mm(PSs[p_][:], PTt[i][:, kc, :], WPP[:, kc, nb * 512:(nb + 1) * 512], kc == 0, kc == 1,
                           [f"PTt{i}", "WPP"], [f"PSp{p_}"])
                    add("act", lambda e, nb=nb, p_=p_, EP=EP: e.copy(out=EP[:, nb * 512:(nb + 1) * 512], in_=PSs[p_][:]),
                        [f"PSp{p_}"], [EPn])
                    if nb % 2 == 1:
                        yield
                act(JK[:], EP[:], AF.Square, [EPn], ["JKp", f"SSa{i}"], accum_out=SSq[i][:, 0:1])
                act(SSq[i][:, 1:2], SSq[i][:, 0:1], AF.Sqrt, [f"SSa{i}"], [f"SSb{i}"], scale=1.0 / D, bias=EPS)
                dve("reciprocal", [f"SSb{i}"], [f"SSc{i}"], out=SSq[i][:, 2:3], in_=SSq[i][:, 1:2])
                dve("scalar_tensor_tensor", [EPn, f"SSc{i}", "GBp"], [EPn], out=EP[:], in0=EP[:],
                    scalar=SSq[i][:, 2:3], in1=GB[:], op0=ALU.mult, op1=ALU.mult)
                dma("pool", eS[t * 128:(t + 1) * 128, :], EP[:], [EPn], [])
                yield

        mTb3 = mT_b.rearrange("(k p) m -> k p m", p=128)
        mTa3 = mT_a.rearrange("(k p) m -> k p m", p=128)
        rs_ops = {}

        def issue_rsa(init_ids):
            rs_ops["a"] = add("pool", lambda e: e.collective_compute("ReduceScatter", ALU.add, replica_groups=RG, ins=[rs_a],
                                                                      outs=[mT_a], dma_qos="P3"), (), (), kind="cc", after=init_ids)

        gemm_pass(mTb3, QT, w_out[2048:4096, :],
                  [(pi * 512, 512, "T", epi_res(xq, h1)) for pi in range(8)], alloc_res, KC=16, side=ple_side, a_after=[rsb], nps=5,
                  a_view=lambda bi: mTb3[:, :, bi * 512:(bi + 1) * 512].rearrange("k p m -> p k m"), post_load=issue_rsa)
        gemm_pass(mTa3, QT, w_out[0:2048, :],
                  [(pi * 512, 512, "T", epi_res(h1, h1)) for pi in range(8)], alloc_res, KC=16, a_after=[rs_ops["a"]],
                  a_view=lambda bi: mTa3[:, :, bi * 512:(bi + 1) * 512].rearrange("k p m -> p k m"))

        norm_transpose(h1, umT, QT, 1, "H")

        def epi_up(c, ps, pname, pi, ch, bi):
            i = c["n"] % 2
            c["n"] += 1
            act(c["OF"][i][:], ps[:], AF.Relu, [pname], [f"OF{i}"])
            dve("tensor_tensor", [f"OF{i}"], [f"OB{i}"], out=c["OB"][i][:], in0=c["OF"][i][:], in1=c["OF"][i][:], op=ALU.mult)
            dma("sp", hidT[bi, :, pi * 4 + ch, :], c["OB"][i][:], [f"OB{i}"], [])

        gemm_pass(umT, QT, w_up, [(pi * 512, 512, "F", epi_up) for pi in range(32)], alloc_res)
        for q in range(4):
            gemm_pass(hidT, QT, w_down[q * D:(q + 1) * D, :],
                      [(pi * 512, 512, "T", epi_res(h1 if q == 0 else h2, h2)) for pi in range(8)], alloc_res,
                      a_view=lambda bi, q=q: hidT[bi, :, q * 32:(q + 1) * 32, :])

        norm_transpose(h2, upT, QT, 2, "K")

        def epi_gate(c, ps, pname, pi, tt, bi):
            i = c["n"] % 2
            c["n"] += 1
            r0, c0 = bi * 512 + tt * 128, pi * 512
            dma("sp", c["XR"][i][:], h2[r0:r0 + 128, c0:c0 + 512], (), [f"XR{i}"])
            dma("sp", c["ER"][i][:], eS[r0:r0 + 128, c0:c0 + 512], (), [f"ER{i}"])
            act(c["OF"][i][:], ps[:], AF.Sigmoid, [pname], [f"OF{i}"])
            dve("tensor_tensor", [f"OF{i}", f"ER{i}"], [f"OF{i}"], out=c["OF"][i][:], in0=c["OF"][i][:], in1=c["ER"][i][:], op=ALU.mult)
            dve("tensor_tensor", [f"OF{i}", f"XR{i}"], [f"OF{i}"], out=c["OF"][i][:], in0=c["OF"][i][:], in1=c["XR"][i][:], op=ALU.add)
            dma("sp", out[r0:r0 + 128, c0:c0 + 512], c["OF"][i][:], [f"OF{i}"], [])

        gemm_pass(upT, QT, w_gate, [(pi * 512, 512, "T", epi_gate) for pi in range(8)], alloc_res)

        for name, dst in dbg_out.items():
            src = dbg_src[name]
            dma("sp", dst, src, (), [])
        Sc.emit(top)
        print("ops per engine:", Sc.stats, flush=True)
    return nc


def host_consts():
    ident = np.eye(128, dtype=np.float32).astype(ml_dtypes.bfloat16)
    p = np.arange(128)[:, None]
    j = np.arange(512)[None, :]
    cm = np.zeros((128, 4, 512), np.float32)
    for d in range(4):
        cm[:, d, :] = np.where(j >= 128 * d + p, 0.0, NEG)
    s = np.arange(128)[:, None]
    t = np.arange(128)[None, :]
    hm = ((s // 64 == t // 64) & (s <= t)).astype(np.float32)
    rm = np.ones((128, 512), np.float32)
    rm[:, ::64] = 0.0
    sel4 = np.zeros((4, 4, 128), np.float32)
    for h in range(4):
        sel4[h, h, :] = 1.0
    return {"ident": ident, "cmask": cm.reshape(128, 2048), "hmask": hm, "rmask": rm,
            "sel4": sel4.reshape(4, 512), "id4": np.eye(4, dtype=np.float32)}


def make_in_maps(inp, S):
    QT = S // 4
    f = lambda a: np.ascontiguousarray(np.asarray(a, dtype=np.float32))
    x, p = f(inp["x"]), f(inp["p"])
    w_in = f(inp["w_in"])[0]
    consts = host_consts()
    g_rows = np.stack([f(inp["norm_mix_g"])[0], f(inp["norm_mlp_g"])[0], f(inp["ple_norm_g"])[0], f(inp["ple_post_g"])[0]])
    lbl = f(inp["hgrn_lb_logits"])
    shared = {"w_out": f(inp["w_out"])[0], "w_up": f(inp["w_up"])[0], "w_down": f(inp["w_down"])[0],
              "w_gate": f(inp["w_ple_gate"])[0], "w_pp": f(inp["w_ple_proj"])[0], "g_rows": g_rows}
    shared.update(consts)
    w_in_r = []
    for r in range(4):
        sl = lambda o: w_in[:, o + r * 512:o + (r + 1) * 512]
        w_in_r.append(np.ascontiguousarray(np.concatenate(
            [sl(0), w_in[:, 6144 + 4 * r:6144 + 4 * r + 4], sl(2048), sl(4096), sl(6160), sl(8208), sl(10256), sl(12304)], axis=1)))
    maps = []
    for c in range(8):
        b, r = c // 4, c % 4
        cols = np.zeros((128, 16), np.float32)
        cols[:, 0] = f(inp["fox_q_norm_g"])[0]
        cols[:, 1] = f(inp["fox_k_norm_g"])[0]
        cols[:, 2] = f(inp["hgrn_norm_g"])[0]
        cols[:, 3:7] = lbl[0].reshape(16, 128)[4 * r:4 * r + 4].T
        cols[:, 7:11] = lbl[1].reshape(16, 128)[4 * r:4 * r + 4].T
        cols[:, 11 + r] = 1.0
        m = dict(shared)
        m.update({"xb": x[b], "xq": np.ascontiguousarray(x[b, r * QT:(r + 1) * QT]),
                  "pq": np.ascontiguousarray(p[0, b, r * QT:(r + 1) * QT]), "w_in": w_in_r[r], "cols": cols,
                  "fbias": np.ascontiguousarray(f(inp["fox_f_bias"])[0, 4 * r:4 * r + 4].reshape(4, 1))})
        maps.append(m)
    return maps


_NC_CACHE = {}


def kernel(**inputs):
    S = int(np.asarray(inputs["x"]).shape[1])
    QT = S // 4
    if S not in _NC_CACHE:
        _NC_CACHE[S] = build_nc(S)
    nc = _NC_CACHE[S]
    maps = make_in_maps(inputs, S)
    res = run_bass_kernel_spmd(nc, maps, core_ids=list(range(8)))
    outp = np.empty((2, S, D), np.float32)
    for c in range(8):
        b, r = c // 4, c % 4
        outp[b, r * QT:(r + 1) * QT] = res.results[c]["out"]
    return outp
```

```python
import math
from contextlib import ExitStack

import numpy as np
import ml_dtypes
import concourse.bass as bass
import concourse.mybir as mybir
from concourse.bass_utils import run_bass_kernel_spmd

F32 = mybir.dt.float32
BF16 = mybir.dt.bfloat16
AF = mybir.ActivationFunctionType
ALU = mybir.AluOpType

D = 4096
HD = 128
DFF = 16384
PLE = 256
EPS = 1e-6
NEG = -1.0e30
HG_OFFSET = 16


class Sched:
    ENGS = ("pe", "act", "dve", "pool", "sp")

    def __init__(self, nc, nsem=4, ndsem=8):
        self.nc = nc
        self.ops = []
        self.deps = []
        self.last_writer = {}
        self.readers = {}
        self.nsem = nsem
        self.ndsem = ndsem
        self.last_c = {}
        self.last_d = {e: [] for e in self.ENGS}
        self.excl = set()

    def add(self, eng, fn, reads=(), writes=(), kind="c", after=()):
        i = len(self.ops)
        d = set(after)
        if self.excl:
            ex = [s for s in reads if s in self.excl]
            if ex:
                reads = [s for s in reads if s not in self.excl]
                writes = list(writes) + ex
        for s in reads:
            w = self.last_writer.get(s)
            if w is not None:
                d.add(w)
        for s in writes:
            w = self.last_writer.get(s)
            if w is not None:
                d.add(w)
            d.update(self.readers.get(s, ()))
        for s in reads:
            self.readers.setdefault(s, []).append(i)
        for s in writes:
            self.last_writer[s] = i
            self.readers[s] = []
        d.discard(i)
        self.ops.append((eng, fn, kind))
        self.deps.append(d)
        if kind == "d":
            self.last_d[eng] = (self.last_d[eng] + [i])[-self.ndsem:]
        elif kind != "cc":
            self.last_c[eng] = i
        return i

    def dma(self, eng, out, in_, reads=(), writes=(), after=()):
        return self.add(eng, lambda e: e.dma_start(out=out, in_=in_), reads, writes, kind="d", after=after)

    def barrier(self):
        L = set(self.last_c.values())
        for e in self.ENGS:
            L.update(self.last_d[e])
        for e in self.ENGS:
            i = self.add(e, None, (), (), kind="n")
            self.deps[i] = set(x for x in L if x != i)
        self.last_writer = {}
        self.readers = {}

    def emit(self, stack):
        nc = self.nc
        ops, deps = self.ops, self.deps
        n = len(ops)
        has_dep = [False] * n
        for i in range(n):
            for d in deps[i]:
                if ops[d][0] == "pe" and ops[i][0] == "pe" and ops[d][2] == "c" and ops[i][2] == "c":
                    continue
                has_dep[d] = True
        csem = {e: [stack.enter_context(nc.semaphore(f"c_{e}_{k}")) for k in range(self.nsem)] for e in self.ENGS}
        dsem = {e: [stack.enter_context(nc.semaphore(f"d_{e}_{k}")) for k in range(self.ndsem)] for e in self.ENGS}
        sig = [None] * n
        ccount = {e: 0 for e in self.ENGS}
        dcount = {e: 0 for e in self.ENGS}
        throttle = [None] * n
        for i, (eng, fn, kind) in enumerate(ops):
            if kind == "d":
                j = dcount[eng]
                dcount[eng] += 1
                slot, rnd = j % self.ndsem, j // self.ndsem
                sig[i] = (dsem[eng][slot], 16 * (rnd + 1), "d", eng, j)
                if rnd > 0:
                    throttle[i] = (dsem[eng][slot], 16 * rnd)
            elif kind == "cc":
                ccsem = stack.enter_context(nc.semaphore(f"cc_{i}"))
                sig[i] = (ccsem, 1, "x", eng, 0)
            elif kind == "c" and has_dep[i]:
                k = ccount[eng]
                ccount[eng] += 1
                sig[i] = (csem[eng][k % self.nsem], k // self.nsem + 1, "c", eng, k)
        per_eng = {e: [] for e in self.ENGS}
        for i, (eng, fn, kind) in enumerate(ops):
            per_eng[eng].append(i)
        self.stats = {e: len(per_eng[e]) for e in self.ENGS}
        final_d = dict(dcount)
        cc_final = []

        def make_body(eng):
            idxs = per_eng[eng]

            def body(e):
                waited = {}
                known = {}

                def wait(sem, val):
                    key = id(sem)
                    if waited.get(key, 0) >= val:
                        return
                    e.wait_ge(sem, val)
                    waited[key] = val

                for i in idxs:
                    _, fn, kind = ops[i]
                    if throttle[i] is not None:
                        wait(*throttle[i])
                    cmax = {}
                    dmax = {}
                    for d in deps[i]:
                        s = sig[d]
                        if s is None:
                            continue
                        sem, val, skind, peng, k = s
                        if skind == "c":
                            if peng == "pe" and eng == "pe" and kind == "c":
                                continue
                            if k > cmax.get(peng, (-1,))[0]:
                                cmax[peng] = (k, sem, val)
                        else:
                            key = id(sem)
                            if val > dmax.get(key, (0,))[0]:
                                dmax[key] = (val, sem)
                    for peng, (k, sem, val) in cmax.items():
                        if known.get(peng, -1) >= k:
                            continue
                        wait(sem, val)
                        known[peng] = k
                    for key, (val, sem) in dmax.items():
                        wait(sem, val)
                    if fn is None:
                        continue
                    ins = fn(e)
                    s = sig[i]
                    if s is not None:
                        ins.then_inc(s[0], 16 if s[2] == "d" else 1)
                    if kind == "cc":
                        cc_final.append(s)
                if eng == "sp":
                    for q in self.ENGS:
                        cnt = final_d[q]
                        for slot in range(min(cnt, self.ndsem)):
                            uses = (cnt - slot + self.ndsem - 1) // self.ndsem
                            wait(dsem[q][slot], 16 * uses)

            return body

        block = stack.enter_context(nc.Block())
        block.tensor(make_body("pe"))
        block.scalar(make_body("act"))
        block.vector(make_body("dve"))
        block.gpsimd(make_body("pool"))
        block.sync(make_body("sp"))


def build_nc(S, dbg=()):
    QT = S // 4
    NBA = S // 512
    NBQ = QT // 512
    NKT = S // 128
    nc = bass.Bass("TRN2", target_bir_lowering=False)
    dt_in = lambda name, shape, dt=F32: nc.dram_tensor(name, list(shape), dt, kind="ExternalInput").ap()
    dt_sc = lambda name, shape, dt: nc.dram_tensor(name, list(shape), dt, kind="Internal").ap()

    xb = dt_in("xb", [S, D])
    xq = dt_in("xq", [QT, D])
    pq = dt_in("pq", [QT, PLE])
    w_in = dt_in("w_in", [D, 3588])
    w_out = dt_in("w_out", [D, D])
    w_up = dt_in("w_up", [D, DFF])
    w_down = dt_in("w_down", [DFF, D])
    w_gate = dt_in("w_gate", [D, D])
    w_pp = dt_in("w_pp", [PLE, D])
    g_rows = dt_in("g_rows", [4, D])
    cols = dt_in("cols", [128, 16])
    fbias = dt_in("fbias", [4, 1])
    ident_in = dt_in("ident", [128, 128], BF16)
    cmask_in = dt_in("cmask", [128, 4 * 512])
    hmask_in = dt_in("hmask", [128, 128])
    rmask_in = dt_in("rmask", [128, 512])
    sel4_in = dt_in("sel4", [4, 4 * 128])
    id4_in = dt_in("id4", [4, 4])
    out = nc.dram_tensor("out", [QT, D], F32, kind="ExternalOutput").ap()

    uT = dt_sc("uT", [S // 512, 128, 32, 512], BF16)
    qa_raw = dt_sc("qa_raw", [4, 128, S], F32)
    ka_raw = dt_sc("ka_raw", [4, 128, S], F32)
    qT = dt_sc("qT", [4, 128, S], BF16)
    kT = dt_sc("kT", [4, 128, S], BF16)
    va = dt_sc("va", [4, 128, S // 128, 128], BF16)
    fa_raw = dt_sc("fa_raw", [4, S], F32)
    qbT = dt_sc("qbT", [4, 128, S], F32)
    sgT = dt_sc("sgT", [4, 128, S], F32)
    gbT = dt_sc("gbT", [4, 128, S], F32)
    vb = dt_sc("vb", [4, 128, S // 128, 128], BF16)
    rs_a = dt_sc("rs_a", [4 * 2048, QT], BF16)
    rs_b = dt_sc("rs_b", [4 * 2048, QT], BF16)
    mT_a = dt_sc("mT_a", [2048, QT], BF16)
    mT_b = dt_sc("mT_b", [2048, QT], BF16)
    h1 = dt_sc("h1", [QT, D], F32)
    umT = dt_sc("umT", [QT // 512, 128, 32, 512], BF16)
    hidT = dt_sc("hidT", [QT // 512, 128, 128, 512], BF16)
    h2 = dt_sc("h2", [QT, D], F32)
    upT = dt_sc("upT", [QT // 512, 128, 32, 512], BF16)
    eS = dt_sc("eS", [QT, D], F32)
    dbg_out = {}
    dbg_src = {}
    for name in dbg:
        src = locals()[name]
        dbg_src[name] = src
        dbg_out[name] = nc.dram_tensor("dbg_" + name, list(src.shape), src.dtype, kind="ExternalOutput").ap()

    with ExitStack() as top:
        Sc = Sched(nc)
        add, dma = Sc.add, Sc.dma

        def mm(o, lhsT, rhs, start, stop, r, w):
            add("pe", lambda e: e.matmul(o, lhsT=lhsT, rhs=rhs, start=start, stop=stop), r, w)

        def tr(o, in_, ident, r, w):
            add("pe", lambda e: e.transpose(out=o, in_=in_, identity=ident), r, w)

        def act(o, in_, func, r, w, **kw):
            add("act", lambda e: e.activation(out=o, in_=in_, func=func, **kw), r, w)

        def dve(method, r, w, **kw):
            add("dve", lambda e: getattr(e, method)(**kw), r, w)

        uid = [0]

        def sbt(st, name, shape, dt):
            uid[0] += 1
            return st.enter_context(nc.sbuf_tensor(f"{name}_{uid[0]}", list(shape), dt))

        def pst(st, name, shape, dt):
            uid[0] += 1
            Sc.excl.add(name)
            return st.enter_context(nc.psum_tensor(f"{name}_{uid[0]}", list(shape), dt))
        IDN = sbt(top, "IDN", [128, 128], BF16)
        COLS = sbt(top, "COLS", [128, 16], F32)
        CD = sbt(top, "CD", [128, 24], F32)
        ONESF = sbt(top, "ONESF", [128, 128], F32)
        ONESB = sbt(top, "ONESB", [128, 128], BF16)
        ONESM = sbt(top, "ONESM", [128, 128], BF16)
        zst = ExitStack()
        ZT = sbt(zst, "ZT", [128, 2048], BF16)
        dma("sp", IDN[:], ident_in, (), ["IDN"])
        dma("sp", COLS[:], cols, (), ["COLS"])
        add("dve", lambda e: e.memset(ONESF[:], 1.0 / 128), (), ["ONESF"])
        add("dve", lambda e: e.memset(ONESB[:], 1.0), (), ["ONESB"])
        add("dve", lambda e: e.memset(ONESM[:], 1.0 / 128), (), ["ONESM"])
        add("dve", lambda e: e.memset(ZT[:], 0.0), (), ["ZT"])
        add("dve", lambda e: e.tensor_scalar_mul(out=CD[:, 0:1], in0=COLS[:, 0:1], scalar1=1.0 / math.sqrt(HD)), ["COLS"], ["CD0"])
        add("dve", lambda e: e.tensor_sub(out=CD[:, 13:17], in0=COLS[:, 3:7], in1=COLS[:, 7:11]), ["COLS"], ["CD13"])
        act(CD[:, 1:5], CD[:, 13:17], AF.Sigmoid, ["CD13"], ["CD1"])
        act(CD[:, 5:9], CD[:, 13:17], AF.Sigmoid, ["CD13"], ["CD5"], scale=-1.0)
        add("dve", lambda e: e.tensor_scalar_mul(out=CD[:, 9:13], in0=CD[:, 5:9], scalar1=-1.0), ["CD5"], ["CD9"])
        zw = min(QT, 2048)
        for rs_ in (rs_a, rs_b):
            for i in range(4 * 2048 // 128):
                for j in range(QT // zw):
                    dma("sp", rs_[i * 128:(i + 1) * 128, j * zw:(j + 1) * zw], ZT[:, 0:zw], ["ZT"], [])
        Sc.barrier()
        zst.close()

        def norm_transpose(src, dstT, M, grow, tag):
            with ExitStack() as st:
                X = [sbt(st, f"X{i}", [128, D], F32) for i in range(4)]
                GB = sbt(st, "GB", [128, D], F32)
                JK = sbt(st, "JK", [128, D], BF16)
                U = [sbt(st, f"U{i}", [128, D], BF16) for i in range(4)]
                UT = [sbt(st, f"UT{i}", [128, 32, 512], BF16) for i in range(2)]
                SSq = [sbt(st, f"SS{i}", [128, 4], F32) for i in range(4)]
                PB = [pst(st, f"PB{i}", [128, 1024], BF16) for i in range(4)]
                dma("sp", GB[:], g_rows[grow:grow + 1, :].partition_broadcast(128), (), ["GB"])
                nt = M // 128
                for t_ in range(min(3, nt)):
                    dma("sp", X[t_][:], src[t_ * 128:(t_ + 1) * 128, :], (), [f"X{t_}"])

                def stats(t):
                    i = t % 4
                    act(JK[:], X[i][:], AF.Square, [f"X{i}"], ["JK", f"SSa{i}"], accum_out=SSq[i][:, 0:1])
                    act(SSq[i][:, 1:2], SSq[i][:, 0:1], AF.Sqrt, [f"SSa{i}"], [f"SSb{i}"], scale=1.0 / D, bias=EPS)

                def recip(t):
                    i = t % 4
                    dve("reciprocal", [f"SSb{i}"], [f"SSc{i}"], out=SSq[i][:, 2:3], in_=SSq[i][:, 1:2])

                stats(0)
                recip(0)
                for t in range(nt):
                    i = t % 4
                    g, tt = t // 4, t % 4
                    gi = g % 2
                    if t + 3 < nt:
                        dma("sp", X[(t + 3) % 4][:], src[(t + 3) * 128:(t + 4) * 128, :], (), [f"X{(t + 3) % 4}"])
                    if t + 1 < nt:
                        stats(t + 1)
                    dve("scalar_tensor_tensor", [f"X{i}", f"SSc{i}", "GB"], [f"U{i}"], out=U[i][:], in0=X[i][:],
                        scalar=SSq[i][:, 2:3], in1=GB[:], op0=ALU.mult, op1=ALU.mult)
                    if t + 1 < nt:
                        recip(t + 1)
                    for q in range(4):
                        pb = (t * 4 + q) % 4
                        for j in range(8):
                            kc = q * 8 + j
                            tr(PB[pb][:, j * 128:(j + 1) * 128], U[i][:, kc * 128:(kc + 1) * 128], IDN[:],
                               [f"U{i}", "IDN"], [f"PB{pb}"])
                        o = UT[gi][:, q * 8:(q + 1) * 8, tt * 128:(tt + 1) * 128]
                        src_ps = PB[pb][:].rearrange("p (j m) -> p j m", j=8)
                        if q % 2 == 0:
                            add("act", lambda e, o=o, s_=src_ps: e.copy(out=o, in_=s_), [f"PB{pb}"], [f"UT{gi}"])
                        else:
                            add("dve", lambda e, o=o, s_=src_ps: e.tensor_copy(out=o, in_=s_), [f"PB{pb}"], [f"UT{gi}"])
                    if tt == 3:
                        dma("pool", dstT[g], UT[gi][:], [f"UT{gi}"], [])
            Sc.barrier()

        def gemm_pass(aT, M, w, panels, extra_alloc=None, KC=32, side=None, a_after=(), nps=6, side_from=0,
                      a_view=None, post_load=None):
            with ExitStack() as st:
                nblk = M // 512
                resident = nblk <= 4
                wmax = max(p_[1] for p_ in panels)
                WP = [sbt(st, f"WP{i}", [128, KC, (wmax + 7) // 8 * 8], BF16) for i in range(2)]
                AB = [sbt(st, f"AB{i}", [128, KC, 512], BF16) for i in range(nblk if resident else 3)]
                PS = [pst(st, f"PS{i}", [128, 512], F32) for i in range(nps)]
                ctx = extra_alloc(st) if extra_alloc else None
                sgen = side(st) if side else None
                psi = 0
                seq = [(pi, bi) for pi in range(len(panels)) for bi in range(nblk)]

                def load_w(pi):
                    c0, ncol, _, _ = panels[pi]
                    wi = pi % 2
                    ids = []
                    for kg in range(KC // 8):
                        ids.append(dma("pool", WP[wi][:, kg * 8:(kg + 1) * 8, 0:ncol],
                                       w[kg * 1024:(kg + 1) * 1024, c0:c0 + ncol].rearrange("(k p) n -> p k n", p=128),
                                       (), [f"WP{wi}_{kg}"]))
                    return ids

                def load_a(n):
                    pi, bi = seq[n]
                    ai = bi if resident else n % 3
                    src = a_view(bi) if a_view else aT[bi]
                    ids = []
                    for kg in range(KC // 8):
                        ids.append(dma("sp", AB[ai][:, kg * 8:(kg + 1) * 8, :], src[:, kg * 8:(kg + 1) * 8, :], (), [f"AB{ai}_{kg}"],
                                       after=a_after))
                    return ids

                def side_step(pi):
                    if sgen is not None and pi >= side_from:
                        next(sgen, None)

                init_ids = load_w(0)
                if resident:
                    for b_ in range(nblk):
                        init_ids += load_a(b_)
                else:
                    init_ids += load_a(0)
                    init_ids += load_a(1)
                if post_load:
                    post_load(init_ids)
                if hasattr(panels[0][3], "pre"):
                    panels[0][3].pre(ctx, 0, 0, 0)
                for n, (pi, bi) in enumerate(seq):
                    c0, ncol, orient, epi = panels[pi]
                    wi = pi % 2
                    ai = bi if resident else n % 3
                    if bi == 0 and pi + 1 < len(panels):
                        load_w(pi + 1)
                    if not resident and n + 2 < len(seq):
                        load_a(n + 2)
                    nsub = (ncol + 127) // 128 if orient == "F" else 4
                    for sub in range(nsub):
                        ps = PS[psi % nps]
                        pname = f"PS{psi % nps}"
                        psi += 1
                        if orient == "F":
                            cw = min(128, ncol - sub * 128)
                            for kc in range(KC):
                                mm(ps[0:cw, :], WP[wi][:, kc, sub * 128:sub * 128 + cw], AB[ai][:, kc, :], kc == 0, kc == KC - 1,
                                   [f"WP{wi}_{kc // 8}", f"AB{ai}_{kc // 8}"], [pname])
                        else:
                            for kc in range(KC):
                                mm(ps[:, 0:ncol], AB[ai][:, kc, sub * 128:(sub + 1) * 128], WP[wi][:, kc, 0:ncol], kc == 0, kc == KC - 1,
                                   [f"WP{wi}_{kc // 8}", f"AB{ai}_{kc // 8}"], [pname])
                        nxt = None
                        if sub + 1 < nsub:
                            nxt = (pi, sub + 1, bi)
                        elif n + 1 < len(seq):
                            nxt = (seq[n + 1][0], 0, seq[n + 1][1])
                        if nxt is not None and hasattr(panels[nxt[0]][3], "pre"):
                            panels[nxt[0]][3].pre(ctx, *nxt)
                        epi(ctx, ps, pname, pi, sub, bi)
                        side_step(pi)
                if sgen is not None:
                    for _ in sgen:
                        pass
            Sc.barrier()

        norm_transpose(xb, uT, S, 0, "A")

        def alloc_B(st):
            c = {}
            c["OF"] = [sbt(st, f"OF{i}", [128, 512], F32) for i in range(3)]
            c["OB"] = [sbt(st, f"OB{i}", [128, 512], BF16) for i in range(2)]
            c["n"] = 0
            return c

        def epi_F(dst, func, tag=None):
            def epi(c, ps, pname, pi, ch, bi):
                i = c["n"] % 3
                c["n"] += 1
                o = c["OF"][i]
                act(o[:], ps[:], func, [pname], [f"OF{i}"])
                dma("sp", dst[ch, :, bi * 512:(bi + 1) * 512], o[:], [f"OF{i}"], [f"{tag}_{ch}_{bi}"] if tag else [])
            return epi

        def epi_T16(dst):
            def epi(c, ps, pname, pi, tt, bi):
                i = c["n"] % 2
                c["n"] += 1
                o = c["OB"][i]
                dve("tensor_copy", [pname], [f"OB{i}"], out=o[:], in_=ps[:])
                kt = bi * 4 + tt
                dma("sp", dst[:, :, kt, :].rearrange("h p d -> p h d"), o[:].rearrange("p (h d) -> p h d", h=4), [f"OB{i}"], [])
            return epi

        def epi_fa(c, ps, pname, pi, ch, bi):
            i = c["n"] % 3
            c["n"] += 1
            o = c["OF"][i]
            act(o[0:4, :], ps[0:4, :], AF.Copy, [pname], [f"OF{i}"])
            dma("sp", fa_raw[:, bi * 512:(bi + 1) * 512], o[0:4, :], [f"OF{i}"], [])

        epi_qa = epi_F(qa_raw, AF.Copy, "qa")

        def epi_qa_fa(c, ps, pname, pi, ch, bi):
            (epi_fa if ch == 4 else epi_qa)(c, ps, pname, pi, ch, bi)

        panels_B = [
            (0, 516, "F", epi_qa_fa),
            (516, 512, "F", epi_F(ka_raw, AF.Copy, "ka")),
            (1028, 512, "T", epi_T16(va)),
            (2564, 512, "T", epi_T16(vb)),
            (1540, 512, "F", epi_F(qbT, AF.Silu)),
            (3076, 512, "F", epi_F(gbT, AF.Silu)),
            (2052, 512, "F", epi_F(sgT, AF.Sigmoid)),
        ]
        def c_side(st):
            RAW = [sbt(st, f"RAW{i}", [128, 512], F32) for i in range(3)]
            SQ = [sbt(st, f"SQ{i}", [128, 512], F32) for i in range(3)]
            SD = [sbt(st, f"SD{i}", [128, 512], F32) for i in range(3)]
            O16 = [sbt(st, f"O16{i}", [128, 512], BF16) for i in range(3)]
            SQH = [sbt(st, f"SQH{i}", [128, 512], BF16) for i in range(3)]
            SQL = [sbt(st, f"SQL{i}", [128, 512], BF16) for i in range(3)]
            PM = [pst(st, f"PM{i}", [128, 512], F32) for i in range(2)]
            items = [(src, dst, gcol, tag, h, bi) for (src, dst, gcol, tag) in
                     ((qa_raw, qT, CD[:, 0:1], "qa"), (ka_raw, kT, COLS[:, 1:2], "ka")) for h in range(4) for bi in range(NBA)]
            N = len(items)

            def sA(n):
                src, dst, gcol, tag, h, bi = items[n]
                i = n % 3
                dma("act", RAW[i][:], src[h, :, bi * 512:(bi + 1) * 512], [f"{tag}_{h}_{bi}"], [f"RAW{i}"])
                act(SQ[i][:], RAW[i][:], AF.Square, [f"RAW{i}"], [f"SQ{i}"])
                add("act", lambda e, i=i: e.copy(out=SQH[i][:], in_=SQ[i][:]), [f"SQ{i}"], [f"SQH{i}"])
                dve("tensor_tensor", [f"SQ{i}", f"SQH{i}"], [f"SQL{i}"], out=SQL[i][:], in0=SQ[i][:], in1=SQH[i][:], op=ALU.subtract)

            def sB(n):
                i, j = n % 3, n % 2
                mm(PM[j][:], ONESM[:], SQH[i][:], True, False, [f"SQH{i}"], [f"PM{j}"])
                mm(PM[j][:], ONESM[:], SQL[i][:], False, True, [f"SQL{i}"], [f"PM{j}"])

            def sC(n):
                src, dst, gcol, tag, h, bi = items[n]
                i, j = n % 3, n % 2
                act(SD[i][:], PM[j][:], AF.Sqrt, [f"PM{j}"], [f"SD{i}"], bias=EPS, scale=1.0)
                dve("reciprocal", [f"SD{i}"], [f"SD{i}"], out=SD[i][:], in_=SD[i][:])
                dve("scalar_tensor_tensor", [f"RAW{i}", f"SD{i}"], [f"O16{i}"], out=O16[i][:],
                    in0=RAW[i][:], scalar=gcol, in1=SD[i][:], op0=ALU.mult, op1=ALU.mult)
                dma("pool", dst[h, :, bi * 512:(bi + 1) * 512], O16[i][:], [f"O16{i}"], [])

            for k in range(N + 2):
                if k < N:
                    sA(k)
                if 0 <= k - 1 < N:
                    sB(k - 1)
                if 0 <= k - 2 < N:
                    sC(k - 2)
                yield

        gemm_pass(uT, S, w_in, panels_B, alloc_B, side=c_side, nps=5, side_from=2)

        with ExitStack() as st:
            HM = sbt(st, "HM", [128, 128], F32)
            RM = sbt(st, "RM", [128, 512], F32)
            per = lambda name, shape, dt: [sbt(st, f"{name}{i}", shape, dt) for i in range(4)]
            QB, SG, GG = per("QB", [128, 512], F32), per("SG", [128, 512], F32), per("GG", [128, 512], F32)
            VB = per("VB", [128, 4, 128], BF16)
            LOGF, KBt, Bt, EB, ENB, KE = (per(nm, [128, 512], F32) for nm in ("LOGF", "KBt", "Bt", "EB", "ENB", "KE"))
            QE16, KE16, KD16 = (per(nm, [128, 512], BF16) for nm in ("QE16", "KE16", "KD16"))
            KDT = per("KDT", [128, 4, 128], BF16)
            SQh, SDh, Yh = (per(nm, [128, 512], F32) for nm in ("SQh", "SDh", "Yh"))
            OSh = per("OSh", [128, 4, 512], BF16)
            ST = [[sbt(st, f"ST{h}_{j}", [128, 128], F32) for j in range(2)] for h in range(4)]
            S16 = [[sbt(st, f"S16{h}_{j}", [128, 128], BF16) for j in range(2)] for h in range(4)]
            ATM = per("ATM", [128, 128], BF16)
            PAT = pst(st, "PAT", [128, 512], F32)
            PUs = [pst(st, f"PU{i}", [128, 512], F32) for i in range(2)]
            PMS = pst(st, "PKM", [128, 512], F32)
            PKD = PMS[:].bitcast(BF16)
            POT = [pst(st, f"POT{i}", [128, 512], F32) for i in range(4)]
            dma("sp", HM[:], hmask_in, (), ["HM"])
            dma("sp", RM[:], rmask_in, (), ["RM"])
            for h in range(4):
                add("dve", lambda e, h=h: e.memset(ST[h][0][:], 0.0), (), [f"ST{h}_0"])
                add("dve", lambda e, h=h: e.memset(S16[h][0][:], 0.0), (), [f"S16{h}_0"])

            def group_gen(heads):
                cur = {h: 0 for h in heads}
                for bi in range(NBA):
                    sl = slice(bi * 512, (bi + 1) * 512)
                    for h in heads:
                        i = h
                        dma("sp", QB[i][:], qbT[h, :, sl], (), [f"QB{i}"])
                        dma("sp", SG[i][:], sgT[h, :, sl], (), [f"SG{i}"])
                        dma("sp", GG[i][:], gbT[h, :, sl], (), [f"GG{i}"])
                        dma("sp", VB[i][:], vb[h, :, bi * 4:(bi + 1) * 4, :], (), [f"VB{i}"])
                    yield
                    for h in heads:
                        i = h
                        act(LOGF[i][:], SG[i][:], AF.Ln, [f"SG{i}"], [f"LOGF{i}"], scale=CD[:, 5 + h:6 + h], bias=CD[:, 1 + h:2 + h])
                        act(KBt[i][:], SG[i][:], AF.Identity, [f"SG{i}"], [f"KBt{i}"], scale=CD[:, 9 + h:10 + h], bias=CD[:, 5 + h:6 + h])
                        yield
                        add("dve", lambda e, i=i: e.tensor_tensor_scan(out=Bt[i][:], data0=RM[:], data1=LOGF[i][:], initial=0.0,
                                                                       op0=ALU.mult, op1=ALU.add), ["RM", f"LOGF{i}"], [f"Bt{i}"])
                        act(EB[i][:], Bt[i][:], AF.Exp, [f"Bt{i}"], [f"EB{i}"])
                        act(ENB[i][:], Bt[i][:], AF.Exp, [f"Bt{i}"], [f"ENB{i}"], scale=-1.0)
                        yield
                        dve("tensor_tensor", [f"QB{i}", f"EB{i}"], [f"QE16{i}"], out=QE16[i][:], in0=QB[i][:], in1=EB[i][:], op=ALU.mult)
                        dve("tensor_tensor", [f"KBt{i}", f"ENB{i}"], [f"KE{i}"], out=KE[i][:], in0=KBt[i][:], in1=ENB[i][:], op=ALU.mult)
                        add("act", lambda e, i=i: e.copy(out=KE16[i][:], in_=KE[i][:]), [f"KE{i}"], [f"KE16{i}"])
                        yield
                        for c in range(8):
                            cs = slice(c * 64, (c + 1) * 64)
                            dve("tensor_scalar_mul", [f"KE{i}", f"EB{i}"], [f"KD16{i}_{c}"], out=KD16[i][:, cs], in0=KE[i][:, cs],
                                scalar1=EB[i][:, c * 64 + 63:c * 64 + 64])
                            if c == 3:
                                yield
                        yield
                        hp = h % 2
                        for tt in range(4):
                            tr(PKD[:, hp * 512 + tt * 128:hp * 512 + (tt + 1) * 128], KD16[i][:, tt * 128:(tt + 1) * 128], IDN[:],
                               [f"KD16{i}_{2 * tt}", f"KD16{i}_{2 * tt + 1}", "IDN"], ["PKM"])
                        add("act", lambda e, i=i, hp=hp: e.copy(out=KDT[i][:], in_=PKD[:, hp * 512:(hp + 1) * 512].rearrange("p (t d) -> p t d", t=4)),
                            ["PKM"], [f"KDT{i}"])
                        yield
                    for tt in range(4):
                        ts_ = slice(tt * 128, (tt + 1) * 128)
                        for h in heads:
                            i = h
                            pas = slice(h * 128, (h + 1) * 128)
                            mm(PAT[:, pas], KE16[i][:, ts_], QE16[i][:, ts_], True, True, [f"KE16{i}", f"QE16{i}"], ["PAT"])
                            dve("tensor_tensor", ["PAT", "HM"], [f"ATM{h}"], out=ATM[h][:], in0=PAT[:, pas], in1=HM[:], op=ALU.mult)
                        yield
                        for c in range(2):
                            tok = tt * 128 + c * 64
                            tk = slice(tok, tok + 64)
                            for h in heads:
                                i = h
                                pus = slice(h * 128, (h + 1) * 128)
                                cu = cur[h]
                                nx = 1 - cu
                                mm(POT[h][:, tk], VB[i][:, tt, :], ATM[h][:, c * 64:(c + 1) * 64], True, False,
                                   [f"VB{i}", f"ATM{h}"], [f"POT{h}"])
                                mm(POT[h][:, tk], S16[h][cu][:], QE16[i][:, tk], False, True, [f"S16{h}_{cu}", f"QE16{i}"], [f"POT{h}"])
                                mm(PUs[h % 2][:, pus], KDT[i][c * 64:(c + 1) * 64, tt, :], VB[i][c * 64:(c + 1) * 64, tt, :], True, True,
                                   [f"KDT{i}", f"VB{i}"], [f"PU{h % 2}"])
                                dve("scalar_tensor_tensor", [f"ST{h}_{cu}", f"EB{i}", f"PU{h % 2}"], [f"ST{h}_{nx}"], out=ST[h][nx][:],
                                    in0=ST[h][cu][:], scalar=EB[i][:, tok + 63:tok + 64], in1=PUs[h % 2][:, pus], op0=ALU.mult, op1=ALU.add)
                                add("act", lambda e, h=h, nx=nx: e.copy(out=S16[h][nx][:], in_=ST[h][nx][:]), [f"ST{h}_{nx}"], [f"S16{h}_{nx}"])
                                cur[h] = nx
                            yield
                    for h in heads:
                        i = h
                        act(SQh[i][:], POT[h][:], AF.Square, [f"POT{h}"], [f"SQh{i}"])
                        mm(PMS[:], ONESF[:], SQh[i][:], True, True, ["ONESF", f"SQh{i}"], ["PKM"])
                        act(SDh[i][:], PMS[:], AF.Sqrt, ["PKM"], [f"SDh{i}"], bias=EPS, scale=1.0)
                        yield
                        dve("reciprocal", [f"SDh{i}"], [f"SDh{i}"], out=SDh[i][:], in_=SDh[i][:])
                        dve("scalar_tensor_tensor", [f"POT{h}", f"SDh{i}"], [f"Yh{i}"], out=Yh[i][:], in0=POT[h][:],
                            scalar=COLS[:, 2:3], in1=SDh[i][:], op0=ALU.mult, op1=ALU.mult)
                        dve("tensor_tensor", [f"Yh{i}", f"GG{i}"], [f"Yh{i}"], out=Yh[i][:], in0=Yh[i][:], in1=GG[i][:], op=ALU.mult)
                        yield
                        for k in range(4):
                            act(OSh[i][:, k, :], Yh[i][:], AF.Identity, [f"Yh{i}"], [f"OSh{i}"], scale=COLS[:, 11 + k:12 + k])
                        quarter, off = (bi * 512) // QT, (bi * 512) % QT
                        for k in range(4):
                            r0 = quarter * 2048 + k * 512 + h * 128
                            dma("sp", rs_b[r0:r0 + 128, off:off + 512], OSh[i][:, k, :], [f"OSh{i}"], [])
                        yield

            gA, gB = group_gen((0, 1)), group_gen((2, 3))
            for _ in range(HG_OFFSET):
                next(gA, None)
            doneA = doneB = False
            while not (doneA and doneB):
                if not doneA:
                    try:
                        next(gA)
                    except StopIteration:
                        doneA = True
                if not doneB:
                    try:
                        next(gB)
                    except StopIteration:
                        doneB = True
        Sc.barrier()

        RG = [[0, 1, 2, 3], [4, 5, 6, 7]]

        with ExitStack() as st:
            FROW = sbt(st, "FROW", [4, S], F32)
            NEGF = sbt(st, "NEGF", [128, NKT * 4], F32)
            SEL4 = sbt(st, "SEL4", [4, 512], F32)
            PF = pst(st, "PF", [128, 512], F32)
            with ExitStack() as st2:
                FTMP = sbt(st2, "FTMP", [4, S], F32)
                ONE4 = sbt(st2, "ONE4", [4, S], F32)
                NB4 = sbt(st2, "NB4", [4, 2], F32)
                ID4 = sbt(st2, "ID4", [4, 4], F32)
                dma("sp", FROW[:], fa_raw, (), ["FROW"])
                dma("sp", NB4[:, 0:1], fbias, (), ["NB4"])
                dma("sp", ID4[:], id4_in, (), ["ID4"])
                add("dve", lambda e: e.memset(ONE4[:], 1.0), (), ["ONE4"])
                add("dve", lambda e: e.tensor_scalar_mul(out=NB4[:, 1:2], in0=NB4[:, 0:1], scalar1=-1.0), ["NB4"], ["NB4b"])
                act(FTMP[:], FROW[:], AF.Exp, ["FROW", "NB4b"], ["FTMP"], bias=NB4[:, 1:2], scale=-1.0)
                act(FTMP[:], FTMP[:], AF.Ln, ["FTMP"], ["FTMP"], bias=1.0, scale=1.0)
                add("dve", lambda e: e.tensor_tensor_scan(out=FROW[:], data0=ONE4[:], data1=FTMP[:], initial=0.0,
                                                          op0=ALU.mult, op1=ALU.subtract), ["ONE4", "FTMP"], ["FROW"])
                for kt0 in range(0, NKT, 128):
                    nk = min(NKT, kt0 + 128) - kt0
                    for kt in range(kt0, kt0 + nk):
                        mm(PF[:, (kt - kt0) * 4:(kt - kt0) * 4 + 4], FROW[:, kt * 128:(kt + 1) * 128], ID4[:], True, True,
                           ["FROW", "ID4"], ["PF"])
                    add("act", lambda e, kt0=kt0, nk=nk: e.mul(out=NEGF[:, kt0 * 4:(kt0 + nk) * 4], in_=PF[:, 0:nk * 4], mul=-1.0),
                        ["PF"], ["NEGF"])
            Sc.barrier()
            CM = sbt(st, "CM", [128, 4, 512], F32)
            KTs = [sbt(st, f"KT{i}", [128, S], BF16) for i in range(2)]
            QTs = [sbt(st, f"QT{i}", [128, S], BF16) for i in range(2)]
            VT = [sbt(st, f"VT{i}", [128, NKT, 128], BF16) for i in range(2)]
            FQ = [sbt(st, f"FQ{i}", [128, 5, 512], F32) for i in range(2)]
            TMP = [sbt(st, f"TMP{i}", [128, 512], F32) for i in range(4)]
            PT = [sbt(st, f"PT{i}", [128, 512], BF16) for i in range(4)]
            RL = sbt(st, "RL", [128, 512], F32)
            OA = [sbt(st, f"OA{i}", [128, 512], F32) for i in range(2)]
            OS = [sbt(st, f"OS{i}", [128, 4, 512], BF16) for i in range(2)]
            PA = [pst(st, f"PA{i}", [128, 512], F32) for i in range(4)]
            PO = [pst(st, f"PO{i}", [128, 512], F32) for i in range(2)]
            PL = [pst(st, f"PL{i}", [128, 512], F32) for i in range(1)]
            dma("sp", SEL4[:], sel4_in, (), ["SEL4"])
            dma("sp", CM[:], cmask_in.rearrange("p (j n) -> p j n", j=4), (), ["CM"])

            LA = 3
            blocks = [(h, qb) for h in range(4) for qb in range(NBA)]
            tiles = []
            for n_, (h, qb) in enumerate(blocks):
                nkt = 4 * (qb + 1)
                for kt in range(nkt):
                    tiles.append((n_, h, qb, kt, nkt))

            def load_head(h):
                hi = h % 2
                return [dma("sp", KTs[hi][:], kT[h], (), [f"KT{hi}"]),
                        dma("sp", QTs[hi][:], qT[h], (), [f"QT{hi}"]),
                        dma("sp", VT[hi][:], va[h], (), [f"VT{hi}"])]

            def prologue(n_):
                h, qb = blocks[n_]
                qi = n_ % 2
                qs = slice(qb * 512, (qb + 1) * 512)
                mm(PF[:], SEL4[:, h * 128:(h + 1) * 128], FROW[:, qs], True, True, ["SEL4", "FROW"], ["PF"])
                add("act", lambda e: e.copy(out=FQ[qi][:, 0, :], in_=PF[:]), ["PF"], [f"FQ{qi}"])
                for d in range(4):
                    add("pool", lambda e, d=d: e.tensor_add(out=FQ[qi][:, 1 + d, :], in0=CM[:, d, :], in1=FQ[qi][:, 0, :]),
                        [f"FQ{qi}", "CM"], [f"FQm{qi}_{d}"])

            def front(i):
                n_, h, qb, kt, nkt = tiles[i]
                hi, qi, a = h % 2, n_ % 2, i % 4
                d = kt - 4 * qb
                mm(PA[a][:], KTs[hi][:, kt * 128:(kt + 1) * 128], QTs[hi][:, qb * 512:(qb + 1) * 512], True, True,
                   [f"KT{hi}", f"QT{hi}"], [f"PA{a}"])
                if d >= 0:
                    fq, fslot = FQ[qi][:, 1 + d, :], f"FQm{qi}_{d}"
                else:
                    fq, fslot = FQ[qi][:, 0, :], f"FQ{qi}"
                dve("tensor_tensor", [f"PA{a}", fslot], [f"TMP{a}"], out=TMP[a][:], in0=PA[a][:], in1=fq, op=ALU.add)
                act(PT[a][:], TMP[a][:], AF.Exp, [f"TMP{a}", "NEGF"], [f"PT{a}"],
                    bias=NEGF[:, kt * 4 + h:kt * 4 + h + 1], scale=1.0)

            def back(i):
                n_, h, qb, kt, nkt = tiles[i]
                hi, qi, a = h % 2, n_ % 2, i % 4
                mm(PO[qi][:], VT[hi][:, kt, :], PT[a][:], kt == 0, kt == nkt - 1, [f"VT{hi}", f"PT{a}"], [f"PO{qi}"])
                mm(PL[0][:], ONESB[:], PT[a][:], kt == 0, kt == nkt - 1, ["ONESB", f"PT{a}"], ["PL0"])
                if kt == nkt - 1:
                    dve("reciprocal", ["PL0"], ["RL"], out=RL[:], in_=PL[0][:])
                    dve("tensor_tensor", [f"PO{qi}", "RL"], [f"OA{qi}"], out=OA[qi][:], in0=PO[qi][:], in1=RL[:], op=ALU.mult)
                    for k in range(4):
                        act(OS[qi][:, k, :], OA[qi][:], AF.Identity, [f"OA{qi}", "COLS"], [f"OS{qi}"], scale=COLS[:, 11 + k:12 + k])
                    quarter, off = (qb * 512) // QT, (qb * 512) % QT
                    for k in range(4):
                        r0 = quarter * 2048 + k * 512 + h * 128
                        dma("sp", rs_a[r0:r0 + 128, off:off + 512], OS[qi][:, k, :], [f"OS{qi}"], [])

            hl = load_head(0) + load_head(1)
            rsb = add("pool", lambda e: e.collective_compute("ReduceScatter", ALU.add, replica_groups=RG, ins=[rs_b], outs=[mT_b], dma_qos="P3"),
                      (), (), kind="cc", after=hl)
            prologue(0)
            for i in range(len(tiles) + LA):
                if i < len(tiles):
                    n_, h, qb, kt, nkt = tiles[i]
                    if kt == 0 and n_ + 1 < len(blocks):
                        prologue(n_ + 1)
                    front(i)
                if i - LA >= 0:
                    back(i - LA)
                    n_, h, qb, kt, nkt = tiles[i - LA]
                    if qb == NBA - 1 and kt == nkt - 1 and h + 2 < 4:
                        load_head(h + 2)
        Sc.barrier()

        def alloc_res(st):
            c = {"XR": [sbt(st, f"XR{i}", [128, 512], F32) for i in range(2)],
                 "ER": [sbt(st, f"ER{i}", [128, 512], F32) for i in range(2)],
                 "OF": [sbt(st, f"OF{i}", [128, 512], F32) for i in range(2)],
                 "OB": [sbt(st, f"OB{i}", [128, 512], BF16) for i in range(2)], "n": 0}
            return c

        def epi_res(prev, dst):
            def pre(c, pi, tt, bi):
                i = c.setdefault("pn", 0) % 2
                c["pn"] += 1
                r0, c0 = bi * 512 + tt * 128, pi * 512
                dma("sp", c["XR"][i][:], prev[r0:r0 + 128, c0:c0 + 512], (), [f"XR{i}"])

            def epi(c, ps, pname, pi, tt, bi):
                i = c["n"] % 2
                c["n"] += 1
                r0, c0 = bi * 512 + tt * 128, pi * 512
                dve("tensor_tensor", [pname, f"XR{i}"], [f"OF{i}"], out=c["OF"][i][:], in0=ps[:], in1=c["XR"][i][:], op=ALU.add)
                dma("sp", dst[r0:r0 + 128, c0:c0 + 512], c["OF"][i][:], [f"OF{i}"], [])
            epi.pre = pre
            return epi

        def ple_side(st):
            WPP = sbt(st, "WPP", [128, 2, D], BF16)
            GB = sbt(st, "GBp", [128, D], F32)
            PQ = [sbt(st, f"PQ{i}", [128, PLE], F32) for i in range(2)]
            P16 = [sbt(st, f"P16{i}", [128, PLE], BF16) for i in range(2)]
            PTt = [sbt(st, f"PTt{i}", [128, 2, 128], BF16) for i in range(2)]
            EPs = [sbt(st, f"EP{i}", [128, D], F32) for i in range(2)]
            JK = sbt(st, "JKp", [128, D], BF16)
            SSq = [sbt(st, f"SSp{i}", [128, 4], F32) for i in range(2)]
            PSs = [pst(st, f"PSp{i}", [128, 512], F32) for i in range(2)]
            PBp = pst(st, "PBp", [128, 1024], BF16)
            dma("pool", WPP[:], w_pp.rearrange("(k p) n -> p k n", p=128), (), ["WPP"])
            dma("sp", GB[:], g_rows[3:4, :].partition_broadcast(128), (), ["GBp"])
            yield
            for t in range(QT // 128):
                i = t % 2
                EP = EPs[i]
                EPn = f"EP{i}"
                dma("act", PQ[i][:], pq[t * 128:(t + 1) * 128, :], (), [f"PQ{i}"])
                dve("tensor_copy", [f"PQ{i}"], [f"P16{i}"], out=P16[i][:], in_=PQ[i][:])
                yield
                for kc in range(2):
                    tr(PBp[:, kc * 128:(kc + 1) * 128], P16[i][:, kc * 128:(kc + 1) * 128], IDN[:], [f"P16{i}"], ["PBp"])
                add("act", lambda e, i=i: e.copy(out=PTt[i][:], in_=PBp[:, 0:256].rearrange("p (k m) -> p k m", k=2)), ["PBp"], [f"PTt{i}"])
                yield
                for nb in range(8):
                    p_ = nb % 2
                    for kc in range(2):
                        mm(PSs[p_][:], PTt[i][:, kc, :], WPP[:, kc, nb * 512:(nb + 1) * 512], kc == 0, kc == 1,
                           [f"PTt{i}", "WPP"], [f"PSp{p_}"])
                    add("act", lambda e, nb=nb, p_=p_, EP=EP: e.copy(out=EP[:, nb * 512:(nb + 1) * 512], in_=PSs[p_][:]),
                        [f"PSp{p_}"], [EPn])
                    if nb % 2 == 1:
                        yield
                act(JK[:], EP[:], AF.Square, [EPn], ["JKp", f"SSa{i}"], accum_out=SSq[i][:, 0:1])
                act(SSq[i][:, 1:2], SSq[i][:, 0:1], AF.Sqrt, [f"SSa{i}"], [f"SSb{i}"], scale=1.0 / D, bias=EPS)
                dve("reciprocal", [f"SSb{i}"], [f"SSc{i}"], out=SSq[i][:, 2:3], in_=SSq[i][:, 1:2])
                dve("scalar_tensor_tensor", [EPn, f"SSc{i}", "GBp"], [EPn], out=EP[:], in0=EP[:],
                    scalar=SSq[i][:, 2:3], in1=GB[:], op0=ALU.mult, op1=ALU.mult)
                dma("pool", eS[t * 128:(t + 1) * 128, :], EP[:], [EPn], [])
                yield

        mTb3 = mT_b.rearrange("(k p) m -> k p m", p=128)
        mTa3 = mT_a.rearrange("(k p) m -> k p m", p=128)
        rs_ops = {}

        def issue_rsa(init_ids):
            rs_ops["a"] = add("pool", lambda e: e.collective_compute("ReduceScatter", ALU.add, replica_groups=RG, ins=[rs_a],
                                                                      outs=[mT_a], dma_qos="P3"), (), (), kind="cc", after=init_ids)

        gemm_pass(mTb3, QT, w_out[2048:4096, :],
                  [(pi * 512, 512, "T", epi_res(xq, h1)) for pi in range(8)], alloc_res, KC=16, side=ple_side, a_after=[rsb], nps=5,
                  a_view=lambda bi: mTb3[:, :, bi * 512:(bi + 1) * 512].rearrange("k p m -> p k m"), post_load=issue_rsa)
        gemm_pass(mTa3, QT, w_out[0:2048, :],
                  [(pi * 512, 512, "T", epi_res(h1, h1)) for pi in range(8)], alloc_res, KC=16, a_after=[rs_ops["a"]],
                  a_view=lambda bi: mTa3[:, :, bi * 512:(bi + 1) * 512].rearrange("k p m -> p k m"))

        norm_transpose(h1, umT, QT, 1, "H")

        def epi_up(c, ps, pname, pi, ch, bi):
            i = c["n"] % 2
            c["n"] += 1
            act(c["OF"][i][:], ps[:], AF.Relu, [pname], [f"OF{i}"])
            dve("tensor_tensor", [f"OF{i}"], [f"OB{i}"], out=c["OB"][i][:], in0=c["OF"][i][:], in1=c["OF"][i][:], op=ALU.mult)
            dma("sp", hidT[bi, :, pi * 4 + ch, :], c["OB"][i][:], [f"OB{i}"], [])

        gemm_pass(umT, QT, w_up, [(pi * 512, 512, "F", epi_up) for pi in range(32)], alloc_res)
        for q in range(4):
            gemm_pass(hidT, QT, w_down[q * D:(q + 1) * D, :],
                      [(pi * 512, 512, "T", epi_res(h1 if q == 0 else h2, h2)) for pi in range(8)], alloc_res,
                      a_view=lambda bi, q=q: hidT[bi, :, q * 32:(q + 1) * 32, :])

        norm_transpose(h2, upT, QT, 2, "K")

        def epi_gate(c, ps, pname, pi, tt, bi):
            i = c["n"] % 2
            c["n"] += 1
            r0, c0 = bi * 512 + tt * 128, pi * 512
            dma("sp", c["XR"][i][:], h2[r0:r0 + 128, c0:c0 + 512], (), [f"XR{i}"])
            dma("sp", c["ER"][i][:], eS[r0:r0 + 128, c0:c0 + 512], (), [f"ER{i}"])
            act(c["OF"][i][:], ps[:], AF.Sigmoid, [pname], [f"OF{i}"])
            dve("tensor_tensor", [f"OF{i}", f"ER{i}"], [f"OF{i}"], out=c["OF"][i][:], in0=c["OF"][i][:], in1=c["ER"][i][:], op=ALU.mult)
            dve("tensor_tensor", [f"OF{i}", f"XR{i}"], [f"OF{i}"], out=c["OF"][i][:], in0=c["OF"][i][:], in1=c["XR"][i][:], op=ALU.add)
            dma("sp", out[r0:r0 + 128, c0:c0 + 512], c["OF"][i][:], [f"OF{i}"], [])

        gemm_pass(upT, QT, w_gate, [(pi * 512, 512, "T", epi_gate) for pi in range(8)], alloc_res)

        for name, dst in dbg_out.items():
            src = dbg_src[name]
            dma("sp", dst, src, (), [])
        Sc.emit(top)
        print("ops per engine:", Sc.stats, flush=True)
    return nc


def host_consts():
    ident = np.eye(128, dtype=np.float32).astype(ml_dtypes.bfloat16)
    p = np.arange(128)[:, None]
    j = np.arange(512)[None, :]
    cm = np.zeros((128, 4, 512), np.float32)
    for d in range(4):
        cm[:, d, :] = np.where(j >= 128 * d + p, 0.0, NEG)
    s = np.arange(128)[:, None]
    t = np.arange(128)[None, :]
    hm = ((s // 64 == t // 64) & (s <= t)).astype(np.float32)
    rm = np.ones((128, 512), np.float32)
    rm[:, ::64] = 0.0
    sel4 = np.zeros((4, 4, 128), np.float32)
    for h in range(4):
        sel4[h, h, :] = 1.0
    return {"ident": ident, "cmask": cm.reshape(128, 2048), "hmask": hm, "rmask": rm,
            "sel4": sel4.reshape(4, 512), "id4": np.eye(4, dtype=np.float32)}


def make_in_maps(inp, S):
    QT = S // 4
    f = lambda a: np.ascontiguousarray(np.asarray(a, dtype=np.float32))
    x, p = f(inp["x"]), f(inp["p"])
    w_in = f(inp["w_in"])[0]
    consts = host_consts()
    g_rows = np.stack([f(inp["norm_mix_g"])[0], f(inp["norm_mlp_g"])[0], f(inp["ple_norm_g"])[0], f(inp["ple_post_g"])[0]])
    lbl = f(inp["hgrn_lb_logits"])
    shared = {"w_out": f(inp["w_out"])[0], "w_up": f(inp["w_up"])[0], "w_down": f(inp["w_down"])[0],
              "w_gate": f(inp["w_ple_gate"])[0], "w_pp": f(inp["w_ple_proj"])[0], "g_rows": g_rows}
    shared.update(consts)
    w_in_r = []
    for r in range(4):
        sl = lambda o: w_in[:, o + r * 512:o + (r + 1) * 512]
        w_in_r.append(np.ascontiguousarray(np.concatenate(
            [sl(0), w_in[:, 6144 + 4 * r:6144 + 4 * r + 4], sl(2048), sl(4096), sl(6160), sl(8208), sl(10256), sl(12304)], axis=1)))
    maps = []
    for c in range(8):
        b, r = c // 4, c % 4
        cols = np.zeros((128, 16), np.float32)
        cols[:, 0] = f(inp["fox_q_norm_g"])[0]
        cols[:, 1] = f(inp["fox_k_norm_g"])[0]
        cols[:, 2] = f(inp["hgrn_norm_g"])[0]
        cols[:, 3:7] = lbl[0].reshape(16, 128)[4 * r:4 * r + 4].T
        cols[:, 7:11] = lbl[1].reshape(16, 128)[4 * r:4 * r + 4].T
        cols[:, 11 + r] = 1.0
        m = dict(shared)
        m.update({"xb": x[b], "xq": np.ascontiguousarray(x[b, r * QT:(r + 1) * QT]),
                  "pq": np.ascontiguousarray(p[0, b, r * QT:(r + 1) * QT]), "w_in": w_in_r[r], "cols": cols,
                  "fbias": np.ascontiguousarray(f(inp["fox_f_bias"])[0, 4 * r:4 * r + 4].reshape(4, 1))})
        maps.append(m)
    return maps


_NC_CACHE = {}


def kernel(**inputs):
    S = int(np.asarray(inputs["x"]).shape[1])
    QT = S // 4
    if S not in _NC_CACHE:
        _NC_CACHE[S] = build_nc(S)
    nc = _NC_CACHE[S]
    maps = make_in_maps(inputs, S)
    res = run_bass_kernel_spmd(nc, maps, core_ids=list(range(8)))
    outp = np.empty((2, S, D), np.float32)
    for c in range(8):
        b, r = c // 4, c % 4
        outp[b, r * QT:(r + 1) * QT] = res.results[c]["out"]
    return outp
```

```python
import math
from contextlib import ExitStack

import numpy as np
import ml_dtypes
import concourse.bass as bass
import concourse.mybir as mybir
from concourse.bass_utils import run_bass_kernel_spmd

F32 = mybir.dt.float32
BF16 = mybir.dt.bfloat16
AF = mybir.ActivationFunctionType
ALU = mybir.AluOpType

D = 4096
HD = 128
DFF = 16384
PLE = 256
EPS = 1e-6
NEG = -1.0e30
HG_OFFSET = 16


class Sched:
    ENGS = ("pe", "act", "dve", "pool", "sp")

    def __init__(self, nc, nsem=4, ndsem=8):
        self.nc = nc
        self.ops = []
        self.deps = []
        self.last_writer = {}
        self.readers = {}
        self.nsem = nsem
        self.ndsem = ndsem
        self.last_c = {}
        self.last_d = {e: [] for e in self.ENGS}
        self.excl = set()

    def add(self, eng, fn, reads=(), writes=(), kind="c", after=()):
        i = len(self.ops)
        d = set(after)
        if self.excl:
            ex = [s for s in reads if s in self.excl]
            if ex:
                reads = [s for s in reads if s not in self.excl]
                writes = list(writes) + ex
        for s in reads:
            w = self.last_writer.get(s)
            if w is not None:
                d.add(w)
        for s in writes:
            w = self.last_writer.get(s)
            if w is not None:
                d.add(w)
            d.update(self.readers.get(s, ()))
        for s in reads:
            self.readers.setdefault(s, []).append(i)
        for s in writes:
            self.last_writer[s] = i
            self.readers[s] = []
        d.discard(i)
        self.ops.append((eng, fn, kind))
        self.deps.append(d)
        if kind == "d":
            self.last_d[eng] = (self.last_d[eng] + [i])[-self.ndsem:]
        elif kind != "cc":
            self.last_c[eng] = i
        return i

    def dma(self, eng, out, in_, reads=(), writes=(), after=()):
        return self.add(eng, lambda e: e.dma_start(out=out, in_=in_), reads, writes, kind="d", after=after)

    def barrier(self):
        L = set(self.last_c.values())
        for e in self.ENGS:
            L.update(self.last_d[e])
        for e in self.ENGS:
            i = self.add(e, None, (), (), kind="n")
            self.deps[i] = set(x for x in L if x != i)
        self.last_writer = {}
        self.readers = {}

    def emit(self, stack):
        nc = self.nc
        ops, deps = self.ops, self.deps
        n = len(ops)
        has_dep = [False] * n
        for i in range(n):
            for d in deps[i]:
                if ops[d][0] == "pe" and ops[i][0] == "pe" and ops[d][2] == "c" and ops[i][2] == "c":
                    continue
                has_dep[d] = True
        csem = {e: [stack.enter_context(nc.semaphore(f"c_{e}_{k}")) for k in range(self.nsem)] for e in self.ENGS}
        dsem = {e: [stack.enter_context(nc.semaphore(f"d_{e}_{k}")) for k in range(self.ndsem)] for e in self.ENGS}
        sig = [None] * n
        ccount = {e: 0 for e in self.ENGS}
        dcount = {e: 0 for e in self.ENGS}
        throttle = [None] * n
        for i, (eng, fn, kind) in enumerate(ops):
            if kind == "d":
                j = dcount[eng]
                dcount[eng] += 1
                slot, rnd = j % self.ndsem, j // self.ndsem
                sig[i] = (dsem[eng][slot], 16 * (rnd + 1), "d", eng, j)
                if rnd > 0:
                    throttle[i] = (dsem[eng][slot], 16 * rnd)
            elif kind == "cc":
                ccsem = stack.enter_context(nc.semaphore(f"cc_{i}"))
                sig[i] = (ccsem, 1, "x", eng, 0)
            elif kind == "c" and has_dep[i]:
                k = ccount[eng]
                ccount[eng] += 1
                sig[i] = (csem[eng][k % self.nsem], k // self.nsem + 1, "c", eng, k)
        per_eng = {e: [] for e in self.ENGS}
        for i, (eng, fn, kind) in enumerate(ops):
            per_eng[eng].append(i)
        self.stats = {e: len(per_eng[e]) for e in self.ENGS}
        final_d = dict(dcount)
        cc_final = []

        def make_body(eng):
            idxs = per_eng[eng]

            def body(e):
                waited = {}
                known = {}

                def wait(sem, val):
                    key = id(sem)
                    if waited.get(key, 0) >= val:
                        return
                    e.wait_ge(sem, val)
                    waited[key] = val

                for i in idxs:
                    _, fn, kind = ops[i]
                    if throttle[i] is not None:
                        wait(*throttle[i])
                    cmax = {}
                    dmax = {}
                    for d in deps[i]:
                        s = sig[d]
                        if s is None:
                            continue
                        sem, val, skind, peng, k = s
                        if skind == "c":
                            if peng == "pe" and eng == "pe" and kind == "c":
                                continue
                            if k > cmax.get(peng, (-1,))[0]:
                                cmax[peng] = (k, sem, val)
                        else:
                            key = id(sem)
                            if val > dmax.get(key, (0,))[0]:
                                dmax[key] = (val, sem)
                    for peng, (k, sem, val) in cmax.items():
                        if known.get(peng, -1) >= k:
                            continue
                        wait(sem, val)
                        known[peng] = k
                    for key, (val, sem) in dmax.items():
                        wait(sem, val)
                    if fn is None:
                        continue
                    ins = fn(e)
                    s = sig[i]
                    if s is not None:
                        ins.then_inc(s[0], 16 if s[2] == "d" else 1)
                    if kind == "cc":
                        cc_final.append(s)
                if eng == "sp":
                    for q in self.ENGS:
                        cnt = final_d[q]
                        for slot in range(min(cnt, self.ndsem)):
                            uses = (cnt - slot + self.ndsem - 1) // self.ndsem
                            wait(dsem[q][slot], 16 * uses)

            return body

        block = stack.enter_context(nc.Block())
        block.tensor(make_body("pe"))
        block.scalar(make_body("act"))
        block.vector(make_body("dve"))
        block.gpsimd(make_body("pool"))
        block.sync(make_body("sp"))


def build_nc(S, dbg=()):
    QT = S // 4
    NBA = S // 512
    NBQ = QT // 512
    NKT = S // 128
    nc = bass.Bass("TRN2", target_bir_lowering=False)
    dt_in = lambda name, shape, dt=F32: nc.dram_tensor(name, list(shape), dt, kind="ExternalInput").ap()
    dt_sc = lambda name, shape, dt: nc.dram_tensor(name, list(shape), dt, kind="Internal").ap()

    xb = dt_in("xb", [S, D])
    xq = dt_in("xq", [QT, D])
    pq = dt_in("pq", [QT, PLE])
    w_in = dt_in("w_in", [D, 3588])
    w_out = dt_in("w_out", [D, D])
    w_up = dt_in("w_up", [D, DFF])
    w_down = dt_in("w_down", [DFF, D])
    w_gate = dt_in("w_gate", [D, D])
    w_pp = dt_in("w_pp", [PLE, D])
    g_rows = dt_in("g_rows", [4, D])
    cols = dt_in("cols", [128, 16])
    fbias = dt_in("fbias", [4, 1])
    ident_in = dt_in("ident", [128, 128], BF16)
    cmask_in = dt_in("cmask", [128, 4 * 512])
    hmask_in = dt_in("hmask", [128, 128])
    rmask_in = dt_in("rmask", [128, 512])
    sel4_in = dt_in("sel4", [4, 4 * 128])
    id4_in = dt_in("id4", [4, 4])
    out = nc.dram_tensor("out", [QT, D], F32, kind="ExternalOutput").ap()

    uT = dt_sc("uT", [S // 512, 128, 32, 512], BF16)
    qa_raw = dt_sc("qa_raw", [4, 128, S], F32)
    ka_raw = dt_sc("ka_raw", [4, 128, S], F32)
    qT = dt_sc("qT", [4, 128, S], BF16)
    kT = dt_sc("kT", [4, 128, S], BF16)
    va = dt_sc("va", [4, 128, S // 128, 128], BF16)
    fa_raw = dt_sc("fa_raw", [4, S], F32)
    qbT = dt_sc("qbT", [4, 128, S], F32)
    sgT = dt_sc("sgT", [4, 128, S], F32)
    gbT = dt_sc("gbT", [4, 128, S], F32)
    vb = dt_sc("vb", [4, 128, S // 128, 128], BF16)
    rs_a = dt_sc("rs_a", [4 * 2048, QT], BF16)
    rs_b = dt_sc("rs_b", [4 * 2048, QT], BF16)
    mT_a = dt_sc("mT_a", [2048, QT], BF16)
    mT_b = dt_sc("mT_b", [2048, QT], BF16)
    h1 = dt_sc("h1", [QT, D], F32)
    umT = dt_sc("umT", [QT // 512, 128, 32, 512], BF16)
    hidT = dt_sc("hidT", [QT // 512, 128, 128, 512], BF16)
    h2 = dt_sc("h2", [QT, D], F32)
    upT = dt_sc("upT", [QT // 512, 128, 32, 512], BF16)
    eS = dt_sc("eS", [QT, D], F32)
    dbg_out = {}
    dbg_src = {}
    for name in dbg:
        src = locals()[name]
        dbg_src[name] = src
        dbg_out[name] = nc.dram_tensor("dbg_" + name, list(src.shape), src.dtype, kind="ExternalOutput").ap()

    with ExitStack() as top:
        Sc = Sched(nc)
        add, dma = Sc.add, Sc.dma

        def mm(o, lhsT, rhs, start, stop, r, w):
            add("pe", lambda e: e.matmul(o, lhsT=lhsT, rhs=rhs, start=start, stop=stop), r, w)

        def tr(o, in_, ident, r, w):
            add("pe", lambda e: e.transpose(out=o, in_=in_, identity=ident), r, w)

        def act(o, in_, func, r, w, **kw):
            add("act", lambda e: e.activation(out=o, in_=in_, func=func, **kw), r, w)

        def dve(method, r, w, **kw):
            add("dve", lambda e: getattr(e, method)(**kw), r, w)

        uid = [0]

        def sbt(st, name, shape, dt):
            uid[0] += 1
            return st.enter_context(nc.sbuf_tensor(f"{name}_{uid[0]}", list(shape), dt))

        def pst(st, name, shape, dt):
            uid[0] += 1
            Sc.excl.add(name)
            return st.enter_context(nc.psum_tensor(f"{name}_{uid[0]}", list(shape), dt))
        IDN = sbt(top, "IDN", [128, 128], BF16)
        COLS = sbt(top, "COLS", [128, 16], F32)
        CD = sbt(top, "CD", [128, 24], F32)
        ONESF = sbt(top, "ONESF", [128, 128], F32)
        ONESB = sbt(top, "ONESB", [128, 128], BF16)
        zst = ExitStack()
        ZT = sbt(zst, "ZT", [128, 2048], BF16)
        dma("sp", IDN[:], ident_in, (), ["IDN"])
        dma("sp", COLS[:], cols, (), ["COLS"])
        add("dve", lambda e: e.memset(ONESF[:], 1.0 / 128), (), ["ONESF"])
        add("dve", lambda e: e.memset(ONESB[:], 1.0), (), ["ONESB"])
        add("dve", lambda e: e.memset(ZT[:], 0.0), (), ["ZT"])
        add("dve", lambda e: e.tensor_scalar_mul(out=CD[:, 0:1], in0=COLS[:, 0:1], scalar1=1.0 / math.sqrt(HD)), ["COLS"], ["CD0"])
        add("dve", lambda e: e.tensor_sub(out=CD[:, 13:17], in0=COLS[:, 3:7], in1=COLS[:, 7:11]), ["COLS"], ["CD13"])
        act(CD[:, 1:5], CD[:, 13:17], AF.Sigmoid, ["CD13"], ["CD1"])
        act(CD[:, 5:9], CD[:, 13:17], AF.Sigmoid, ["CD13"], ["CD5"], scale=-1.0)
        add("dve", lambda e: e.tensor_scalar_mul(out=CD[:, 9:13], in0=CD[:, 5:9], scalar1=-1.0), ["CD5"], ["CD9"])
        zw = min(QT, 2048)
        for rs_ in (rs_a, rs_b):
            for i in range(4 * 2048 // 128):
                for j in range(QT // zw):
                    dma("sp", rs_[i * 128:(i + 1) * 128, j * zw:(j + 1) * zw], ZT[:, 0:zw], ["ZT"], [])
        Sc.barrier()
        zst.close()

        def norm_transpose(src, dstT, M, grow, tag):
            with ExitStack() as st:
                X = [sbt(st, f"X{i}", [128, D], F32) for i in range(4)]
                GB = sbt(st, "GB", [128, D], F32)
                JK = sbt(st, "JK", [128, D], BF16)
                U = [sbt(st, f"U{i}", [128, D], BF16) for i in range(4)]
                UT = [sbt(st, f"UT{i}", [128, 32, 512], BF16) for i in range(2)]
                SSq = [sbt(st, f"SS{i}", [128, 4], F32) for i in range(4)]
                PB = [pst(st, f"PB{i}", [128, 1024], BF16) for i in range(4)]
                dma("sp", GB[:], g_rows[grow:grow + 1, :].partition_broadcast(128), (), ["GB"])
                nt = M // 128
                for t_ in range(min(3, nt)):
                    dma("sp", X[t_][:], src[t_ * 128:(t_ + 1) * 128, :], (), [f"X{t_}"])

                def stats(t):
                    i = t % 4
                    act(JK[:], X[i][:], AF.Square, [f"X{i}"], ["JK", f"SSa{i}"], accum_out=SSq[i][:, 0:1])
                    act(SSq[i][:, 1:2], SSq[i][:, 0:1], AF.Sqrt, [f"SSa{i}"], [f"SSb{i}"], scale=1.0 / D, bias=EPS)

                def recip(t):
                    i = t % 4
                    dve("reciprocal", [f"SSb{i}"], [f"SSc{i}"], out=SSq[i][:, 2:3], in_=SSq[i][:, 1:2])

                stats(0)
                recip(0)
                for t in range(nt):
                    i = t % 4
                    g, tt = t // 4, t % 4
                    gi = g % 2
                    if t + 3 < nt:
                        dma("sp", X[(t + 3) % 4][:], src[(t + 3) * 128:(t + 4) * 128, :], (), [f"X{(t + 3) % 4}"])
                    if t + 1 < nt:
                        stats(t + 1)
                    dve("scalar_tensor_tensor", [f"X{i}", f"SSc{i}", "GB"], [f"U{i}"], out=U[i][:], in0=X[i][:],
                        scalar=SSq[i][:, 2:3], in1=GB[:], op0=ALU.mult, op1=ALU.mult)
                    if t + 1 < nt:
                        recip(t + 1)
                    for q in range(4):
                        pb = (t * 4 + q) % 4
                        for j in range(8):
                            kc = q * 8 + j
                            tr(PB[pb][:, j * 128:(j + 1) * 128], U[i][:, kc * 128:(kc + 1) * 128], IDN[:],
                               [f"U{i}", "IDN"], [f"PB{pb}"])
                        o = UT[gi][:, q * 8:(q + 1) * 8, tt * 128:(tt + 1) * 128]
                        src_ps = PB[pb][:].rearrange("p (j m) -> p j m", j=8)
                        if q % 2 == 0:
                            add("act", lambda e, o=o, s_=src_ps: e.copy(out=o, in_=s_), [f"PB{pb}"], [f"UT{gi}"])
                        else:
                            add("dve", lambda e, o=o, s_=src_ps: e.tensor_copy(out=o, in_=s_), [f"PB{pb}"], [f"UT{gi}"])
                    if tt == 3:
                        dma("pool", dstT[g], UT[gi][:], [f"UT{gi}"], [])
            Sc.barrier()

        def gemm_pass(aT, M, w, panels, extra_alloc=None, KC=32, side=None, a_after=(), nps=6, side_from=0,
                      a_view=None, post_load=None):
            with ExitStack() as st:
                nblk = M // 512
                resident = nblk <= 4
                wmax = max(p_[1] for p_ in panels)
                WP = [sbt(st, f"WP{i}", [128, KC, (wmax + 7) // 8 * 8], BF16) for i in range(2)]
                AB = [sbt(st, f"AB{i}", [128, KC, 512], BF16) for i in range(nblk if resident else 3)]
                PS = [pst(st, f"PS{i}", [128, 512], F32) for i in range(nps)]
                ctx = extra_alloc(st) if extra_alloc else None
                sgen = side(st) if side else None
                psi = 0
                seq = [(pi, bi) for pi in range(len(panels)) for bi in range(nblk)]

                def load_w(pi):
                    c0, ncol, _, _ = panels[pi]
                    wi = pi % 2
                    ids = []
                    for kg in range(KC // 8):
                        ids.append(dma("pool", WP[wi][:, kg * 8:(kg + 1) * 8, 0:ncol],
                                       w[kg * 1024:(kg + 1) * 1024, c0:c0 + ncol].rearrange("(k p) n -> p k n", p=128),
                                       (), [f"WP{wi}_{kg}"]))
                    return ids

                def load_a(n):
                    pi, bi = seq[n]
                    ai = bi if resident else n % 3
                    src = a_view(bi) if a_view else aT[bi]
                    ids = []
                    for kg in range(KC // 8):
                        ids.append(dma("sp", AB[ai][:, kg * 8:(kg + 1) * 8, :], src[:, kg * 8:(kg + 1) * 8, :], (), [f"AB{ai}_{kg}"],
                                       after=a_after))
                    return ids

                def side_step(pi):
                    if sgen is not None and pi >= side_from:
                        next(sgen, None)

                init_ids = load_w(0)
                if resident:
                    for b_ in range(nblk):
                        init_ids += load_a(b_)
                else:
                    init_ids += load_a(0)
                    init_ids += load_a(1)
                if post_load:
                    post_load(init_ids)
                if hasattr(panels[0][3], "pre"):
                    panels[0][3].pre(ctx, 0, 0, 0)
                for n, (pi, bi) in enumerate(seq):
                    c0, ncol, orient, epi = panels[pi]
                    wi = pi % 2
                    ai = bi if resident else n % 3
                    if bi == 0 and pi + 1 < len(panels):
                        load_w(pi + 1)
                    if not resident and n + 2 < len(seq):
                        load_a(n + 2)
                    nsub = (ncol + 127) // 128 if orient == "F" else 4
                    for sub in range(nsub):
                        ps = PS[psi % nps]
                        pname = f"PS{psi % nps}"
                        psi += 1
                        if orient == "F":
                            cw = min(128, ncol - sub * 128)
                            for kc in range(KC):
                                mm(ps[0:cw, :], WP[wi][:, kc, sub * 128:sub * 128 + cw], AB[ai][:, kc, :], kc == 0, kc == KC - 1,
                                   [f"WP{wi}_{kc // 8}", f"AB{ai}_{kc // 8}"], [pname])
                        else:
                            for kc in range(KC):
                                mm(ps[:, 0:ncol], AB[ai][:, kc, sub * 128:(sub + 1) * 128], WP[wi][:, kc, 0:ncol], kc == 0, kc == KC - 1,
                                   [f"WP{wi}_{kc // 8}", f"AB{ai}_{kc // 8}"], [pname])
                        nxt = None
                        if sub + 1 < nsub:
                            nxt = (pi, sub + 1, bi)
                        elif n + 1 < len(seq):
                            nxt = (seq[n + 1][0], 0, seq[n + 1][1])
                        if nxt is not None and hasattr(panels[nxt[0]][3], "pre"):
                            panels[nxt[0]][3].pre(ctx, *nxt)
                        epi(ctx, ps, pname, pi, sub, bi)
                        side_step(pi)
                if sgen is not None:
                    for _ in sgen:
                        pass
            Sc.barrier()

        norm_transpose(xb, uT, S, 0, "A")

        def alloc_B(st):
            c = {}
            c["OF"] = [sbt(st, f"OF{i}", [128, 512], F32) for i in range(3)]
            c["OB"] = [sbt(st, f"OB{i}", [128, 512], BF16) for i in range(2)]
            c["n"] = 0
            return c

        def epi_F(dst, func, tag=None):
            def epi(c, ps, pname, pi, ch, bi):
                i = c["n"] % 3
                c["n"] += 1
                o = c["OF"][i]
                act(o[:], ps[:], func, [pname], [f"OF{i}"])
                dma("sp", dst[ch, :, bi * 512:(bi + 1) * 512], o[:], [f"OF{i}"], [f"{tag}_{ch}_{bi}"] if tag else [])
            return epi

        def epi_T16(dst):
            def epi(c, ps, pname, pi, tt, bi):
                i = c["n"] % 2
                c["n"] += 1
                o = c["OB"][i]
                dve("tensor_copy", [pname], [f"OB{i}"], out=o[:], in_=ps[:])
                kt = bi * 4 + tt
                dma("sp", dst[:, :, kt, :].rearrange("h p d -> p h d"), o[:].rearrange("p (h d) -> p h d", h=4), [f"OB{i}"], [])
            return epi

        def epi_fa(c, ps, pname, pi, ch, bi):
            i = c["n"] % 3
            c["n"] += 1
            o = c["OF"][i]
            act(o[0:4, :], ps[0:4, :], AF.Copy, [pname], [f"OF{i}"])
            dma("sp", fa_raw[:, bi * 512:(bi + 1) * 512], o[0:4, :], [f"OF{i}"], [])

        epi_qa = epi_F(qa_raw, AF.Copy, "qa")

        def epi_qa_fa(c, ps, pname, pi, ch, bi):
            (epi_fa if ch == 4 else epi_qa)(c, ps, pname, pi, ch, bi)

        panels_B = [
            (0, 516, "F", epi_qa_fa),
            (516, 512, "F", epi_F(ka_raw, AF.Copy, "ka")),
            (1028, 512, "T", epi_T16(va)),
            (2564, 512, "T", epi_T16(vb)),
            (1540, 512, "F", epi_F(qbT, AF.Silu)),
            (3076, 512, "F", epi_F(gbT, AF.Silu)),
            (2052, 512, "F", epi_F(sgT, AF.Sigmoid)),
        ]
        def c_side(st):
            RAW = [sbt(st, f"RAW{i}", [128, 512], F32) for i in range(3)]
            SQ = [sbt(st, f"SQ{i}", [128, 512], F32) for i in range(3)]
            SD = [sbt(st, f"SD{i}", [128, 512], F32) for i in range(3)]
            O16 = [sbt(st, f"O16{i}", [128, 512], BF16) for i in range(3)]
            PM = [pst(st, f"PM{i}", [128, 512], F32) for i in range(2)]
            items = [(src, dst, gcol, tag, h, bi) for (src, dst, gcol, tag) in
                     ((qa_raw, qT, CD[:, 0:1], "qa"), (ka_raw, kT, COLS[:, 1:2], "ka")) for h in range(4) for bi in range(NBA)]
            N = len(items)

            def sA(n):
                src, dst, gcol, tag, h, bi = items[n]
                i = n % 3
                dma("act", RAW[i][:], src[h, :, bi * 512:(bi + 1) * 512], [f"{tag}_{h}_{bi}"], [f"RAW{i}"])
                act(SQ[i][:], RAW[i][:], AF.Square, [f"RAW{i}"], [f"SQ{i}"])

            def sB(n):
                i, j = n % 3, n % 2
                mm(PM[j][:], ONESF[:], SQ[i][:], True, True, [f"SQ{i}"], [f"PM{j}"])

            def sC(n):
                src, dst, gcol, tag, h, bi = items[n]
                i, j = n % 3, n % 2
                act(SD[i][:], PM[j][:], AF.Sqrt, [f"PM{j}"], [f"SD{i}"], bias=EPS, scale=1.0)
                dve("reciprocal", [f"SD{i}"], [f"SD{i}"], out=SD[i][:], in_=SD[i][:])
                dve("scalar_tensor_tensor", [f"RAW{i}", f"SD{i}"], [f"O16{i}"], out=O16[i][:],
                    in0=RAW[i][:], scalar=gcol, in1=SD[i][:], op0=ALU.mult, op1=ALU.mult)
                dma("pool", dst[h, :, bi * 512:(bi + 1) * 512], O16[i][:], [f"O16{i}"], [])

            for k in range(N + 2):
                if k < N:
                    sA(k)
                if 0 <= k - 1 < N:
                    sB(k - 1)
                if 0 <= k - 2 < N:
                    sC(k - 2)
                yield

        gemm_pass(uT, S, w_in, panels_B, alloc_B, side=c_side, nps=5, side_from=2)

        with ExitStack() as st:
            HM = sbt(st, "HM", [128, 128], F32)
            RM = sbt(st, "RM", [128, 512], F32)
            per = lambda name, shape, dt: [sbt(st, f"{name}{i}", shape, dt) for i in range(4)]
            QB, SG, GG = per("QB", [128, 512], F32), per("SG", [128, 512], F32), per("GG", [128, 512], F32)
            VB = per("VB", [128, 4, 128], BF16)
            LOGF, KBt, Bt, EB, ENB, KE = (per(nm, [128, 512], F32) for nm in ("LOGF", "KBt", "Bt", "EB", "ENB", "KE"))
            QE16, KE16, KD16 = (per(nm, [128, 512], BF16) for nm in ("QE16", "KE16", "KD16"))
            KDT = per("KDT", [128, 4, 128], BF16)
            SQh, SDh, Yh = (per(nm, [128, 512], F32) for nm in ("SQh", "SDh", "Yh"))
            OSh = per("OSh", [128, 4, 512], BF16)
            ST = [[sbt(st, f"ST{h}_{j}", [128, 128], F32) for j in range(2)] for h in range(4)]
            S16 = [[sbt(st, f"S16{h}_{j}", [128, 128], BF16) for j in range(2)] for h in range(4)]
            ATM = per("ATM", [128, 128], BF16)
            PAT = pst(st, "PAT", [128, 512], F32)
            PUs = [pst(st, f"PU{i}", [128, 512], F32) for i in range(2)]
            PMS = pst(st, "PKM", [128, 512], F32)
            PKD = PMS[:].bitcast(BF16)
            POT = [pst(st, f"POT{i}", [128, 512], F32) for i in range(4)]
            dma("sp", HM[:], hmask_in, (), ["HM"])
            dma("sp", RM[:], rmask_in, (), ["RM"])
            for h in range(4):
                add("dve", lambda e, h=h: e.memset(ST[h][0][:], 0.0), (), [f"ST{h}_0"])
                add("dve", lambda e, h=h: e.memset(S16[h][0][:], 0.0), (), [f"S16{h}_0"])

            def group_gen(heads):
                cur = {h: 0 for h in heads}
                for bi in range(NBA):
                    sl = slice(bi * 512, (bi + 1) * 512)
                    for h in heads:
                        i = h
                        dma("sp", QB[i][:], qbT[h, :, sl], (), [f"QB{i}"])
                        dma("sp", SG[i][:], sgT[h, :, sl], (), [f"SG{i}"])
                        dma("sp", GG[i][:], gbT[h, :, sl], (), [f"GG{i}"])
                        dma("sp", VB[i][:], vb[h, :, bi * 4:(bi + 1) * 4, :], (), [f"VB{i}"])
                    yield
                    for h in heads:
                        i = h
                        act(LOGF[i][:], SG[i][:], AF.Ln, [f"SG{i}"], [f"LOGF{i}"], scale=CD[:, 5 + h:6 + h], bias=CD[:, 1 + h:2 + h])
                        act(KBt[i][:], SG[i][:], AF.Identity, [f"SG{i}"], [f"KBt{i}"], scale=CD[:, 9 + h:10 + h], bias=CD[:, 5 + h:6 + h])
                        yield
                        add("dve", lambda e, i=i: e.tensor_tensor_scan(out=Bt[i][:], data0=RM[:], data1=LOGF[i][:], initial=0.0,
                                                                       op0=ALU.mult, op1=ALU.add), ["RM", f"LOGF{i}"], [f"Bt{i}"])
                        act(EB[i][:], Bt[i][:], AF.Exp, [f"Bt{i}"], [f"EB{i}"])
                        act(ENB[i][:], Bt[i][:], AF.Exp, [f"Bt{i}"], [f"ENB{i}"], scale=-1.0)
                        yield
                        dve("tensor_tensor", [f"QB{i}", f"EB{i}"], [f"QE16{i}"], out=QE16[i][:], in0=QB[i][:], in1=EB[i][:], op=ALU.mult)
                        dve("tensor_tensor", [f"KBt{i}", f"ENB{i}"], [f"KE{i}"], out=KE[i][:], in0=KBt[i][:], in1=ENB[i][:], op=ALU.mult)
                        add("act", lambda e, i=i: e.copy(out=KE16[i][:], in_=KE[i][:]), [f"KE{i}"], [f"KE16{i}"])
                        yield
                        for c in range(8):
                            cs = slice(c * 64, (c + 1) * 64)
                            dve("tensor_scalar_mul", [f"KE{i}", f"EB{i}"], [f"KD16{i}_{c}"], out=KD16[i][:, cs], in0=KE[i][:, cs],
                                scalar1=EB[i][:, c * 64 + 63:c * 64 + 64])
                            if c == 3:
                                yield
                        yield
                        hp = h % 2
                        for tt in range(4):
                            tr(PKD[:, hp * 512 + tt * 128:hp * 512 + (tt + 1) * 128], KD16[i][:, tt * 128:(tt + 1) * 128], IDN[:],
                               [f"KD16{i}_{2 * tt}", f"KD16{i}_{2 * tt + 1}", "IDN"], ["PKM"])
                        add("act", lambda e, i=i, hp=hp: e.copy(out=KDT[i][:], in_=PKD[:, hp * 512:(hp + 1) * 512].rearrange("p (t d) -> p t d", t=4)),
                            ["PKM"], [f"KDT{i}"])
                        yield
                    for tt in range(4):
                        ts_ = slice(tt * 128, (tt + 1) * 128)
                        for h in heads:
                            i = h
                            pas = slice(h * 128, (h + 1) * 128)
                            mm(PAT[:, pas], KE16[i][:, ts_], QE16[i][:, ts_], True, True, [f"KE16{i}", f"QE16{i}"], ["PAT"])
                            dve("tensor_tensor", ["PAT", "HM"], [f"ATM{h}"], out=ATM[h][:], in0=PAT[:, pas], in1=HM[:], op=ALU.mult)
                        yield
                        for c in range(2):
                            tok = tt * 128 + c * 64
                            tk = slice(tok, tok + 64)
                            for h in heads:
                                i = h
                                pus = slice(h * 128, (h + 1) * 128)
                                cu = cur[h]
                                nx = 1 - cu
                                mm(POT[h][:, tk], VB[i][:, tt, :], ATM[h][:, c * 64:(c + 1) * 64], True, False,
                                   [f"VB{i}", f"ATM{h}"], [f"POT{h}"])
                                mm(POT[h][:, tk], S16[h][cu][:], QE16[i][:, tk], False, True, [f"S16{h}_{cu}", f"QE16{i}"], [f"POT{h}"])
                                mm(PUs[h % 2][:, pus], KDT[i][c * 64:(c + 1) * 64, tt, :], VB[i][c * 64:(c + 1) * 64, tt, :], True, True,
                                   [f"KDT{i}", f"VB{i}"], [f"PU{h % 2}"])
                                dve("scalar_tensor_tensor", [f"ST{h}_{cu}", f"EB{i}", f"PU{h % 2}"], [f"ST{h}_{nx}"], out=ST[h][nx][:],
                                    in0=ST[h][cu][:], scalar=EB[i][:, tok + 63:tok + 64], in1=PUs[h % 2][:, pus], op0=ALU.mult, op1=ALU.add)
                                add("act", lambda e, h=h, nx=nx: e.copy(out=S16[h][nx][:], in_=ST[h][nx][:]), [f"ST{h}_{nx}"], [f"S16{h}_{nx}"])
                                cur[h] = nx
                            yield
                    for h in heads:
                        i = h
                        act(SQh[i][:], POT[h][:], AF.Square, [f"POT{h}"], [f"SQh{i}"])
                        mm(PMS[:], ONESF[:], SQh[i][:], True, True, ["ONESF", f"SQh{i}"], ["PKM"])
                        act(SDh[i][:], PMS[:], AF.Sqrt, ["PKM"], [f"SDh{i}"], bias=EPS, scale=1.0)
                        yield
                        dve("reciprocal", [f"SDh{i}"], [f"SDh{i}"], out=SDh[i][:], in_=SDh[i][:])
                        dve("scalar_tensor_tensor", [f"POT{h}", f"SDh{i}"], [f"Yh{i}"], out=Yh[i][:], in0=POT[h][:],
                            scalar=COLS[:, 2:3], in1=SDh[i][:], op0=ALU.mult, op1=ALU.mult)
                        dve("tensor_tensor", [f"Yh{i}", f"GG{i}"], [f"Yh{i}"], out=Yh[i][:], in0=Yh[i][:], in1=GG[i][:], op=ALU.mult)
                        yield
                        for k in range(4):
                            act(OSh[i][:, k, :], Yh[i][:], AF.Identity, [f"Yh{i}"], [f"OSh{i}"], scale=COLS[:, 11 + k:12 + k])
                        quarter, off = (bi * 512) // QT, (bi * 512) % QT
                        for k in range(4):
                            r0 = quarter * 2048 + k * 512 + h * 128
                            dma("sp", rs_b[r0:r0 + 128, off:off + 512], OSh[i][:, k, :], [f"OSh{i}"], [])
                        yield

            gA, gB = group_gen((0, 1)), group_gen((2, 3))
            for _ in range(HG_OFFSET):
                next(gA, None)
            doneA = doneB = False
            while not (doneA and doneB):
                if not doneA:
                    try:
                        next(gA)
                    except StopIteration:
                        doneA = True
                if not doneB:
                    try:
                        next(gB)
                    except StopIteration:
                        doneB = True
        Sc.barrier()

        RG = [[0, 1, 2, 3], [4, 5, 6, 7]]

        with ExitStack() as st:
            FROW = sbt(st, "FROW", [4, S], F32)
            NEGF = sbt(st, "NEGF", [128, NKT * 4], F32)
            SEL4 = sbt(st, "SEL4", [4, 512], F32)
            PF = pst(st, "PF", [128, 512], F32)
            with ExitStack() as st2:
                FTMP = sbt(st2, "FTMP", [4, S], F32)
                ONE4 = sbt(st2, "ONE4", [4, S], F32)
                NB4 = sbt(st2, "NB4", [4, 2], F32)
                ID4 = sbt(st2, "ID4", [4, 4], F32)
                dma("sp", FROW[:], fa_raw, (), ["FROW"])
                dma("sp", NB4[:, 0:1], fbias, (), ["NB4"])
                dma("sp", ID4[:], id4_in, (), ["ID4"])
                add("dve", lambda e: e.memset(ONE4[:], 1.0), (), ["ONE4"])
                add("dve", lambda e: e.tensor_scalar_mul(out=NB4[:, 1:2], in0=NB4[:, 0:1], scalar1=-1.0), ["NB4"], ["NB4b"])
                act(FTMP[:], FROW[:], AF.Exp, ["FROW", "NB4b"], ["FTMP"], bias=NB4[:, 1:2], scale=-1.0)
                act(FTMP[:], FTMP[:], AF.Ln, ["FTMP"], ["FTMP"], bias=1.0, scale=1.0)
                add("dve", lambda e: e.tensor_tensor_scan(out=FROW[:], data0=ONE4[:], data1=FTMP[:], initial=0.0,
                                                          op0=ALU.mult, op1=ALU.subtract), ["ONE4", "FTMP"], ["FROW"])
                for kt0 in range(0, NKT, 128):
                    nk = min(NKT, kt0 + 128) - kt0
                    for kt in range(kt0, kt0 + nk):
                        mm(PF[:, (kt - kt0) * 4:(kt - kt0) * 4 + 4], FROW[:, kt * 128:(kt + 1) * 128], ID4[:], True, True,
                           ["FROW", "ID4"], ["PF"])
                    add("act", lambda e, kt0=kt0, nk=nk: e.mul(out=NEGF[:, kt0 * 4:(kt0 + nk) * 4], in_=PF[:, 0:nk * 4], mul=-1.0),
                        ["PF"], ["NEGF"])
            Sc.barrier()
            CM = sbt(st, "CM", [128, 4, 512], F32)
            KTs = [sbt(st, f"KT{i}", [128, S], BF16) for i in range(2)]
            QTs = [sbt(st, f"QT{i}", [128, S], BF16) for i in range(2)]
            VT = [sbt(st, f"VT{i}", [128, NKT, 128], BF16) for i in range(2)]
            FQ = [sbt(st, f"FQ{i}", [128, 5, 512], F32) for i in range(2)]
            TMP = [sbt(st, f"TMP{i}", [128, 512], F32) for i in range(4)]
            PT = [sbt(st, f"PT{i}", [128, 512], BF16) for i in range(4)]
            RL = sbt(st, "RL", [128, 512], F32)
            OA = [sbt(st, f"OA{i}", [128, 512], F32) for i in range(2)]
            OS = [sbt(st, f"OS{i}", [128, 4, 512], BF16) for i in range(2)]
            PA = [pst(st, f"PA{i}", [128, 512], F32) for i in range(4)]
            PO = [pst(st, f"PO{i}", [128, 512], F32) for i in range(2)]
            PL = [pst(st, f"PL{i}", [128, 512], F32) for i in range(1)]
            dma("sp", SEL4[:], sel4_in, (), ["SEL4"])
            dma("sp", CM[:], cmask_in.rearrange("p (j n) -> p j n", j=4), (), ["CM"])

            LA = 3
            blocks = [(h, qb) for h in range(4) for qb in range(NBA)]
            tiles = []
            for n_, (h, qb) in enumerate(blocks):
                nkt = 4 * (qb + 1)
                for kt in range(nkt):
                    tiles.append((n_, h, qb, kt, nkt))

            def load_head(h):
                hi = h % 2
                return [dma("sp", KTs[hi][:], kT[h], (), [f"KT{hi}"]),
                        dma("sp", QTs[hi][:], qT[h], (), [f"QT{hi}"]),
                        dma("sp", VT[hi][:], va[h], (), [f"VT{hi}"])]

            def prologue(n_):
                h, qb = blocks[n_]
                qi = n_ % 2
                qs = slice(qb * 512, (qb + 1) * 512)
                mm(PF[:], SEL4[:, h * 128:(h + 1) * 128], FROW[:, qs], True, True, ["SEL4", "FROW"], ["PF"])
                add("act", lambda e: e.copy(out=FQ[qi][:, 0, :], in_=PF[:]), ["PF"], [f"FQ{qi}"])
                for d in range(4):
                    add("pool", lambda e, d=d: e.tensor_add(out=FQ[qi][:, 1 + d, :], in0=CM[:, d, :], in1=FQ[qi][:, 0, :]),
                        [f"FQ{qi}", "CM"], [f"FQm{qi}_{d}"])

            def front(i):
                n_, h, qb, kt, nkt = tiles[i]
                hi, qi, a = h % 2, n_ % 2, i % 4
                d = kt - 4 * qb
                cs = slice(128 * d, 512) if d > 0 else slice(0, 512)
                qcs = slice(qb * 512 + cs.start, (qb + 1) * 512)
                mm(PA[a][:, cs], KTs[hi][:, kt * 128:(kt + 1) * 128], QTs[hi][:, qcs], True, True,
                   [f"KT{hi}", f"QT{hi}"], [f"PA{a}"])
                if d >= 0:
                    fq, fslot = FQ[qi][:, 1 + d, cs], f"FQm{qi}_{d}"
                else:
                    fq, fslot = FQ[qi][:, 0, cs], f"FQ{qi}"
                dve("tensor_tensor", [f"PA{a}", fslot], [f"TMP{a}"], out=TMP[a][:, cs], in0=PA[a][:, cs], in1=fq, op=ALU.add)
                act(PT[a][:, cs], TMP[a][:, cs], AF.Exp, [f"TMP{a}", "NEGF"], [f"PT{a}"],
                    bias=NEGF[:, kt * 4 + h:kt * 4 + h + 1], scale=1.0)

            def back(i):
                n_, h, qb, kt, nkt = tiles[i]
                hi, qi, a = h % 2, n_ % 2, i % 4
                d = kt - 4 * qb
                cs = slice(128 * d, 512) if d > 0 else slice(0, 512)
                mm(PO[qi][:, cs], VT[hi][:, kt, :], PT[a][:, cs], kt == 0, kt == nkt - 1, [f"VT{hi}", f"PT{a}"], [f"PO{qi}"])
                mm(PL[0][:, cs], ONESB[:], PT[a][:, cs], kt == 0, kt == nkt - 1, ["ONESB", f"PT{a}"], ["PL0"])
                if kt == nkt - 1:
                    dve("reciprocal", ["PL0"], ["RL"], out=RL[:], in_=PL[0][:])
                    dve("tensor_tensor", [f"PO{qi}", "RL"], [f"OA{qi}"], out=OA[qi][:], in0=PO[qi][:], in1=RL[:], op=ALU.mult)
                    for k in range(4):
                        act(OS[qi][:, k, :], OA[qi][:], AF.Identity, [f"OA{qi}", "COLS"], [f"OS{qi}"], scale=COLS[:, 11 + k:12 + k])
                    quarter, off = (qb * 512) // QT, (qb * 512) % QT
                    for k in range(4):
                        r0 = quarter * 2048 + k * 512 + h * 128
                        dma("sp", rs_a[r0:r0 + 128, off:off + 512], OS[qi][:, k, :], [f"OS{qi}"], [])

            hl = load_head(0) + load_head(1)
            rsb = add("pool", lambda e: e.collective_compute("ReduceScatter", ALU.add, replica_groups=RG, ins=[rs_b], outs=[mT_b], dma_qos="P3"),
                      (), (), kind="cc", after=hl)
            prologue(0)
            for i in range(len(tiles) + LA):
                if i < len(tiles):
                    n_, h, qb, kt, nkt = tiles[i]
                    if kt == 0 and n_ + 1 < len(blocks):
                        prologue(n_ + 1)
                    front(i)
                if i - LA >= 0:
                    back(i - LA)
                    n_, h, qb, kt, nkt = tiles[i - LA]
                    if qb == NBA - 1 and kt == nkt - 1 and h + 2 < 4:
                        load_head(h + 2)
        Sc.barrier()

        def alloc_res(st):
            c = {"XR": [sbt(st, f"XR{i}", [128, 512], F32) for i in range(2)],
                 "ER": [sbt(st, f"ER{i}", [128, 512], F32) for i in range(2)],
                 "OF": [sbt(st, f"OF{i}", [128, 512], F32) for i in range(2)],
                 "OB": [sbt(st, f"OB{i}", [128, 512], BF16) for i in range(2)], "n": 0}
            return c

        def epi_res(prev, dst):
            def pre(c, pi, tt, bi):
                i = c.setdefault("pn", 0) % 2
                c["pn"] += 1
                r0, c0 = bi * 512 + tt * 128, pi * 512
                dma("sp", c["XR"][i][:], prev[r0:r0 + 128, c0:c0 + 512], (), [f"XR{i}"])

            def epi(c, ps, pname, pi, tt, bi):
                i = c["n"] % 2
                c["n"] += 1
                r0, c0 = bi * 512 + tt * 128, pi * 512
                dve("tensor_tensor", [pname, f"XR{i}"], [f"OF{i}"], out=c["OF"][i][:], in0=ps[:], in1=c["XR"][i][:], op=ALU.add)
                dma("sp", dst[r0:r0 + 128, c0:c0 + 512], c["OF"][i][:], [f"OF{i}"], [])
            epi.pre = pre
            return epi

        def ple_side(st):
            WPP = sbt(st, "WPP", [128, 2, D], BF16)
            GB = sbt(st, "GBp", [128, D], F32)
            PQ = [sbt(st, f"PQ{i}", [128, PLE], F32) for i in range(2)]
            P16 = [sbt(st, f"P16{i}", [128, PLE], BF16) for i in range(2)]
            PTt = [sbt(st, f"PTt{i}", [128, 2, 128], BF16) for i in range(2)]
            EPs = [sbt(st, f"EP{i}", [128, D], F32) for i in range(2)]
            JK = sbt(st, "JKp", [128, D], BF16)
            SSq = [sbt(st, f"SSp{i}", [128, 4], F32) for i in range(2)]
            PSs = [pst(st, f"PSp{i}", [128, 512], F32) for i in range(2)]
            PBp = pst(st, "PBp", [128, 1024], BF16)
            dma("pool", WPP[:], w_pp.rearrange("(k p) n -> p k n", p=128), (), ["WPP"])
            dma("sp", GB[:], g_rows[3:4, :].partition_broadcast(128), (), ["GBp"])
            yield
            for t in range(QT // 128):
                i = t % 2
                EP = EPs[i]
                EPn = f"EP{i}"
                dma("act", PQ[i][:], pq[t * 128:(t + 1) * 128, :], (), [f"PQ{i}"])
                dve("tensor_copy", [f"PQ{i}"], [f"P16{i}"], out=P16[i][:], in_=PQ[i][:])
                yield
                for kc in range(2):
                    tr(PBp[:, kc * 128:(kc + 1) * 128], P16[i][:, kc * 128:(kc + 1) * 128], IDN[:], [f"P16{i}"], ["PBp"])
                add("act", lambda e, i=i: e.copy(out=PTt[i][:], in_=PBp[:, 0:256].rearrange("p (k m) -> p k m", k=2)), ["PBp"], [f"PTt{i}"])
                yield
                for nb in range(8):
                    p_ = nb % 2
                    for kc in range(2):
                        mm(PSs[p_][:], PTt[i][:, kc, :], WPP[:, kc, nb * 512:(nb + 1) * 512], kc == 0, kc == 1,
                           [f"PTt{i}", "WPP"], [f"PSp{p_}"])
                    add("act", lambda e, nb=nb, p_=p_, EP=EP: e.copy(out=EP[:, nb * 512:(nb + 1) * 512], in_=PSs[p_][:]),
                        [f"PSp{p_}"], [EPn])
                    if nb % 2 == 1:
                        yield
                act(JK[:], EP[:], AF.Square, [EPn], ["JKp", f"SSa{i}"], accum_out=SSq[i][:, 0:1])
                act(SSq[i][:, 1:2], SSq[i][:, 0:1], AF.Sqrt, [f"SSa{i}"], [f"SSb{i}"], scale=1.0 / D, bias=EPS)
                dve("reciprocal", [f"SSb{i}"], [f"SSc{i}"], out=SSq[i][:, 2:3], in_=SSq[i][:, 1:2])
                dve("scalar_tensor_tensor", [EPn, f"SSc{i}", "GBp"], [EPn], out=EP[:], in0=EP[:],
                    scalar=SSq[i][:, 2:3], in1=GB[:], op0=ALU.mult, op1=ALU.mult)
                dma("pool", eS[t * 128:(t + 1) * 128, :], EP[:], [EPn], [])
                yield

        mTb3 = mT_b.rearrange("(k p) m -> k p m", p=128)
        mTa3 = mT_a.rearrange("(k p) m -> k p m", p=128)
        rs_ops = {}

        def issue_rsa(init_ids):
            rs_ops["a"] = add("pool", lambda e: e.collective_compute("ReduceScatter", ALU.add, replica_groups=RG, ins=[rs_a],
                                                                      outs=[mT_a], dma_qos="P3"), (), (), kind="cc", after=init_ids)

        gemm_pass(mTb3, QT, w_out[2048:4096, :],
                  [(pi * 512, 512, "T", epi_res(xq, h1)) for pi in range(8)], alloc_res, KC=16, side=ple_side, a_after=[rsb], nps=5,
                  a_view=lambda bi: mTb3[:, :, bi * 512:(bi + 1) * 512].rearrange("k p m -> p k m"), post_load=issue_rsa)
        gemm_pass(mTa3, QT, w_out[0:2048, :],
                  [(pi * 512, 512, "T", epi_res(h1, h1)) for pi in range(8)], alloc_res, KC=16, a_after=[rs_ops["a"]],
                  a_view=lambda bi: mTa3[:, :, bi * 512:(bi + 1) * 512].rearrange("k p m -> p k m"))

        norm_transpose(h1, umT, QT, 1, "H")

        def epi_up(c, ps, pname, pi, ch, bi):
            i = c["n"] % 2
            c["n"] += 1
            act(c["OF"][i][:], ps[:], AF.Relu, [pname], [f"OF{i}"])
            dve("tensor_tensor", [f"OF{i}"], [f"OB{i}"], out=c["OB"][i][:], in0=c["OF"][i][:], in1=c["OF"][i][:], op=ALU.mult)
            dma("sp", hidT[bi, :, pi * 4 + ch, :], c["OB"][i][:], [f"OB{i}"], [])

        gemm_pass(umT, QT, w_up, [(pi * 512, 512, "F", epi_up) for pi in range(32)], alloc_res)
        for q in range(4):
            gemm_pass(hidT, QT, w_down[q * D:(q + 1) * D, :],
                      [(pi * 512, 512, "T", epi_res(h1 if q == 0 else h2, h2)) for pi in range(8)], alloc_res,
                      a_view=lambda bi, q=q: hidT[bi, :, q * 32:(q + 1) * 32, :])

        norm_transpose(h2, upT, QT, 2, "K")

        def epi_gate(c, ps, pname, pi, tt, bi):
            i = c["n"] % 2
            c["n"] += 1
            r0, c0 = bi * 512 + tt * 128, pi * 512
            dma("sp", c["XR"][i][:], h2[r0:r0 + 128, c0:c0 + 512], (), [f"XR{i}"])
            dma("sp", c["ER"][i][:], eS[r0:r0 + 128, c0:c0 + 512], (), [f"ER{i}"])
            act(c["OF"][i][:], ps[:], AF.Sigmoid, [pname], [f"OF{i}"])
            dve("tensor_tensor", [f"OF{i}", f"ER{i}"], [f"OF{i}"], out=c["OF"][i][:], in0=c["OF"][i][:], in1=c["ER"][i][:], op=ALU.mult)
            dve("tensor_tensor", [f"OF{i}", f"XR{i}"], [f"OF{i}"], out=c["OF"][i][:], in0=c["OF"][i][:], in1=c["XR"][i][:], op=ALU.add)
            dma("sp", out[r0:r0 + 128, c0:c0 + 512], c["OF"][i][:], [f"OF{i}"], [])

        gemm_pass(upT, QT, w_gate, [(pi * 512, 512, "T", epi_gate) for pi in range(8)], alloc_res)

        for name, dst in dbg_out.items():
            src = dbg_src[name]
            dma("sp", dst, src, (), [])
        Sc.emit(top)
        print("ops per engine:", Sc.stats, flush=True)
    return nc


def host_consts():
    ident = np.eye(128, dtype=np.float32).astype(ml_dtypes.bfloat16)
    p = np.arange(128)[:, None]
    j = np.arange(512)[None, :]
    cm = np.zeros((128, 4, 512), np.float32)
    for d in range(4):
        cm[:, d, :] = np.where(j >= 128 * d + p, 0.0, NEG)
    s = np.arange(128)[:, None]
    t = np.arange(128)[None, :]
    hm = ((s // 64 == t // 64) & (s <= t)).astype(np.float32)
    rm = np.ones((128, 512), np.float32)
    rm[:, ::64] = 0.0
    sel4 = np.zeros((4, 4, 128), np.float32)
    for h in range(4):
        sel4[h, h, :] = 1.0
    return {"ident": ident, "cmask": cm.reshape(128, 2048), "hmask": hm, "rmask": rm,
            "sel4": sel4.reshape(4, 512), "id4": np.eye(4, dtype=np.float32)}


def make_in_maps(inp, S):
    QT = S // 4
    f = lambda a: np.ascontiguousarray(np.asarray(a, dtype=np.float32))
    x, p = f(inp["x"]), f(inp["p"])
    w_in = f(inp["w_in"])[0]
    consts = host_consts()
    g_rows = np.stack([f(inp["norm_mix_g"])[0], f(inp["norm_mlp_g"])[0], f(inp["ple_norm_g"])[0], f(inp["ple_post_g"])[0]])
    lbl = f(inp["hgrn_lb_logits"])
    shared = {"w_out": f(inp["w_out"])[0], "w_up": f(inp["w_up"])[0], "w_down": f(inp["w_down"])[0],
              "w_gate": f(inp["w_ple_gate"])[0], "w_pp": f(inp["w_ple_proj"])[0], "g_rows": g_rows}
    shared.update(consts)
    w_in_r = []
    for r in range(4):
        sl = lambda o: w_in[:, o + r * 512:o + (r + 1) * 512]
        w_in_r.append(np.ascontiguousarray(np.concatenate(
            [sl(0), w_in[:, 6144 + 4 * r:6144 + 4 * r + 4], sl(2048), sl(4096), sl(6160), sl(8208), sl(10256), sl(12304)], axis=1)))
    maps = []
    for c in range(8):
        b, r = c // 4, c % 4
        cols = np.zeros((128, 16), np.float32)
        cols[:, 0] = f(inp["fox_q_norm_g"])[0]
        cols[:, 1] = f(inp["fox_k_norm_g"])[0]
        cols[:, 2] = f(inp["hgrn_norm_g"])[0]
        cols[:, 3:7] = lbl[0].reshape(16, 128)[4 * r:4 * r + 4].T
        cols[:, 7:11] = lbl[1].reshape(16, 128)[4 * r:4 * r + 4].T
        cols[:, 11 + r] = 1.0
        m = dict(shared)
        m.update({"xb": x[b], "xq": np.ascontiguousarray(x[b, r * QT:(r + 1) * QT]),
                  "pq": np.ascontiguousarray(p[0, b, r * QT:(r + 1) * QT]), "w_in": w_in_r[r], "cols": cols,
                  "fbias": np.ascontiguousarray(f(inp["fox_f_bias"])[0, 4 * r:4 * r + 4].reshape(4, 1))})
        maps.append(m)
    return maps


_NC_CACHE = {}


def kernel(**inputs):
    S = int(np.asarray(inputs["x"]).shape[1])
    QT = S // 4
    if S not in _NC_CACHE:
        _NC_CACHE[S] = build_nc(S)
    nc = _NC_CACHE[S]
    maps = make_in_maps(inputs, S)
    res = run_bass_kernel_spmd(nc, maps, core_ids=list(range(8)))
    outp = np.empty((2, S, D), np.float32)
    for c in range(8):
        b, r = c // 4, c % 4
        outp[b, r * QT:(r + 1) * QT] = res.results[c]["out"]
    return outp
```

```python
import math
from contextlib import ExitStack

import numpy as np
import ml_dtypes
import concourse.bass as bass
import concourse.mybir as mybir
from concourse.bass_utils import run_bass_kernel_spmd

F32 = mybir.dt.float32
BF16 = mybir.dt.bfloat16
AF = mybir.ActivationFunctionType
ALU = mybir.AluOpType

D = 4096
HD = 128
DFF = 16384
PLE = 256
EPS = 1e-6
NEG = -1.0e30
HG_OFFSET = 16


class Sched:
    ENGS = ("pe", "act", "dve", "pool", "sp")

    def __init__(self, nc, nsem=4, ndsem=8):
        self.nc = nc
        self.ops = []
        self.deps = []
        self.last_writer = {}
        self.readers = {}
        self.nsem = nsem
        self.ndsem = ndsem
        self.last_c = {}
        self.last_d = {e: [] for e in self.ENGS}
        self.excl = set()

    def add(self, eng, fn, reads=(), writes=(), kind="c", after=()):
        i = len(self.ops)
        d = set(after)
        if self.excl:
            ex = [s for s in reads if s in self.excl]
            if ex:
                reads = [s for s in reads if s not in self.excl]
                writes = list(writes) + ex
        for s in reads:
            w = self.last_writer.get(s)
            if w is not None:
                d.add(w)
        for s in writes:
            w = self.last_writer.get(s)
            if w is not None:
                d.add(w)
            d.update(self.readers.get(s, ()))
        for s in reads:
            self.readers.setdefault(s, []).append(i)
        for s in writes:
            self.last_writer[s] = i
            self.readers[s] = []
        d.discard(i)
        self.ops.append((eng, fn, kind))
        self.deps.append(d)
        if kind == "d":
            self.last_d[eng] = (self.last_d[eng] + [i])[-self.ndsem:]
        elif kind != "cc":
            self.last_c[eng] = i
        return i

    def dma(self, eng, out, in_, reads=(), writes=(), after=()):
        return self.add(eng, lambda e: e.dma_start(out=out, in_=in_), reads, writes, kind="d", after=after)

    def barrier(self):
        L = set(self.last_c.values())
        for e in self.ENGS:
            L.update(self.last_d[e])
        for e in self.ENGS:
            i = self.add(e, None, (), (), kind="n")
            self.deps[i] = set(x for x in L if x != i)
        self.last_writer = {}
        self.readers = {}

    def emit(self, stack):
        nc = self.nc
        ops, deps = self.ops, self.deps
        n = len(ops)
        has_dep = [False] * n
        for i in range(n):
            for d in deps[i]:
                if ops[d][0] == "pe" and ops[i][0] == "pe" and ops[d][2] == "c" and ops[i][2] == "c":
                    continue
                has_dep[d] = True
        csem = {e: [stack.enter_context(nc.semaphore(f"c_{e}_{k}")) for k in range(self.nsem)] for e in self.ENGS}
        dsem = {e: [stack.enter_context(nc.semaphore(f"d_{e}_{k}")) for k in range(self.ndsem)] for e in self.ENGS}
        sig = [None] * n
        ccount = {e: 0 for e in self.ENGS}
        dcount = {e: 0 for e in self.ENGS}
        throttle = [None] * n
        for i, (eng, fn, kind) in enumerate(ops):
            if kind == "d":
                j = dcount[eng]
                dcount[eng] += 1
                slot, rnd = j % self.ndsem, j // self.ndsem
                sig[i] = (dsem[eng][slot], 16 * (rnd + 1), "d", eng, j)
                if rnd > 0:
                    throttle[i] = (dsem[eng][slot], 16 * rnd)
            elif kind == "cc":
                ccsem = stack.enter_context(nc.semaphore(f"cc_{i}"))
                sig[i] = (ccsem, 1, "x", eng, 0)
            elif kind == "c" and has_dep[i]:
                k = ccount[eng]
                ccount[eng] += 1
                sig[i] = (csem[eng][k % self.nsem], k // self.nsem + 1, "c", eng, k)
        per_eng = {e: [] for e in self.ENGS}
        for i, (eng, fn, kind) in enumerate(ops):
            per_eng[eng].append(i)
        self.stats = {e: len(per_eng[e]) for e in self.ENGS}
        final_d = dict(dcount)
        cc_final = []

        def make_body(eng):
            idxs = per_eng[eng]

            def body(e):
                waited = {}
                known = {}

                def wait(sem, val):
                    key = id(sem)
                    if waited.get(key, 0) >= val:
                        return
                    e.wait_ge(sem, val)
                    waited[key] = val

                for i in idxs:
                    _, fn, kind = ops[i]
                    if throttle[i] is not None:
                        wait(*throttle[i])
                    cmax = {}
                    dmax = {}
                    for d in deps[i]:
                        s = sig[d]
                        if s is None:
                            continue
                        sem, val, skind, peng, k = s
                        if skind == "c":
                            if peng == "pe" and eng == "pe" and kind == "c":
                                continue
                            if k > cmax.get(peng, (-1,))[0]:
                                cmax[peng] = (k, sem, val)
                        else:
                            key = id(sem)
                            if val > dmax.get(key, (0,))[0]:
                                dmax[key] = (val, sem)
                    for peng, (k, sem, val) in cmax.items():
                        if known.get(peng, -1) >= k:
                            continue
                        wait(sem, val)
                        known[peng] = k
                    for key, (val, sem) in dmax.items():
                        wait(sem, val)
                    if fn is None:
                        continue
                    ins = fn(e)
                    s = sig[i]
                    if s is not None:
                        ins.then_inc(s[0], 16 if s[2] == "d" else 1)
                    if kind == "cc":
                        cc_final.append(s)
                if eng == "sp":
                    for q in self.ENGS:
                        cnt = final_d[q]
                        for slot in range(min(cnt, self.ndsem)):
                            uses = (cnt - slot + self.ndsem - 1) // self.ndsem
                            wait(dsem[q][slot], 16 * uses)

            return body

        block = stack.enter_context(nc.Block())
        block.tensor(make_body("pe"))
        block.scalar(make_body("act"))
        block.vector(make_body("dve"))
        block.gpsimd(make_body("pool"))
        block.sync(make_body("sp"))


def build_nc(S, dbg=()):
    QT = S // 4
    NBA = S // 512
    NBQ = QT // 512
    NKT = S // 128
    nc = bass.Bass("TRN2", target_bir_lowering=False)
    dt_in = lambda name, shape, dt=F32: nc.dram_tensor(name, list(shape), dt, kind="ExternalInput").ap()
    dt_sc = lambda name, shape, dt: nc.dram_tensor(name, list(shape), dt, kind="Internal").ap()

    xb = dt_in("xb", [S, D])
    xq = dt_in("xq", [QT, D])
    pq = dt_in("pq", [QT, PLE])
    w_in = dt_in("w_in", [D, 3588])
    w_out = dt_in("w_out", [D, D])
    w_up = dt_in("w_up", [D, DFF])
    w_down = dt_in("w_down", [DFF, D])
    w_gate = dt_in("w_gate", [D, D])
    w_pp = dt_in("w_pp", [PLE, D])
    g_rows = dt_in("g_rows", [4, D])
    cols = dt_in("cols", [128, 16])
    fbias = dt_in("fbias", [4, 1])
    ident_in = dt_in("ident", [128, 128], BF16)
    cmask_in = dt_in("cmask", [128, 4 * 512])
    hmask_in = dt_in("hmask", [128, 128])
    rmask_in = dt_in("rmask", [128, 512])
    sel4_in = dt_in("sel4", [4, 4 * 128])
    id4_in = dt_in("id4", [4, 4])
    out = nc.dram_tensor("out", [QT, D], F32, kind="ExternalOutput").ap()

    uT = dt_sc("uT", [S // 512, 128, 32, 512], BF16)
    qa_raw = dt_sc("qa_raw", [4, 128, S], F32)
    ka_raw = dt_sc("ka_raw", [4, 128, S], F32)
    qT = dt_sc("qT", [4, 128, S], BF16)
    kT = dt_sc("kT", [4, 128, S], BF16)
    va = dt_sc("va", [4, 128, S // 128, 128], BF16)
    fa_raw = dt_sc("fa_raw", [4, S], F32)
    qbT = dt_sc("qbT", [4, 128, S], F32)
    sgT = dt_sc("sgT", [4, 128, S], F32)
    gbT = dt_sc("gbT", [4, 128, S], F32)
    vb = dt_sc("vb", [4, 128, S // 128, 128], BF16)
    rs_a = dt_sc("rs_a", [4 * 2048, QT], BF16)
    rs_b = dt_sc("rs_b", [4 * 2048, QT], BF16)
    mT_a = dt_sc("mT_a", [2048, QT], BF16)
    mT_b = dt_sc("mT_b", [2048, QT], BF16)
    h1 = dt_sc("h1", [QT, D], F32)
    umT = dt_sc("umT", [QT // 512, 128, 32, 512], BF16)
    hidT = dt_sc("hidT", [QT // 512, 128, 128, 512], BF16)
    h2 = dt_sc("h2", [QT, D], F32)
    upT = dt_sc("upT", [QT // 512, 128, 32, 512], BF16)
    eS = dt_sc("eS", [QT, D], F32)
    dbg_out = {}
    dbg_src = {}
    for name in dbg:
        src = locals()[name]
        dbg_src[name] = src
        dbg_out[name] = nc.dram_tensor("dbg_" + name, list(src.shape), src.dtype, kind="ExternalOutput").ap()

    with ExitStack() as top:
        Sc = Sched(nc)
        add, dma = Sc.add, Sc.dma

        def mm(o, lhsT, rhs, start, stop, r, w):
            add("pe", lambda e: e.matmul(o, lhsT=lhsT, rhs=rhs, start=start, stop=stop), r, w)

        def tr(o, in_, ident, r, w):
            add("pe", lambda e: e.transpose(out=o, in_=in_, identity=ident), r, w)

        def act(o, in_, func, r, w, **kw):
            add("act", lambda e: e.activation(out=o, in_=in_, func=func, **kw), r, w)

        def dve(method, r, w, **kw):
            add("dve", lambda e: getattr(e, method)(**kw), r, w)

        uid = [0]

        def sbt(st, name, shape, dt):
            uid[0] += 1
            return st.enter_context(nc.sbuf_tensor(f"{name}_{uid[0]}", list(shape), dt))

        def pst(st, name, shape, dt):
            uid[0] += 1
            Sc.excl.add(name)
            return st.enter_context(nc.psum_tensor(f"{name}_{uid[0]}", list(shape), dt))
        IDN = sbt(top, "IDN", [128, 128], BF16)
        COLS = sbt(top, "COLS", [128, 16], F32)
        CD = sbt(top, "CD", [128, 24], F32)
        ONESF = sbt(top, "ONESF", [128, 128], F32)
        ONESB = sbt(top, "ONESB", [128, 128], BF16)
        zst = ExitStack()
        ZT = sbt(zst, "ZT", [128, 2048], BF16)
        dma("sp", IDN[:], ident_in, (), ["IDN"])
        dma("sp", COLS[:], cols, (), ["COLS"])
        add("dve", lambda e: e.memset(ONESF[:], 1.0 / 128), (), ["ONESF"])
        add("dve", lambda e: e.memset(ONESB[:], 1.0), (), ["ONESB"])
        add("dve", lambda e: e.memset(ZT[:], 0.0), (), ["ZT"])
        add("dve", lambda e: e.tensor_scalar_mul(out=CD[:, 0:1], in0=COLS[:, 0:1], scalar1=1.0 / math.sqrt(HD)), ["COLS"], ["CD0"])
        add("dve", lambda e: e.tensor_sub(out=CD[:, 13:17], in0=COLS[:, 3:7], in1=COLS[:, 7:11]), ["COLS"], ["CD13"])
        act(CD[:, 1:5], CD[:, 13:17], AF.Sigmoid, ["CD13"], ["CD1"])
        act(CD[:, 5:9], CD[:, 13:17], AF.Sigmoid, ["CD13"], ["CD5"], scale=-1.0)
        add("dve", lambda e: e.tensor_scalar_mul(out=CD[:, 9:13], in0=CD[:, 5:9], scalar1=-1.0), ["CD5"], ["CD9"])
        zw = min(QT, 2048)
        for rs_ in (rs_a, rs_b):
            for i in range(4 * 2048 // 128):
                for j in range(QT // zw):
                    dma("sp", rs_[i * 128:(i + 1) * 128, j * zw:(j + 1) * zw], ZT[:, 0:zw], ["ZT"], [])
        Sc.barrier()
        zst.close()

        def norm_transpose(src, dstT, M, grow, tag):
            with ExitStack() as st:
                X = [sbt(st, f"X{i}", [128, D], F32) for i in range(4)]
                GB = sbt(st, "GB", [128, D], F32)
                JK = sbt(st, "JK", [128, D], BF16)
                U = [sbt(st, f"U{i}", [128, D], BF16) for i in range(4)]
                UT = [sbt(st, f"UT{i}", [128, 32, 512], BF16) for i in range(2)]
                SSq = [sbt(st, f"SS{i}", [128, 4], F32) for i in range(4)]
                PB = [pst(st, f"PB{i}", [128, 1024], BF16) for i in range(4)]
                dma("sp", GB[:], g_rows[grow:grow + 1, :].partition_broadcast(128), (), ["GB"])
                nt = M // 128
                for t_ in range(min(3, nt)):
                    dma("sp", X[t_][:], src[t_ * 128:(t_ + 1) * 128, :], (), [f"X{t_}"])

                def stats(t):
                    i = t % 4
                    act(JK[:], X[i][:], AF.Square, [f"X{i}"], ["JK", f"SSa{i}"], accum_out=SSq[i][:, 0:1])
                    act(SSq[i][:, 1:2], SSq[i][:, 0:1], AF.Sqrt, [f"SSa{i}"], [f"SSb{i}"], scale=1.0 / D, bias=EPS)

                def recip(t):
                    i = t % 4
                    dve("reciprocal", [f"SSb{i}"], [f"SSc{i}"], out=SSq[i][:, 2:3], in_=SSq[i][:, 1:2])

                stats(0)
                recip(0)
                for t in range(nt):
                    i = t % 4
                    g, tt = t // 4, t % 4
                    gi = g % 2
                    if t + 3 < nt:
                        dma("sp", X[(t + 3) % 4][:], src[(t + 3) * 128:(t + 4) * 128, :], (), [f"X{(t + 3) % 4}"])
                    if t + 1 < nt:
                        stats(t + 1)
                    dve("scalar_tensor_tensor", [f"X{i}", f"SSc{i}", "GB"], [f"U{i}"], out=U[i][:], in0=X[i][:],
                        scalar=SSq[i][:, 2:3], in1=GB[:], op0=ALU.mult, op1=ALU.mult)
                    if t + 1 < nt:
                        recip(t + 1)
                    for q in range(4):
                        pb = (t * 4 + q) % 4
                        for j in range(8):
                            kc = q * 8 + j
                            tr(PB[pb][:, j * 128:(j + 1) * 128], U[i][:, kc * 128:(kc + 1) * 128], IDN[:],
                               [f"U{i}", "IDN"], [f"PB{pb}"])
                        o = UT[gi][:, q * 8:(q + 1) * 8, tt * 128:(tt + 1) * 128]
                        src_ps = PB[pb][:].rearrange("p (j m) -> p j m", j=8)
                        if q % 2 == 0:
                            add("act", lambda e, o=o, s_=src_ps: e.copy(out=o, in_=s_), [f"PB{pb}"], [f"UT{gi}"])
                        else:
                            add("dve", lambda e, o=o, s_=src_ps: e.tensor_copy(out=o, in_=s_), [f"PB{pb}"], [f"UT{gi}"])
                    if tt == 3:
                        dma("pool", dstT[g], UT[gi][:], [f"UT{gi}"], [])
            Sc.barrier()

        def gemm_pass(aT, M, w, panels, extra_alloc=None, KC=32, side=None, a_after=(), nps=6, side_from=0,
                      a_view=None, post_load=None):
            with ExitStack() as st:
                nblk = M // 512
                resident = nblk <= 4
                wmax = max(p_[1] for p_ in panels)
                WP = [sbt(st, f"WP{i}", [128, KC, (wmax + 7) // 8 * 8], BF16) for i in range(2)]
                AB = [sbt(st, f"AB{i}", [128, KC, 512], BF16) for i in range(nblk if resident else 3)]
                PS = [pst(st, f"PS{i}", [128, 512], F32) for i in range(nps)]
                ctx = extra_alloc(st) if extra_alloc else None
                sgen = side(st) if side else None
                psi = 0
                seq = [(pi, bi) for pi in range(len(panels)) for bi in range(nblk)]

                def load_w(pi):
                    c0, ncol, _, _ = panels[pi]
                    wi = pi % 2
                    ids = []
                    for kg in range(KC // 8):
                        ids.append(dma("pool", WP[wi][:, kg * 8:(kg + 1) * 8, 0:ncol],
                                       w[kg * 1024:(kg + 1) * 1024, c0:c0 + ncol].rearrange("(k p) n -> p k n", p=128),
                                       (), [f"WP{wi}_{kg}"]))
                    return ids

                def load_a(n):
                    pi, bi = seq[n]
                    ai = bi if resident else n % 3
                    src = a_view(bi) if a_view else aT[bi]
                    ids = []
                    for kg in range(KC // 8):
                        ids.append(dma("sp", AB[ai][:, kg * 8:(kg + 1) * 8, :], src[:, kg * 8:(kg + 1) * 8, :], (), [f"AB{ai}_{kg}"],
                                       after=a_after))
                    return ids

                def side_step(pi):
                    if sgen is not None and pi >= side_from:
                        next(sgen, None)

                init_ids = load_w(0)
                if resident:
                    for b_ in range(nblk):
                        init_ids += load_a(b_)
                else:
                    init_ids += load_a(0)
                    init_ids += load_a(1)
                if post_load:
                    post_load(init_ids)
                if hasattr(panels[0][3], "pre"):
                    panels[0][3].pre(ctx, 0, 0, 0)
                for n, (pi, bi) in enumerate(seq):
                    c0, ncol, orient, epi = panels[pi]
                    wi = pi % 2
                    ai = bi if resident else n % 3
                    if bi == 0 and pi + 1 < len(panels):
                        load_w(pi + 1)
                    if not resident and n + 2 < len(seq):
                        load_a(n + 2)
                    nsub = (ncol + 127) // 128 if orient == "F" else 4
                    for sub in range(nsub):
                        ps = PS[psi % nps]
                        pname = f"PS{psi % nps}"
                        psi += 1
                        if orient == "F":
                            cw = min(128, ncol - sub * 128)
                            for kc in range(KC):
                                mm(ps[0:cw, :], WP[wi][:, kc, sub * 128:sub * 128 + cw], AB[ai][:, kc, :], kc == 0, kc == KC - 1,
                                   [f"WP{wi}_{kc // 8}", f"AB{ai}_{kc // 8}"], [pname])
                        else:
                            for kc in range(KC):
                                mm(ps[:, 0:ncol], AB[ai][:, kc, sub * 128:(sub + 1) * 128], WP[wi][:, kc, 0:ncol], kc == 0, kc == KC - 1,
                                   [f"WP{wi}_{kc // 8}", f"AB{ai}_{kc // 8}"], [pname])
                        nxt = None
                        if sub + 1 < nsub:
                            nxt = (pi, sub + 1, bi)
                        elif n + 1 < len(seq):
                            nxt = (seq[n + 1][0], 0, seq[n + 1][1])
                        if nxt is not None and hasattr(panels[nxt[0]][3], "pre"):
                            panels[nxt[0]][3].pre(ctx, *nxt)
                        epi(ctx, ps, pname, pi, sub, bi)
                        side_step(pi)
                if sgen is not None:
                    for _ in sgen:
                        pass
            Sc.barrier()

        norm_transpose(xb, uT, S, 0, "A")

        def alloc_B(st):
            c = {}
            c["OF"] = [sbt(st, f"OF{i}", [128, 512], F32) for i in range(3)]
            c["OB"] = [sbt(st, f"OB{i}", [128, 512], BF16) for i in range(2)]
            c["n"] = 0
            return c

        def epi_F(dst, func, tag=None):
            def epi(c, ps, pname, pi, ch, bi):
                i = c["n"] % 3
                c["n"] += 1
                o = c["OF"][i]
                act(o[:], ps[:], func, [pname], [f"OF{i}"])
                dma("sp", dst[ch, :, bi * 512:(bi + 1) * 512], o[:], [f"OF{i}"], [f"{tag}_{ch}_{bi}"] if tag else [])
            return epi

        def epi_T16(dst):
            def epi(c, ps, pname, pi, tt, bi):
                i = c["n"] % 2
                c["n"] += 1
                o = c["OB"][i]
                dve("tensor_copy", [pname], [f"OB{i}"], out=o[:], in_=ps[:])
                kt = bi * 4 + tt
                dma("sp", dst[:, :, kt, :].rearrange("h p d -> p h d"), o[:].rearrange("p (h d) -> p h d", h=4), [f"OB{i}"], [])
            return epi

        def epi_fa(c, ps, pname, pi, ch, bi):
            i = c["n"] % 3
            c["n"] += 1
            o = c["OF"][i]
            act(o[0:4, :], ps[0:4, :], AF.Copy, [pname], [f"OF{i}"])
            dma("sp", fa_raw[:, bi * 512:(bi + 1) * 512], o[0:4, :], [f"OF{i}"], [])

        epi_qa = epi_F(qa_raw, AF.Copy, "qa")

        def epi_qa_fa(c, ps, pname, pi, ch, bi):
            (epi_fa if ch == 4 else epi_qa)(c, ps, pname, pi, ch, bi)

        panels_B = [
            (0, 516, "F", epi_qa_fa),
            (516, 512, "F", epi_F(ka_raw, AF.Copy, "ka")),
            (1028, 512, "T", epi_T16(va)),
            (2564, 512, "T", epi_T16(vb)),
            (1540, 512, "F", epi_F(qbT, AF.Silu)),
            (3076, 512, "F", epi_F(gbT, AF.Silu)),
            (2052, 512, "F", epi_F(sgT, AF.Sigmoid)),
        ]
        def c_side(st):
            RAW = [sbt(st, f"RAW{i}", [128, 512], F32) for i in range(3)]
            SQ = [sbt(st, f"SQ{i}", [128, 512], F32) for i in range(3)]
            SD = [sbt(st, f"SD{i}", [128, 512], F32) for i in range(3)]
            O16 = [sbt(st, f"O16{i}", [128, 512], BF16) for i in range(3)]
            PM = [pst(st, f"PM{i}", [128, 512], F32) for i in range(2)]
            items = [(src, dst, gcol, tag, h, bi) for (src, dst, gcol, tag) in
                     ((qa_raw, qT, CD[:, 0:1], "qa"), (ka_raw, kT, COLS[:, 1:2], "ka")) for h in range(4) for bi in range(NBA)]
            N = len(items)

            def sA(n):
                src, dst, gcol, tag, h, bi = items[n]
                i = n % 3
                dma("act", RAW[i][:], src[h, :, bi * 512:(bi + 1) * 512], [f"{tag}_{h}_{bi}"], [f"RAW{i}"])
                act(SQ[i][:], RAW[i][:], AF.Square, [f"RAW{i}"], [f"SQ{i}"])

            def sB(n):
                i, j = n % 3, n % 2
                mm(PM[j][:], ONESF[:], SQ[i][:], True, True, [f"SQ{i}"], [f"PM{j}"])

            def sC(n):
                src, dst, gcol, tag, h, bi = items[n]
                i, j = n % 3, n % 2
                act(SD[i][:], PM[j][:], AF.Sqrt, [f"PM{j}"], [f"SD{i}"], bias=EPS, scale=1.0)
                dve("reciprocal", [f"SD{i}"], [f"SD{i}"], out=SD[i][:], in_=SD[i][:])
                dve("scalar_tensor_tensor", [f"RAW{i}", f"SD{i}"], [f"O16{i}"], out=O16[i][:],
                    in0=RAW[i][:], scalar=gcol, in1=SD[i][:], op0=ALU.mult, op1=ALU.mult)
                dma("pool", dst[h, :, bi * 512:(bi + 1) * 512], O16[i][:], [f"O16{i}"], [])

            for k in range(N + 2):
                if k < N:
                    sA(k)
                if 0 <= k - 1 < N:
                    sB(k - 1)
                if 0 <= k - 2 < N:
                    sC(k - 2)
                yield

        gemm_pass(uT, S, w_in, panels_B, alloc_B, side=c_side, nps=5, side_from=2)

        with ExitStack() as st:
            HM = sbt(st, "HM", [128, 128], F32)
            RM = sbt(st, "RM", [128, 512], F32)
            per = lambda name, shape, dt: [sbt(st, f"{name}{i}", shape, dt) for i in range(4)]
            QB, SG, GG = per("QB", [128, 512], F32), per("SG", [128, 512], F32), per("GG", [128, 512], F32)
            VB = per("VB", [128, 4, 128], BF16)
            LOGF, KBt, Bt, EB, ENB, KE = (per(nm, [128, 512], F32) for nm in ("LOGF", "KBt", "Bt", "EB", "ENB", "KE"))
            QE16, KE16, KD16 = (per(nm, [128, 512], BF16) for nm in ("QE16", "KE16", "KD16"))
            KDT = per("KDT", [128, 4, 128], BF16)
            SQh, SDh, Yh = (per(nm, [128, 512], F32) for nm in ("SQh", "SDh", "Yh"))
            OSh = per("OSh", [128, 4, 512], BF16)
            ST = [[sbt(st, f"ST{h}_{j}", [128, 128], F32) for j in range(2)] for h in range(4)]
            S16 = [[sbt(st, f"S16{h}_{j}", [128, 128], BF16) for j in range(2)] for h in range(4)]
            ATM = per("ATM", [128, 128], BF16)
            PAT = pst(st, "PAT", [128, 512], F32)
            PUs = [pst(st, f"PU{i}", [128, 512], F32) for i in range(2)]
            PMS = pst(st, "PKM", [128, 512], F32)
            PKD = PMS[:].bitcast(BF16)
            POT = [pst(st, f"POT{i}", [128, 512], F32) for i in range(4)]
            dma("sp", HM[:], hmask_in, (), ["HM"])
            dma("sp", RM[:], rmask_in, (), ["RM"])
            for h in range(4):
                add("dve", lambda e, h=h: e.memset(ST[h][0][:], 0.0), (), [f"ST{h}_0"])
                add("dve", lambda e, h=h: e.memset(S16[h][0][:], 0.0), (), [f"S16{h}_0"])

            def group_gen(heads):
                cur = {h: 0 for h in heads}
                for bi in range(NBA):
                    sl = slice(bi * 512, (bi + 1) * 512)
                    for h in heads:
                        i = h
                        dma("sp", QB[i][:], qbT[h, :, sl], (), [f"QB{i}"])
                        dma("sp", SG[i][:], sgT[h, :, sl], (), [f"SG{i}"])
                        dma("sp", GG[i][:], gbT[h, :, sl], (), [f"GG{i}"])
                        dma("sp", VB[i][:], vb[h, :, bi * 4:(bi + 1) * 4, :], (), [f"VB{i}"])
                    yield
                    for h in heads:
                        i = h
                        act(LOGF[i][:], SG[i][:], AF.Ln, [f"SG{i}"], [f"LOGF{i}"], scale=CD[:, 5 + h:6 + h], bias=CD[:, 1 + h:2 + h])
                        act(KBt[i][:], SG[i][:], AF.Identity, [f"SG{i}"], [f"KBt{i}"], scale=CD[:, 9 + h:10 + h], bias=CD[:, 5 + h:6 + h])
                        yield
                        add("dve", lambda e, i=i: e.tensor_tensor_scan(out=Bt[i][:], data0=RM[:], data1=LOGF[i][:], initial=0.0,
                                                                       op0=ALU.mult, op1=ALU.add), ["RM", f"LOGF{i}"], [f"Bt{i}"])
                        act(EB[i][:], Bt[i][:], AF.Exp, [f"Bt{i}"], [f"EB{i}"])
                        act(ENB[i][:], Bt[i][:], AF.Exp, [f"Bt{i}"], [f"ENB{i}"], scale=-1.0)
                        yield
                        dve("tensor_tensor", [f"QB{i}", f"EB{i}"], [f"QE16{i}"], out=QE16[i][:], in0=QB[i][:], in1=EB[i][:], op=ALU.mult)
                        dve("tensor_tensor", [f"KBt{i}", f"ENB{i}"], [f"KE{i}"], out=KE[i][:], in0=KBt[i][:], in1=ENB[i][:], op=ALU.mult)
                        add("act", lambda e, i=i: e.copy(out=KE16[i][:], in_=KE[i][:]), [f"KE{i}"], [f"KE16{i}"])
                        yield
                        for c in range(8):
                            cs = slice(c * 64, (c + 1) * 64)
                            dve("tensor_scalar_mul", [f"KE{i}", f"EB{i}"], [f"KD16{i}_{c}"], out=KD16[i][:, cs], in0=KE[i][:, cs],
                                scalar1=EB[i][:, c * 64 + 63:c * 64 + 64])
                            if c == 3:
                                yield
                        yield
                        hp = h % 2
                        for tt in range(4):
                            tr(PKD[:, hp * 512 + tt * 128:hp * 512 + (tt + 1) * 128], KD16[i][:, tt * 128:(tt + 1) * 128], IDN[:],
                               [f"KD16{i}_{2 * tt}", f"KD16{i}_{2 * tt + 1}", "IDN"], ["PKM"])
                        add("act", lambda e, i=i, hp=hp: e.copy(out=KDT[i][:], in_=PKD[:, hp * 512:(hp + 1) * 512].rearrange("p (t d) -> p t d", t=4)),
                            ["PKM"], [f"KDT{i}"])
                        yield
                    for tt in range(4):
                        ts_ = slice(tt * 128, (tt + 1) * 128)
                        for h in heads:
                            i = h
                            pas = slice(h * 128, (h + 1) * 128)
                            mm(PAT[:, pas], KE16[i][:, ts_], QE16[i][:, ts_], True, True, [f"KE16{i}", f"QE16{i}"], ["PAT"])
                            dve("tensor_tensor", ["PAT", "HM"], [f"ATM{h}"], out=ATM[h][:], in0=PAT[:, pas], in1=HM[:], op=ALU.mult)
                        yield
                        for c in range(2):
                            tok = tt * 128 + c * 64
                            tk = slice(tok, tok + 64)
                            for h in heads:
                                i = h
                                pus = slice(h * 128, (h + 1) * 128)
                                cu = cur[h]
                                nx = 1 - cu
                                mm(POT[h][:, tk], VB[i][:, tt, :], ATM[h][:, c * 64:(c + 1) * 64], True, False,
                                   [f"VB{i}", f"ATM{h}"], [f"POT{h}"])
                                mm(POT[h][:, tk], S16[h][cu][:], QE16[i][:, tk], False, True, [f"S16{h}_{cu}", f"QE16{i}"], [f"POT{h}"])
                                mm(PUs[h % 2][:, pus], KDT[i][c * 64:(c + 1) * 64, tt, :], VB[i][c * 64:(c + 1) * 64, tt, :], True, True,
                                   [f"KDT{i}", f"VB{i}"], [f"PU{h % 2}"])
                                dve("scalar_tensor_tensor", [f"ST{h}_{cu}", f"EB{i}", f"PU{h % 2}"], [f"ST{h}_{nx}"], out=ST[h][nx][:],
                                    in0=ST[h][cu][:], scalar=EB[i][:, tok + 63:tok + 64], in1=PUs[h % 2][:, pus], op0=ALU.mult, op1=ALU.add)
                                add("act", lambda e, h=h, nx=nx: e.copy(out=S16[h][nx][:], in_=ST[h][nx][:]), [f"ST{h}_{nx}"], [f"S16{h}_{nx}"])
                                cur[h] = nx
                            yield
                    for h in heads:
                        i = h
                        act(SQh[i][:], POT[h][:], AF.Square, [f"POT{h}"], [f"SQh{i}"])
                        mm(PMS[:], ONESF[:], SQh[i][:], True, True, ["ONESF", f"SQh{i}"], ["PKM"])
                        act(SDh[i][:], PMS[:], AF.Sqrt, ["PKM"], [f"SDh{i}"], bias=EPS, scale=1.0)
                        yield
                        dve("reciprocal", [f"SDh{i}"], [f"SDh{i}"], out=SDh[i][:], in_=SDh[i][:])
                        dve("scalar_tensor_tensor", [f"POT{h}", f"SDh{i}"], [f"Yh{i}"], out=Yh[i][:], in0=POT[h][:],
                            scalar=COLS[:, 2:3], in1=SDh[i][:], op0=ALU.mult, op1=ALU.mult)
                        dve("tensor_tensor", [f"Yh{i}", f"GG{i}"], [f"Yh{i}"], out=Yh[i][:], in0=Yh[i][:], in1=GG[i][:], op=ALU.mult)
                        yield
                        for k in range(4):
                            act(OSh[i][:, k, :], Yh[i][:], AF.Identity, [f"Yh{i}"], [f"OSh{i}"], scale=COLS[:, 11 + k:12 + k])
                        quarter, off = (bi * 512) // QT, (bi * 512) % QT
                        for k in range(4):
                            r0 = quarter * 2048 + k * 512 + h * 128
                            dma("sp", rs_b[r0:r0 + 128, off:off + 512], OSh[i][:, k, :], [f"OSh{i}"], [])
                        yield

            gA, gB = group_gen((0, 1)), group_gen((2, 3))
            for _ in range(HG_OFFSET):
                next(gA, None)
            doneA = doneB = False
            while not (doneA and doneB):
                if not doneA:
                    try:
                        next(gA)
                    except StopIteration:
                        doneA = True
                if not doneB:
                    try:
                        next(gB)
                    except StopIteration:
                        doneB = True
        Sc.barrier()

        RG = [[0, 1, 2, 3], [4, 5, 6, 7]]

        with ExitStack() as st:
            FROW = sbt(st, "FROW", [4, S], F32)
            NEGF = sbt(st, "NEGF", [128, NKT * 4], F32)
            SEL4 = sbt(st, "SEL4", [4, 512], F32)
            PF = pst(st, "PF", [128, 512], F32)
            with ExitStack() as st2:
                FTMP = sbt(st2, "FTMP", [4, S], F32)
                ONE4 = sbt(st2, "ONE4", [4, S], F32)
                NB4 = sbt(st2, "NB4", [4, 2], F32)
                ID4 = sbt(st2, "ID4", [4, 4], F32)
                dma("sp", FROW[:], fa_raw, (), ["FROW"])
                dma("sp", NB4[:, 0:1], fbias, (), ["NB4"])
                dma("sp", ID4[:], id4_in, (), ["ID4"])
                add("dve", lambda e: e.memset(ONE4[:], 1.0), (), ["ONE4"])
                add("dve", lambda e: e.tensor_scalar_mul(out=NB4[:, 1:2], in0=NB4[:, 0:1], scalar1=-1.0), ["NB4"], ["NB4b"])
                act(FTMP[:], FROW[:], AF.Exp, ["FROW", "NB4b"], ["FTMP"], bias=NB4[:, 1:2], scale=-1.0)
                act(FTMP[:], FTMP[:], AF.Ln, ["FTMP"], ["FTMP"], bias=1.0, scale=1.0)
                add("dve", lambda e: e.tensor_tensor_scan(out=FROW[:], data0=ONE4[:], data1=FTMP[:], initial=0.0,
                                                          op0=ALU.mult, op1=ALU.subtract), ["ONE4", "FTMP"], ["FROW"])
                for kt0 in range(0, NKT, 128):
                    nk = min(NKT, kt0 + 128) - kt0
                    for kt in range(kt0, kt0 + nk):
                        mm(PF[:, (kt - kt0) * 4:(kt - kt0) * 4 + 4], FROW[:, kt * 128:(kt + 1) * 128], ID4[:], True, True,
                           ["FROW", "ID4"], ["PF"])
                    add("act", lambda e, kt0=kt0, nk=nk: e.mul(out=NEGF[:, kt0 * 4:(kt0 + nk) * 4], in_=PF[:, 0:nk * 4], mul=-1.0),
                        ["PF"], ["NEGF"])
            Sc.barrier()
            CM = sbt(st, "CM", [128, 4, 512], F32)
            KTs = [sbt(st, f"KT{i}", [128, S], BF16) for i in range(2)]
            QTs = [sbt(st, f"QT{i}", [128, S], BF16) for i in range(2)]
            VT = [sbt(st, f"VT{i}", [128, NKT, 128], BF16) for i in range(2)]
            FQ = [sbt(st, f"FQ{i}", [128, 5, 512], F32) for i in range(2)]
            TMP = [sbt(st, f"TMP{i}", [128, 512], F32) for i in range(4)]
            PT = [sbt(st, f"PT{i}", [128, 512], BF16) for i in range(4)]
            RL = sbt(st, "RL", [128, 512], F32)
            OA = [sbt(st, f"OA{i}", [128, 512], F32) for i in range(2)]
            OS = [sbt(st, f"OS{i}", [128, 4, 512], BF16) for i in range(2)]
            PA = [pst(st, f"PA{i}", [128, 512], F32) for i in range(4)]
            PO = [pst(st, f"PO{i}", [128, 512], F32) for i in range(2)]
            PL = [pst(st, f"PL{i}", [128, 512], F32) for i in range(1)]
            dma("sp", SEL4[:], sel4_in, (), ["SEL4"])
            dma("sp", CM[:], cmask_in.rearrange("p (j n) -> p j n", j=4), (), ["CM"])

            LA = 3
            blocks = [(h, qb) for h in range(4) for qb in range(NBA)]
            tiles = []
            for n_, (h, qb) in enumerate(blocks):
                nkt = 4 * (qb + 1)
                for kt in range(nkt):
                    tiles.append((n_, h, qb, kt, nkt))

            def load_head(h):
                hi = h % 2
                return [dma("sp", KTs[hi][:], kT[h], (), [f"KT{hi}"]),
                        dma("sp", QTs[hi][:], qT[h], (), [f"QT{hi}"]),
                        dma("sp", VT[hi][:], va[h], (), [f"VT{hi}"])]

            def prologue(n_):
                h, qb = blocks[n_]
                qi = n_ % 2
                qs = slice(qb * 512, (qb + 1) * 512)
                mm(PF[:], SEL4[:, h * 128:(h + 1) * 128], FROW[:, qs], True, True, ["SEL4", "FROW"], ["PF"])
                add("act", lambda e: e.copy(out=FQ[qi][:, 0, :], in_=PF[:]), ["PF"], [f"FQ{qi}"])
                for d in range(4):
                    add("pool", lambda e, d=d: e.tensor_add(out=FQ[qi][:, 1 + d, :], in0=CM[:, d, :], in1=FQ[qi][:, 0, :]),
                        [f"FQ{qi}", "CM"], [f"FQm{qi}_{d}"])

            def front(i):
                n_, h, qb, kt, nkt = tiles[i]
                hi, qi, a = h % 2, n_ % 2, i % 4
                d = kt - 4 * qb
                mm(PA[a][:], KTs[hi][:, kt * 128:(kt + 1) * 128], QTs[hi][:, qb * 512:(qb + 1) * 512], True, True,
                   [f"KT{hi}", f"QT{hi}"], [f"PA{a}"])
                if d >= 0:
                    fq, fslot = FQ[qi][:, 1 + d, :], f"FQm{qi}_{d}"
                else:
                    fq, fslot = FQ[qi][:, 0, :], f"FQ{qi}"
                dve("tensor_tensor", [f"PA{a}", fslot], [f"TMP{a}"], out=TMP[a][:], in0=PA[a][:], in1=fq, op=ALU.add)
                act(PT[a][:], TMP[a][:], AF.Exp, [f"TMP{a}", "NEGF"], [f"PT{a}"],
                    bias=NEGF[:, kt * 4 + h:kt * 4 + h + 1], scale=1.0)

            def back(i):
                n_, h, qb, kt, nkt = tiles[i]
                hi, qi, a = h % 2, n_ % 2, i % 4
                mm(PO[qi][:], VT[hi][:, kt, :], PT[a][:], kt == 0, kt == nkt - 1, [f"VT{hi}", f"PT{a}"], [f"PO{qi}"])
                mm(PL[0][:], ONESB[:], PT[a][:], kt == 0, kt == nkt - 1, ["ONESB", f"PT{a}"], ["PL0"])
                if kt == nkt - 1:
                    dve("reciprocal", ["PL0"], ["RL"], out=RL[:], in_=PL[0][:])
                    dve("tensor_tensor", [f"PO{qi}", "RL"], [f"OA{qi}"], out=OA[qi][:], in0=PO[qi][:], in1=RL[:], op=ALU.mult)
                    for k in range(4):
                        act(OS[qi][:, k, :], OA[qi][:], AF.Identity, [f"OA{qi}", "COLS"], [f"OS{qi}"], scale=COLS[:, 11 + k:12 + k])
                    quarter, off = (qb * 512) // QT, (qb * 512) % QT
                    for k in range(4):
                        r0 = quarter * 2048 + k * 512 + h * 128
                        dma("sp", rs_a[r0:r0 + 128, off:off + 512], OS[qi][:, k, :], [f"OS{qi}"], [])

            hl = load_head(0) + load_head(1)
            rsb = add("pool", lambda e: e.collective_compute("ReduceScatter", ALU.add, replica_groups=RG, ins=[rs_b], outs=[mT_b], dma_qos="P3"),
                      (), (), kind="cc", after=hl)
            prologue(0)
            for i in range(len(tiles) + LA):
                if i < len(tiles):
                    n_, h, qb, kt, nkt = tiles[i]
                    if kt == 0 and n_ + 1 < len(blocks):
                        prologue(n_ + 1)
                    front(i)
                if i - LA >= 0:
                    back(i - LA)
                    n_, h, qb, kt, nkt = tiles[i - LA]
                    if qb == NBA - 1 and kt == nkt - 1 and h + 2 < 4:
                        load_head(h + 2)
        Sc.barrier()

        def alloc_res(st):
            c = {"XR": [sbt(st, f"XR{i}", [128, 512], F32) for i in range(2)],
                 "ER": [sbt(st, f"ER{i}", [128, 512], F32) for i in range(2)],
                 "OF": [sbt(st, f"OF{i}", [128, 512], F32) for i in range(2)],
                 "OB": [sbt(st, f"OB{i}", [128, 512], BF16) for i in range(2)], "n": 0}
            return c

        def epi_res(prev, dst):
            def pre(c, pi, tt, bi):
                i = c.setdefault("pn", 0) % 2
                c["pn"] += 1
                r0, c0 = bi * 512 + tt * 128, pi * 512
                dma("sp", c["XR"][i][:], prev[r0:r0 + 128, c0:c0 + 512], (), [f"XR{i}"])

            def epi(c, ps, pname, pi, tt, bi):
                i = c["n"] % 2
                c["n"] += 1
                r0, c0 = bi * 512 + tt * 128, pi * 512
                dve("tensor_tensor", [pname, f"XR{i}"], [f"OF{i}"], out=c["OF"][i][:], in0=ps[:], in1=c["XR"][i][:], op=ALU.add)
                dma("sp", dst[r0:r0 + 128, c0:c0 + 512], c["OF"][i][:], [f"OF{i}"], [])
            epi.pre = pre
            return epi

        def ple_side(st):
            WPP = sbt(st, "WPP", [128, 2, D], BF16)
            GB = sbt(st, "GBp", [128, D], F32)
            PQ = [sbt(st, f"PQ{i}", [128, PLE], F32) for i in range(2)]
            P16 = [sbt(st, f"P16{i}", [128, PLE], BF16) for i in range(2)]
            PTt = [sbt(st, f"PTt{i}", [128, 2, 128], BF16) for i in range(2)]
            EPs = [sbt(st, f"EP{i}", [128, D], F32) for i in range(2)]
            JK = sbt(st, "JKp", [128, D], BF16)
            SSq = [sbt(st, f"SSp{i}", [128, 4], F32) for i in range(2)]
            PSs = [pst(st, f"PSp{i}", [128, 512], F32) for i in range(2)]
            PBp = pst(st, "PBp", [128, 1024], BF16)
            dma("pool", WPP[:], w_pp.rearrange("(k p) n -> p k n", p=128), (), ["WPP"])
            dma("sp", GB[:], g_rows[3:4, :].partition_broadcast(128), (), ["GBp"])
            yield
            for t in range(QT // 128):
                i = t % 2
                EP = EPs[i]
                EPn = f"EP{i}"
                dma("act", PQ[i][:], pq[t * 128:(t + 1) * 128, :], (), [f"PQ{i}"])
                dve("tensor_copy", [f"PQ{i}"], [f"P16{i}"], out=P16[i][:], in_=PQ[i][:])
                yield
                for kc in range(2):
                    tr(PBp[:, kc * 128:(kc + 1) * 128], P16[i][:, kc * 128:(kc + 1) * 128], IDN[:], [f"P16{i}"], ["PBp"])
                add("act", lambda e, i=i: e.copy(out=PTt[i][:], in_=PBp[:, 0:256].rearrange("p (k m) -> p k m", k=2)), ["PBp"], [f"PTt{i}"])
                yield
                for nb in range(8):
                    p_ = nb % 2
                    for kc in range(2):
                        mm(PSs[p_][:], PTt[i][:, kc, :], WPP[:, kc, nb * 512:(nb + 1) * 512], kc == 0, kc == 1,
                           [f"PTt{i}", "WPP"], [f"PSp{p_}"])
                    add("act", lambda e, nb=nb, p_=p_, EP=EP: e.copy(out=EP[:, nb * 512:(nb + 1) * 512], in_=PSs[p_][:]),
                        [f"PSp{p_}"], [EPn])
                    if nb % 2 == 1:
                        yield
                act(JK[:], EP[:], AF.Square, [EPn], ["JKp", f"SSa{i}"], accum_out=SSq[i][:, 0:1])
                act(SSq[i][:, 1:2], SSq[i][:, 0:1], AF.Sqrt, [f"SSa{i}"], [f"SSb{i}"], scale=1.0 / D, bias=EPS)
                dve("reciprocal", [f"SSb{i}"], [f"SSc{i}"], out=SSq[i][:, 2:3], in_=SSq[i][:, 1:2])
                dve("scalar_tensor_tensor", [EPn, f"SSc{i}", "GBp"], [EPn], out=EP[:], in0=EP[:],
                    scalar=SSq[i][:, 2:3], in1=GB[:], op0=ALU.mult, op1=ALU.mult)
                dma("pool", eS[t * 128:(t + 1) * 128, :], EP[:], [EPn], [])
                yield

        mTb3 = mT_b.rearrange("(k p) m -> k p m", p=128)
        mTa3 = mT_a.rearrange("(k p) m -> k p m", p=128)
        rs_ops = {}

        def issue_rsa(init_ids):
            rs_ops["a"] = add("pool", lambda e: e.collective_compute("ReduceScatter", ALU.add, replica_groups=RG, ins=[rs_a],
                                                                      outs=[mT_a], dma_qos="P3"), (), (), kind="cc", after=init_ids)

        gemm_pass(mTb3, QT, w_out[2048:4096, :],
                  [(pi * 512, 512, "T", epi_res(xq, h1)) for pi in range(8)], alloc_res, KC=16, a_after=[rsb],
                  a_view=lambda bi: mTb3[:, :, bi * 512:(bi + 1) * 512].rearrange("k p m -> p k m"), post_load=issue_rsa)
        gemm_pass(mTa3, QT, w_out[0:2048, :],
                  [(pi * 512, 512, "T", epi_res(h1, h1)) for pi in range(8)], alloc_res, KC=16, a_after=[rs_ops["a"]],
                  side=ple_side, nps=5,
                  a_view=lambda bi: mTa3[:, :, bi * 512:(bi + 1) * 512].rearrange("k p m -> p k m"))

        norm_transpose(h1, umT, QT, 1, "H")

        def epi_up(c, ps, pname, pi, ch, bi):
            i = c["n"] % 2
            c["n"] += 1
            act(c["OF"][i][:], ps[:], AF.Relu, [pname], [f"OF{i}"])
            dve("tensor_tensor", [f"OF{i}"], [f"OB{i}"], out=c["OB"][i][:], in0=c["OF"][i][:], in1=c["OF"][i][:], op=ALU.mult)
            dma("sp", hidT[bi, :, pi * 4 + ch, :], c["OB"][i][:], [f"OB{i}"], [])

        gemm_pass(umT, QT, w_up, [(pi * 512, 512, "F", epi_up) for pi in range(32)], alloc_res)
        for q in range(4):
            gemm_pass(hidT, QT, w_down[q * D:(q + 1) * D, :],
                      [(pi * 512, 512, "T", epi_res(h1 if q == 0 else h2, h2)) for pi in range(8)], alloc_res,
                      a_view=lambda bi, q=q: hidT[bi, :, q * 32:(q + 1) * 32, :])

        norm_transpose(h2, upT, QT, 2, "K")

        def epi_gate(c, ps, pname, pi, tt, bi):
            i = c["n"] % 2
            c["n"] += 1
            r0, c0 = bi * 512 + tt * 128, pi * 512
            dma("sp", c["XR"][i][:], h2[r0:r0 + 128, c0:c0 + 512], (), [f"XR{i}"])
            dma("sp", c["ER"][i][:], eS[r0:r0 + 128, c0:c0 + 512], (), [f"ER{i}"])
            act(c["OF"][i][:], ps[:], AF.Sigmoid, [pname], [f"OF{i}"])
            dve("tensor_tensor", [f"OF{i}", f"ER{i}"], [f"OF{i}"], out=c["OF"][i][:], in0=c["OF"][i][:], in1=c["ER"][i][:], op=ALU.mult)
            dve("tensor_tensor", [f"OF{i}", f"XR{i}"], [f"OF{i}"], out=c["OF"][i][:], in0=c["OF"][i][:], in1=c["XR"][i][:], op=ALU.add)
            dma("sp", out[r0:r0 + 128, c0:c0 + 512], c["OF"][i][:], [f"OF{i}"], [])

        gemm_pass(upT, QT, w_gate, [(pi * 512, 512, "T", epi_gate) for pi in range(8)], alloc_res)

        for name, dst in dbg_out.items():
            src = dbg_src[name]
            dma("sp", dst, src, (), [])
        Sc.emit(top)
        print("ops per engine:", Sc.stats, flush=True)
    return nc


def host_consts():
    ident = np.eye(128, dtype=np.float32).astype(ml_dtypes.bfloat16)
    p = np.arange(128)[:, None]
    j = np.arange(512)[None, :]
    cm = np.zeros((128, 4, 512), np.float32)
    for d in range(4):
        cm[:, d, :] = np.where(j >= 128 * d + p, 0.0, NEG)
    s = np.arange(128)[:, None]
    t = np.arange(128)[None, :]
    hm = ((s // 64 == t // 64) & (s <= t)).astype(np.float32)
    rm = np.ones((128, 512), np.float32)
    rm[:, ::64] = 0.0
    sel4 = np.zeros((4, 4, 128), np.float32)
    for h in range(4):
        sel4[h, h, :] = 1.0
    return {"ident": ident, "cmask": cm.reshape(128, 2048), "hmask": hm, "rmask": rm,
            "sel4": sel4.reshape(4, 512), "id4": np.eye(4, dtype=np.float32)}


def make_in_maps(inp, S):
    QT = S // 4
    f = lambda a: np.ascontiguousarray(np.asarray(a, dtype=np.float32))
    x, p = f(inp["x"]), f(inp["p"])
    w_in = f(inp["w_in"])[0]
    consts = host_consts()
    g_rows = np.stack([f(inp["norm_mix_g"])[0], f(inp["norm_mlp_g"])[0], f(inp["ple_norm_g"])[0], f(inp["ple_post_g"])[0]])
    lbl = f(inp["hgrn_lb_logits"])
    shared = {"w_out": f(inp["w_out"])[0], "w_up": f(inp["w_up"])[0], "w_down": f(inp["w_down"])[0],
              "w_gate": f(inp["w_ple_gate"])[0], "w_pp": f(inp["w_ple_proj"])[0], "g_rows": g_rows}
    shared.update(consts)
    w_in_r = []
    for r in range(4):
        sl = lambda o: w_in[:, o + r * 512:o + (r + 1) * 512]
        w_in_r.append(np.ascontiguousarray(np.concatenate(
            [sl(0), w_in[:, 6144 + 4 * r:6144 + 4 * r + 4], sl(2048), sl(4096), sl(6160), sl(8208), sl(10256), sl(12304)], axis=1)))
    maps = []
    for c in range(8):
        b, r = c // 4, c % 4
        cols = np.zeros((128, 16), np.float32)
        cols[:, 0] = f(inp["fox_q_norm_g"])[0]
        cols[:, 1] = f(inp["fox_k_norm_g"])[0]
        cols[:, 2] = f(inp["hgrn_norm_g"])[0]
        cols[:, 3:7] = lbl[0].reshape(16, 128)[4 * r:4 * r + 4].T
        cols[:, 7:11] = lbl[1].reshape(16, 128)[4 * r:4 * r + 4].T
        cols[:, 11 + r] = 1.0
        m = dict(shared)
        m.update({"xb": x[b], "xq": np.ascontiguousarray(x[b, r * QT:(r + 1) * QT]),
                  "pq": np.ascontiguousarray(p[0, b, r * QT:(r + 1) * QT]), "w_in": w_in_r[r], "cols": cols,
                  "fbias": np.ascontiguousarray(f(inp["fox_f_bias"])[0, 4 * r:4 * r + 4].reshape(4, 1))})
        maps.append(m)
    return maps


_NC_CACHE = {}


def kernel(**inputs):
    S = int(np.asarray(inputs["x"]).shape[1])
    QT = S // 4
    if S not in _NC_CACHE:
        _NC_CACHE[S] = build_nc(S)
    nc = _NC_CACHE[S]
    maps = make_in_maps(inputs, S)
    res = run_bass_kernel_spmd(nc, maps, core_ids=list(range(8)))
    outp = np.empty((2, S, D), np.float32)
    for c in range(8):
        b, r = c // 4, c % 4
        outp[b, r * QT:(r + 1) * QT] = res.results[c]["out"]
    return outp
```

```python
import math
from contextlib import ExitStack

import numpy as np
import ml_dtypes
import concourse.bass as bass
import concourse.mybir as mybir
from concourse.bass_utils import run_bass_kernel_spmd

F32 = mybir.dt.float32
BF16 = mybir.dt.bfloat16
AF = mybir.ActivationFunctionType
ALU = mybir.AluOpType

D = 4096
HD = 128
DFF = 16384
PLE = 256
EPS = 1e-6
NEG = -1.0e30
HG_OFFSET = 16


class Sched:
    ENGS = ("pe", "act", "dve", "pool", "sp")

    def __init__(self, nc, nsem=4, ndsem=8):
        self.nc = nc
        self.ops = []
        self.deps = []
        self.last_writer = {}
        self.readers = {}
        self.nsem = nsem
        self.ndsem = ndsem
        self.last_c = {}
        self.last_d = {e: [] for e in self.ENGS}
        self.excl = set()

    def add(self, eng, fn, reads=(), writes=(), kind="c", after=()):
        i = len(self.ops)
        d = set(after)
        if self.excl:
            ex = [s for s in reads if s in self.excl]
            if ex:
                reads = [s for s in reads if s not in self.excl]
                writes = list(writes) + ex
        for s in reads:
            w = self.last_writer.get(s)
            if w is not None:
                d.add(w)
        for s in writes:
            w = self.last_writer.get(s)
            if w is not None:
                d.add(w)
            d.update(self.readers.get(s, ()))
        for s in reads:
            self.readers.setdefault(s, []).append(i)
        for s in writes:
            self.last_writer[s] = i
            self.readers[s] = []
        d.discard(i)
        self.ops.append((eng, fn, kind))
        self.deps.append(d)
        if kind == "d":
            self.last_d[eng] = (self.last_d[eng] + [i])[-self.ndsem:]
        elif kind != "cc":
            self.last_c[eng] = i
        return i

    def dma(self, eng, out, in_, reads=(), writes=(), after=()):
        return self.add(eng, lambda e: e.dma_start(out=out, in_=in_), reads, writes, kind="d", after=after)

    def barrier(self):
        L = set(self.last_c.values())
        for e in self.ENGS:
            L.update(self.last_d[e])
        for e in self.ENGS:
            i = self.add(e, None, (), (), kind="n")
            self.deps[i] = set(x for x in L if x != i)
        self.last_writer = {}
        self.readers = {}

    def emit(self, stack):
        nc = self.nc
        ops, deps = self.ops, self.deps
        n = len(ops)
        has_dep = [False] * n
        for i in range(n):
            for d in deps[i]:
                if ops[d][0] == "pe" and ops[i][0] == "pe" and ops[d][2] == "c" and ops[i][2] == "c":
                    continue
                has_dep[d] = True
        csem = {e: [stack.enter_context(nc.semaphore(f"c_{e}_{k}")) for k in range(self.nsem)] for e in self.ENGS}
        dsem = {e: [stack.enter_context(nc.semaphore(f"d_{e}_{k}")) for k in range(self.ndsem)] for e in self.ENGS}
        sig = [None] * n
        ccount = {e: 0 for e in self.ENGS}
        dcount = {e: 0 for e in self.ENGS}
        throttle = [None] * n
        for i, (eng, fn, kind) in enumerate(ops):
            if kind == "d":
                j = dcount[eng]
                dcount[eng] += 1
                slot, rnd = j % self.ndsem, j // self.ndsem
                sig[i] = (dsem[eng][slot], 16 * (rnd + 1), "d", eng, j)
                if rnd > 0:
                    throttle[i] = (dsem[eng][slot], 16 * rnd)
            elif kind == "cc":
                ccsem = stack.enter_context(nc.semaphore(f"cc_{i}"))
                sig[i] = (ccsem, 1, "x", eng, 0)
            elif kind == "c" and has_dep[i]:
                k = ccount[eng]
                ccount[eng] += 1
                sig[i] = (csem[eng][k % self.nsem], k // self.nsem + 1, "c", eng, k)
        per_eng = {e: [] for e in self.ENGS}
        for i, (eng, fn, kind) in enumerate(ops):
            per_eng[eng].append(i)
        self.stats = {e: len(per_eng[e]) for e in self.ENGS}
        final_d = dict(dcount)
        cc_final = []

        def make_body(eng):
            idxs = per_eng[eng]

            def body(e):
                waited = {}
                known = {}

                def wait(sem, val):
                    key = id(sem)
                    if waited.get(key, 0) >= val:
                        return
                    e.wait_ge(sem, val)
                    waited[key] = val

                for i in idxs:
                    _, fn, kind = ops[i]
                    if throttle[i] is not None:
                        wait(*throttle[i])
                    cmax = {}
                    dmax = {}
                    for d in deps[i]:
                        s = sig[d]
                        if s is None:
                            continue
                        sem, val, skind, peng, k = s
                        if skind == "c":
                            if peng == "pe" and eng == "pe" and kind == "c":
                                continue
                            if k > cmax.get(peng, (-1,))[0]:
                                cmax[peng] = (k, sem, val)
                        else:
                            key = id(sem)
                            if val > dmax.get(key, (0,))[0]:
                                dmax[key] = (val, sem)
                    for peng, (k, sem, val) in cmax.items():
                        if known.get(peng, -1) >= k:
                            continue
                        wait(sem, val)
                        known[peng] = k
                    for key, (val, sem) in dmax.items():
                        wait(sem, val)
                    if fn is None:
                        continue
                    ins = fn(e)
                    s = sig[i]
                    if s is not None:
                        ins.then_inc(s[0], 16 if s[2] == "d" else 1)
                    if kind == "cc":
                        cc_final.append(s)
                if eng == "sp":
                    for q in self.ENGS:
                        cnt = final_d[q]
                        for slot in range(min(cnt, self.ndsem)):
                            uses = (cnt - slot + self.ndsem - 1) // self.ndsem
                            wait(dsem[q][slot], 16 * uses)

            return body

        block = stack.enter_context(nc.Block())
        block.tensor(make_body("pe"))
        block.scalar(make_body("act"))
        block.vector(make_body("dve"))
        block.gpsimd(make_body("pool"))
        block.sync(make_body("sp"))


def build_nc(S, dbg=()):
    QT = S // 4
    NBA = S // 512
    NBQ = QT // 512
    NKT = S // 128
    nc = bass.Bass("TRN2", target_bir_lowering=False)
    dt_in = lambda name, shape, dt=F32: nc.dram_tensor(name, list(shape), dt, kind="ExternalInput").ap()
    dt_sc = lambda name, shape, dt: nc.dram_tensor(name, list(shape), dt, kind="Internal").ap()

    xb = dt_in("xb", [S, D])
    xq = dt_in("xq", [QT, D])
    pq = dt_in("pq", [QT, PLE])
    w_in = dt_in("w_in", [D, 3588])
    w_out = dt_in("w_out", [D, D])
    w_up = dt_in("w_up", [D, DFF])
    w_down = dt_in("w_down", [DFF, D])
    w_gate = dt_in("w_gate", [D, D])
    w_pp = dt_in("w_pp", [PLE, D])
    g_rows = dt_in("g_rows", [4, D])
    cols = dt_in("cols", [128, 16])
    fbias = dt_in("fbias", [4, 1])
    ident_in = dt_in("ident", [128, 128], BF16)
    cmask_in = dt_in("cmask", [128, 4 * 512])
    hmask_in = dt_in("hmask", [128, 128])
    rmask_in = dt_in("rmask", [128, 512])
    sel4_in = dt_in("sel4", [4, 4 * 128])
    id4_in = dt_in("id4", [4, 4])
    out = nc.dram_tensor("out", [QT, D], F32, kind="ExternalOutput").ap()

    uT = dt_sc("uT", [S // 512, 128, 32, 512], BF16)
    qa_raw = dt_sc("qa_raw", [4, 128, S], F32)
    ka_raw = dt_sc("ka_raw", [4, 128, S], F32)
    qT = dt_sc("qT", [4, 128, S], BF16)
    kT = dt_sc("kT", [4, 128, S], BF16)
    va = dt_sc("va", [4, 128, S // 128, 128], BF16)
    fa_raw = dt_sc("fa_raw", [4, S], F32)
    qbT = dt_sc("qbT", [4, 128, S], F32)
    sgT = dt_sc("sgT", [4, 128, S], F32)
    gbT = dt_sc("gbT", [4, 128, S], F32)
    vb = dt_sc("vb", [4, 128, S // 128, 128], BF16)
    rs_a = dt_sc("rs_a", [4 * 2048, QT], BF16)
    rs_b = dt_sc("rs_b", [4 * 2048, QT], BF16)
    mT_a = dt_sc("mT_a", [2048, QT], BF16)
    mT_b = dt_sc("mT_b", [2048, QT], BF16)
    h1 = dt_sc("h1", [QT, D], F32)
    umT = dt_sc("umT", [QT // 512, 128, 32, 512], BF16)
    hidT = dt_sc("hidT", [QT // 512, 128, 128, 512], BF16)
    h2 = dt_sc("h2", [QT, D], F32)
    upT = dt_sc("upT", [QT // 512, 128, 32, 512], BF16)
    eS = dt_sc("eS", [QT, D], F32)
    dbg_out = {}
    dbg_src = {}
    for name in dbg:
        src = locals()[name]
        dbg_src[name] = src
        dbg_out[name] = nc.dram_tensor("dbg_" + name, list(src.shape), src.dtype, kind="ExternalOutput").ap()

    with ExitStack() as top:
        Sc = Sched(nc)
        add, dma = Sc.add, Sc.dma

        def mm(o, lhsT, rhs, start, stop, r, w):
            add("pe", lambda e: e.matmul(o, lhsT=lhsT, rhs=rhs, start=start, stop=stop), r, w)

        def tr(o, in_, ident, r, w):
            add("pe", lambda e: e.transpose(out=o, in_=in_, identity=ident), r, w)

        def act(o, in_, func, r, w, **kw):
            add("act", lambda e: e.activation(out=o, in_=in_, func=func, **kw), r, w)

        def dve(method, r, w, **kw):
            add("dve", lambda e: getattr(e, method)(**kw), r, w)

        uid = [0]

        def sbt(st, name, shape, dt):
            uid[0] += 1
            return st.enter_context(nc.sbuf_tensor(f"{name}_{uid[0]}", list(shape), dt))

        def pst(st, name, shape, dt):
            uid[0] += 1
            Sc.excl.add(name)
            return st.enter_context(nc.psum_tensor(f"{name}_{uid[0]}", list(shape), dt))
        IDN = sbt(top, "IDN", [128, 128], BF16)
        COLS = sbt(top, "COLS", [128, 16], F32)
        CD = sbt(top, "CD", [128, 24], F32)
        ONESF = sbt(top, "ONESF", [128, 128], F32)
        ONESB = sbt(top, "ONESB", [128, 128], BF16)
        zst = ExitStack()
        ZT = sbt(zst, "ZT", [128, 2048], BF16)
        dma("sp", IDN[:], ident_in, (), ["IDN"])
        dma("sp", COLS[:], cols, (), ["COLS"])
        add("dve", lambda e: e.memset(ONESF[:], 1.0 / 128), (), ["ONESF"])
        add("dve", lambda e: e.memset(ONESB[:], 1.0), (), ["ONESB"])
        add("dve", lambda e: e.memset(ZT[:], 0.0), (), ["ZT"])
        add("dve", lambda e: e.tensor_scalar_mul(out=CD[:, 0:1], in0=COLS[:, 0:1], scalar1=1.0 / math.sqrt(HD)), ["COLS"], ["CD0"])
        add("dve", lambda e: e.tensor_sub(out=CD[:, 13:17], in0=COLS[:, 3:7], in1=COLS[:, 7:11]), ["COLS"], ["CD13"])
        act(CD[:, 1:5], CD[:, 13:17], AF.Sigmoid, ["CD13"], ["CD1"])
        act(CD[:, 5:9], CD[:, 13:17], AF.Sigmoid, ["CD13"], ["CD5"], scale=-1.0)
        add("dve", lambda e: e.tensor_scalar_mul(out=CD[:, 9:13], in0=CD[:, 5:9], scalar1=-1.0), ["CD5"], ["CD9"])
        zw = min(QT, 2048)
        Sc.barrier()
        zst.close()

        def norm_transpose(src, dstT, M, grow, tag):
            with ExitStack() as st:
                X = [sbt(st, f"X{i}", [128, D], F32) for i in range(4)]
                GB = sbt(st, "GB", [128, D], F32)
                JK = sbt(st, "JK", [128, D], BF16)
                U = [sbt(st, f"U{i}", [128, D], BF16) for i in range(4)]
                UT = [sbt(st, f"UT{i}", [128, 32, 512], BF16) for i in range(2)]
                SSq = [sbt(st, f"SS{i}", [128, 4], F32) for i in range(4)]
                PB = [pst(st, f"PB{i}", [128, 1024], BF16) for i in range(4)]
                dma("sp", GB[:], g_rows[grow:grow + 1, :].partition_broadcast(128), (), ["GB"])
                nt = M // 128
                for t_ in range(min(3, nt)):
                    dma("sp", X[t_][:], src[t_ * 128:(t_ + 1) * 128, :], (), [f"X{t_}"])

                def stats(t):
                    i = t % 4
                    act(JK[:], X[i][:], AF.Square, [f"X{i}"], ["JK", f"SSa{i}"], accum_out=SSq[i][:, 0:1])
                    act(SSq[i][:, 1:2], SSq[i][:, 0:1], AF.Sqrt, [f"SSa{i}"], [f"SSb{i}"], scale=1.0 / D, bias=EPS)

                def recip(t):
                    i = t % 4
                    dve("reciprocal", [f"SSb{i}"], [f"SSc{i}"], out=SSq[i][:, 2:3], in_=SSq[i][:, 1:2])

                stats(0)
                recip(0)
                for t in range(nt):
                    i = t % 4
                    g, tt = t // 4, t % 4
                    gi = g % 2
                    if t + 3 < nt:
                        dma("sp", X[(t + 3) % 4][:], src[(t + 3) * 128:(t + 4) * 128, :], (), [f"X{(t + 3) % 4}"])
                    if t + 1 < nt:
                        stats(t + 1)
                    dve("scalar_tensor_tensor", [f"X{i}", f"SSc{i}", "GB"], [f"U{i}"], out=U[i][:], in0=X[i][:],
                        scalar=SSq[i][:, 2:3], in1=GB[:], op0=ALU.mult, op1=ALU.mult)
                    if t + 1 < nt:
                        recip(t + 1)
                    for q in range(4):
                        pb = (t * 4 + q) % 4
                        for j in range(8):
                            kc = q * 8 + j
                            tr(PB[pb][:, j * 128:(j + 1) * 128], U[i][:, kc * 128:(kc + 1) * 128], IDN[:],
                               [f"U{i}", "IDN"], [f"PB{pb}"])
                        o = UT[gi][:, q * 8:(q + 1) * 8, tt * 128:(tt + 1) * 128]
                        src_ps = PB[pb][:].rearrange("p (j m) -> p j m", j=8)
                        if q % 2 == 0:
                            add("act", lambda e, o=o, s_=src_ps: e.copy(out=o, in_=s_), [f"PB{pb}"], [f"UT{gi}"])
                        else:
                            add("dve", lambda e, o=o, s_=src_ps: e.tensor_copy(out=o, in_=s_), [f"PB{pb}"], [f"UT{gi}"])
                    if tt == 3:
                        dma("pool", dstT[g], UT[gi][:], [f"UT{gi}"], [])
            Sc.barrier()

        def gemm_pass(aT, M, w, panels, extra_alloc=None, KC=32, side=None, a_after=(), nps=6, side_from=0,
                      a_view=None, post_load=None):
            with ExitStack() as st:
                nblk = M // 512
                resident = nblk <= 4
                wmax = max(p_[1] for p_ in panels)
                WP = [sbt(st, f"WP{i}", [128, KC, (wmax + 7) // 8 * 8], BF16) for i in range(2)]
                AB = [sbt(st, f"AB{i}", [128, KC, 512], BF16) for i in range(nblk if resident else 3)]
                PS = [pst(st, f"PS{i}", [128, 512], F32) for i in range(nps)]
                ctx = extra_alloc(st) if extra_alloc else None
                sgen = side(st) if side else None
                psi = 0
                seq = [(pi, bi) for pi in range(len(panels)) for bi in range(nblk)]

                def load_w(pi):
                    c0, ncol, _, _ = panels[pi]
                    wi = pi % 2
                    ids = []
                    for kg in range(KC // 8):
                        ids.append(dma("pool", WP[wi][:, kg * 8:(kg + 1) * 8, 0:ncol],
                                       w[kg * 1024:(kg + 1) * 1024, c0:c0 + ncol].rearrange("(k p) n -> p k n", p=128),
                                       (), [f"WP{wi}_{kg}"]))
                    return ids

                def load_a(n):
                    pi, bi = seq[n]
                    ai = bi if resident else n % 3
                    src = a_view(bi) if a_view else aT[bi]
                    ids = []
                    for kg in range(KC // 8):
                        ids.append(dma("sp", AB[ai][:, kg * 8:(kg + 1) * 8, :], src[:, kg * 8:(kg + 1) * 8, :], (), [f"AB{ai}_{kg}"],
                                       after=a_after))
                    return ids

                def side_step(pi):
                    if sgen is not None and pi >= side_from:
                        next(sgen, None)

                init_ids = load_w(0)
                if resident:
                    for b_ in range(nblk):
                        init_ids += load_a(b_)
                else:
                    init_ids += load_a(0)
                    init_ids += load_a(1)
                if post_load:
                    post_load(init_ids)
                if hasattr(panels[0][3], "pre"):
                    panels[0][3].pre(ctx, 0, 0, 0)
                for n, (pi, bi) in enumerate(seq):
                    c0, ncol, orient, epi = panels[pi]
                    wi = pi % 2
                    ai = bi if resident else n % 3
                    if bi == 0 and pi + 1 < len(panels):
                        load_w(pi + 1)
                    if not resident and n + 2 < len(seq):
                        load_a(n + 2)
                    nsub = (ncol + 127) // 128 if orient == "F" else 4
                    for sub in range(nsub):
                        ps = PS[psi % nps]
                        pname = f"PS{psi % nps}"
                        psi += 1
                        if orient == "F":
                            cw = min(128, ncol - sub * 128)
                            for kc in range(KC):
                                mm(ps[0:cw, :], WP[wi][:, kc, sub * 128:sub * 128 + cw], AB[ai][:, kc, :], kc == 0, kc == KC - 1,
                                   [f"WP{wi}_{kc // 8}", f"AB{ai}_{kc // 8}"], [pname])
                        else:
                            for kc in range(KC):
                                mm(ps[:, 0:ncol], AB[ai][:, kc, sub * 128:(sub + 1) * 128], WP[wi][:, kc, 0:ncol], kc == 0, kc == KC - 1,
                                   [f"WP{wi}_{kc // 8}", f"AB{ai}_{kc // 8}"], [pname])
                        nxt = None
                        if sub + 1 < nsub:
                            nxt = (pi, sub + 1, bi)
                        elif n + 1 < len(seq):
                            nxt = (seq[n + 1][0], 0, seq[n + 1][1])
                        if nxt is not None and hasattr(panels[nxt[0]][3], "pre"):
                            panels[nxt[0]][3].pre(ctx, *nxt)
                        epi(ctx, ps, pname, pi, sub, bi)
                        side_step(pi)
                if sgen is not None:
                    for _ in sgen:
                        pass
            Sc.barrier()

        norm_transpose(xb, uT, S, 0, "A")

        def alloc_B(st):
            c = {}
            c["OF"] = [sbt(st, f"OF{i}", [128, 512], F32) for i in range(3)]
            c["OB"] = [sbt(st, f"OB{i}", [128, 512], BF16) for i in range(2)]
            c["n"] = 0
            return c

        def epi_F(dst, func, tag=None):
            def epi(c, ps, pname, pi, ch, bi):
                i = c["n"] % 3
                c["n"] += 1
                o = c["OF"][i]
                act(o[:], ps[:], func, [pname], [f"OF{i}"])
                dma("sp", dst[ch, :, bi * 512:(bi + 1) * 512], o[:], [f"OF{i}"], [f"{tag}_{ch}_{bi}"] if tag else [])
            return epi

        def epi_T16(dst):
            def epi(c, ps, pname, pi, tt, bi):
                i = c["n"] % 2
                c["n"] += 1
                o = c["OB"][i]
                dve("tensor_copy", [pname], [f"OB{i}"], out=o[:], in_=ps[:])
                kt = bi * 4 + tt
                dma("sp", dst[:, :, kt, :].rearrange("h p d -> p h d"), o[:].rearrange("p (h d) -> p h d", h=4), [f"OB{i}"], [])
            return epi

        def epi_fa(c, ps, pname, pi, ch, bi):
            i = c["n"] % 3
            c["n"] += 1
            o = c["OF"][i]
            act(o[0:4, :], ps[0:4, :], AF.Copy, [pname], [f"OF{i}"])
            dma("sp", fa_raw[:, bi * 512:(bi + 1) * 512], o[0:4, :], [f"OF{i}"], [])

        epi_qa = epi_F(qa_raw, AF.Copy, "qa")

        def epi_qa_fa(c, ps, pname, pi, ch, bi):
            (epi_fa if ch == 4 else epi_qa)(c, ps, pname, pi, ch, bi)

        panels_B = [
            (0, 516, "F", epi_qa_fa),
            (516, 512, "F", epi_F(ka_raw, AF.Copy, "ka")),
            (1028, 512, "T", epi_T16(va)),
            (2564, 512, "T", epi_T16(vb)),
            (1540, 512, "F", epi_F(qbT, AF.Silu)),
            (3076, 512, "F", epi_F(gbT, AF.Silu)),
            (2052, 512, "F", epi_F(sgT, AF.Sigmoid)),
        ]
        def c_side(st):
            RAW = [sbt(st, f"RAW{i}", [128, 512], F32) for i in range(3)]
            SQ = [sbt(st, f"SQ{i}", [128, 512], F32) for i in range(3)]
            SD = [sbt(st, f"SD{i}", [128, 512], F32) for i in range(3)]
            O16 = [sbt(st, f"O16{i}", [128, 512], BF16) for i in range(3)]
            PM = [pst(st, f"PM{i}", [128, 512], F32) for i in range(2)]
            items = [(src, dst, gcol, tag, h, bi) for (src, dst, gcol, tag) in
                     ((qa_raw, qT, CD[:, 0:1], "qa"), (ka_raw, kT, COLS[:, 1:2], "ka")) for h in range(4) for bi in range(NBA)]
            N = len(items)

            def sA(n):
                src, dst, gcol, tag, h, bi = items[n]
                i = n % 3
                dma("act", RAW[i][:], src[h, :, bi * 512:(bi + 1) * 512], [f"{tag}_{h}_{bi}"], [f"RAW{i}"])
                act(SQ[i][:], RAW[i][:], AF.Square, [f"RAW{i}"], [f"SQ{i}"])

            def sB(n):
                i, j = n % 3, n % 2
                mm(PM[j][:], ONESF[:], SQ[i][:], True, True, [f"SQ{i}"], [f"PM{j}"])

            def sC(n):
                src, dst, gcol, tag, h, bi = items[n]
                i, j = n % 3, n % 2
                act(SD[i][:], PM[j][:], AF.Sqrt, [f"PM{j}"], [f"SD{i}"], bias=EPS, scale=1.0)
                dve("reciprocal", [f"SD{i}"], [f"SD{i}"], out=SD[i][:], in_=SD[i][:])
                dve("scalar_tensor_tensor", [f"RAW{i}", f"SD{i}"], [f"O16{i}"], out=O16[i][:],
                    in0=RAW[i][:], scalar=gcol, in1=SD[i][:], op0=ALU.mult, op1=ALU.mult)
                dma("pool", dst[h, :, bi * 512:(bi + 1) * 512], O16[i][:], [f"O16{i}"], [])

            for k in range(N + 2):
                if k < N:
                    sA(k)
                if 0 <= k - 1 < N:
                    sB(k - 1)
                if 0 <= k - 2 < N:
                    sC(k - 2)
                yield

        gemm_pass(uT, S, w_in, panels_B, alloc_B, side=c_side, nps=5, side_from=2)

        with ExitStack() as st:
            HM = sbt(st, "HM", [128, 128], F32)
            RM = sbt(st, "RM", [128, 512], F32)
            per = lambda name, shape, dt: [sbt(st, f"{name}{i}", shape, dt) for i in range(4)]
            QB, SG, GG = per("QB", [128, 512], F32), per("SG", [128, 512], F32), per("GG", [128, 512], F32)
            VB = per("VB", [128, 4, 128], BF16)
            LOGF, KBt, Bt, EB, ENB, KE = (per(nm, [128, 512], F32) for nm in ("LOGF", "KBt", "Bt", "EB", "ENB", "KE"))
            QE16, KE16, KD16 = (per(nm, [128, 512], BF16) for nm in ("QE16", "KE16", "KD16"))
            KDT = per("KDT", [128, 4, 128], BF16)
            SQh, SDh, Yh = (per(nm, [128, 512], F32) for nm in ("SQh", "SDh", "Yh"))
            OSh = per("OSh", [128, 4, 512], BF16)
            ST = [[sbt(st, f"ST{h}_{j}", [128, 128], F32) for j in range(2)] for h in range(4)]
            S16 = [[sbt(st, f"S16{h}_{j}", [128, 128], BF16) for j in range(2)] for h in range(4)]
            ATM = per("ATM", [128, 128], BF16)
            PAT = pst(st, "PAT", [128, 512], F32)
            PUs = [pst(st, f"PU{i}", [128, 512], F32) for i in range(2)]
            PMS = pst(st, "PKM", [128, 512], F32)
            PKD = PMS[:].bitcast(BF16)
            POT = [pst(st, f"POT{i}", [128, 512], F32) for i in range(4)]
            dma("sp", HM[:], hmask_in, (), ["HM"])
            dma("sp", RM[:], rmask_in, (), ["RM"])
            for h in range(4):
                add("dve", lambda e, h=h: e.memset(ST[h][0][:], 0.0), (), [f"ST{h}_0"])
                add("dve", lambda e, h=h: e.memset(S16[h][0][:], 0.0), (), [f"S16{h}_0"])

            def group_gen(heads):
                cur = {h: 0 for h in heads}
                for bi in range(NBA):
                    sl = slice(bi * 512, (bi + 1) * 512)
                    for h in heads:
                        i = h
                        dma("sp", QB[i][:], qbT[h, :, sl], (), [f"QB{i}"])
                        dma("sp", SG[i][:], sgT[h, :, sl], (), [f"SG{i}"])
                        dma("sp", GG[i][:], gbT[h, :, sl], (), [f"GG{i}"])
                        dma("sp", VB[i][:], vb[h, :, bi * 4:(bi + 1) * 4, :], (), [f"VB{i}"])
                    yield
                    for h in heads:
                        i = h
                        act(LOGF[i][:], SG[i][:], AF.Ln, [f"SG{i}"], [f"LOGF{i}"], scale=CD[:, 5 + h:6 + h], bias=CD[:, 1 + h:2 + h])
                        act(KBt[i][:], SG[i][:], AF.Identity, [f"SG{i}"], [f"KBt{i}"], scale=CD[:, 9 + h:10 + h], bias=CD[:, 5 + h:6 + h])
                        yield
                        add("dve", lambda e, i=i: e.tensor_tensor_scan(out=Bt[i][:], data0=RM[:], data1=LOGF[i][:], initial=0.0,
                                                                       op0=ALU.mult, op1=ALU.add), ["RM", f"LOGF{i}"], [f"Bt{i}"])
                        act(EB[i][:], Bt[i][:], AF.Exp, [f"Bt{i}"], [f"EB{i}"])
                        act(ENB[i][:], Bt[i][:], AF.Exp, [f"Bt{i}"], [f"ENB{i}"], scale=-1.0)
                        yield
                        dve("tensor_tensor", [f"QB{i}", f"EB{i}"], [f"QE16{i}"], out=QE16[i][:], in0=QB[i][:], in1=EB[i][:], op=ALU.mult)
                        dve("tensor_tensor", [f"KBt{i}", f"ENB{i}"], [f"KE{i}"], out=KE[i][:], in0=KBt[i][:], in1=ENB[i][:], op=ALU.mult)
                        add("act", lambda e, i=i: e.copy(out=KE16[i][:], in_=KE[i][:]), [f"KE{i}"], [f"KE16{i}"])
                        yield
                        for c in range(8):
                            cs = slice(c * 64, (c + 1) * 64)
                            dve("tensor_scalar_mul", [f"KE{i}", f"EB{i}"], [f"KD16{i}_{c}"], out=KD16[i][:, cs], in0=KE[i][:, cs],
                                scalar1=EB[i][:, c * 64 + 63:c * 64 + 64])
                            if c == 3:
                                yield
                        yield
                        hp = h % 2
                        for tt in range(4):
                            tr(PKD[:, hp * 512 + tt * 128:hp * 512 + (tt + 1) * 128], KD16[i][:, tt * 128:(tt + 1) * 128], IDN[:],
                               [f"KD16{i}_{2 * tt}", f"KD16{i}_{2 * tt + 1}", "IDN"], ["PKM"])
                        add("act", lambda e, i=i, hp=hp: e.copy(out=KDT[i][:], in_=PKD[:, hp * 512:(hp + 1) * 512].rearrange("p (t d) -> p t d", t=4)),
                            ["PKM"], [f"KDT{i}"])
                        yield
                    for tt in range(4):
                        ts_ = slice(tt * 128, (tt + 1) * 128)
                        for h in heads:
                            i = h
                            pas = slice(h * 128, (h + 1) * 128)
                            mm(PAT[:, pas], KE16[i][:, ts_], QE16[i][:, ts_], True, True, [f"KE16{i}", f"QE16{i}"], ["PAT"])
                            dve("tensor_tensor", ["PAT", "HM"], [f"ATM{h}"], out=ATM[h][:], in0=PAT[:, pas], in1=HM[:], op=ALU.mult)
                        yield
                        for c in range(2):
                            tok = tt * 128 + c * 64
                            tk = slice(tok, tok + 64)
                            for h in heads:
                                i = h
                                pus = slice(h * 128, (h + 1) * 128)
                                cu = cur[h]
                                nx = 1 - cu
                                mm(POT[h][:, tk], VB[i][:, tt, :], ATM[h][:, c * 64:(c + 1) * 64], True, False,
                                   [f"VB{i}", f"ATM{h}"], [f"POT{h}"])
                                mm(POT[h][:, tk], S16[h][cu][:], QE16[i][:, tk], False, True, [f"S16{h}_{cu}", f"QE16{i}"], [f"POT{h}"])
                                mm(PUs[h % 2][:, pus], KDT[i][c * 64:(c + 1) * 64, tt, :], VB[i][c * 64:(c + 1) * 64, tt, :], True, True,
                                   [f"KDT{i}", f"VB{i}"], [f"PU{h % 2}"])
                                dve("scalar_tensor_tensor", [f"ST{h}_{cu}", f"EB{i}", f"PU{h % 2}"], [f"ST{h}_{nx}"], out=ST[h][nx][:],
                                    in0=ST[h][cu][:], scalar=EB[i][:, tok + 63:tok + 64], in1=PUs[h % 2][:, pus], op0=ALU.mult, op1=ALU.add)
                                add("act", lambda e, h=h, nx=nx: e.copy(out=S16[h][nx][:], in_=ST[h][nx][:]), [f"ST{h}_{nx}"], [f"S16{h}_{nx}"])
                                cur[h] = nx
                            yield
                    for h in heads:
                        i = h
                        act(SQh[i][:], POT[h][:], AF.Square, [f"POT{h}"], [f"SQh{i}"])
                        mm(PMS[:], ONESF[:], SQh[i][:], True, True, ["ONESF", f"SQh{i}"], ["PKM"])
                        act(SDh[i][:], PMS[:], AF.Sqrt, ["PKM"], [f"SDh{i}"], bias=EPS, scale=1.0)
                        yield
                        dve("reciprocal", [f"SDh{i}"], [f"SDh{i}"], out=SDh[i][:], in_=SDh[i][:])
                        dve("scalar_tensor_tensor", [f"POT{h}", f"SDh{i}"], [f"Yh{i}"], out=Yh[i][:], in0=POT[h][:],
                            scalar=COLS[:, 2:3], in1=SDh[i][:], op0=ALU.mult, op1=ALU.mult)
                        dve("tensor_tensor", [f"Yh{i}", f"GG{i}"], [f"Yh{i}"], out=Yh[i][:], in0=Yh[i][:], in1=GG[i][:], op=ALU.mult)
                        yield
                        for k in range(4):
                            act(OSh[i][:, k, :], Yh[i][:], AF.Identity, [f"Yh{i}"], [f"OSh{i}"], scale=COLS[:, 11 + k:12 + k])
                        quarter, off = (bi * 512) // QT, (bi * 512) % QT
                        for k in range(4):
                            r0 = quarter * 2048 + k * 512 + h * 128
                            dma("sp", rs_b[r0:r0 + 128, off:off + 512], OSh[i][:, k, :], [f"OSh{i}"], [])
                        yield

            gA, gB = group_gen((0, 1)), group_gen((2, 3))
            for _ in range(HG_OFFSET):
                next(gA, None)
            doneA = doneB = False
            while not (doneA and doneB):
                if not doneA:
                    try:
                        next(gA)
                    except StopIteration:
                        doneA = True
                if not doneB:
                    try:
                        next(gB)
                    except StopIteration:
                        doneB = True
        Sc.barrier()

        RG = [[0, 1, 2, 3], [4, 5, 6, 7]]

        with ExitStack() as st:
            FROW = sbt(st, "FROW", [4, S], F32)
            NEGF = sbt(st, "NEGF", [128, NKT * 4], F32)
            SEL4 = sbt(st, "SEL4", [4, 512], F32)
            PF = pst(st, "PF", [128, 512], F32)
            with ExitStack() as st2:
                FTMP = sbt(st2, "FTMP", [4, S], F32)
                ONE4 = sbt(st2, "ONE4", [4, S], F32)
                NB4 = sbt(st2, "NB4", [4, 2], F32)
                ID4 = sbt(st2, "ID4", [4, 4], F32)
                dma("sp", FROW[:], fa_raw, (), ["FROW"])
                dma("sp", NB4[:, 0:1], fbias, (), ["NB4"])
                dma("sp", ID4[:], id4_in, (), ["ID4"])
                add("dve", lambda e: e.memset(ONE4[:], 1.0), (), ["ONE4"])
                add("dve", lambda e: e.tensor_scalar_mul(out=NB4[:, 1:2], in0=NB4[:, 0:1], scalar1=-1.0), ["NB4"], ["NB4b"])
                act(FTMP[:], FROW[:], AF.Exp, ["FROW", "NB4b"], ["FTMP"], bias=NB4[:, 1:2], scale=-1.0)
                act(FTMP[:], FTMP[:], AF.Ln, ["FTMP"], ["FTMP"], bias=1.0, scale=1.0)
                add("dve", lambda e: e.tensor_tensor_scan(out=FROW[:], data0=ONE4[:], data1=FTMP[:], initial=0.0,
                                                          op0=ALU.mult, op1=ALU.subtract), ["ONE4", "FTMP"], ["FROW"])
                for kt0 in range(0, NKT, 128):
                    nk = min(NKT, kt0 + 128) - kt0
                    for kt in range(kt0, kt0 + nk):
                        mm(PF[:, (kt - kt0) * 4:(kt - kt0) * 4 + 4], FROW[:, kt * 128:(kt + 1) * 128], ID4[:], True, True,
                           ["FROW", "ID4"], ["PF"])
                    add("act", lambda e, kt0=kt0, nk=nk: e.mul(out=NEGF[:, kt0 * 4:(kt0 + nk) * 4], in_=PF[:, 0:nk * 4], mul=-1.0),
                        ["PF"], ["NEGF"])
            Sc.barrier()
            CM = sbt(st, "CM", [128, 4, 512], F32)
            KTs = [sbt(st, f"KT{i}", [128, S], BF16) for i in range(2)]
            QTs = [sbt(st, f"QT{i}", [128, S], BF16) for i in range(2)]
            VT = [sbt(st, f"VT{i}", [128, NKT, 128], BF16) for i in range(2)]
            FQ = [sbt(st, f"FQ{i}", [128, 5, 512], F32) for i in range(2)]
            TMP = [sbt(st, f"TMP{i}", [128, 512], F32) for i in range(4)]
            PT = [sbt(st, f"PT{i}", [128, 512], BF16) for i in range(4)]
            RL = sbt(st, "RL", [128, 512], F32)
            OA = [sbt(st, f"OA{i}", [128, 512], F32) for i in range(2)]
            OS = [sbt(st, f"OS{i}", [128, 4, 512], BF16) for i in range(2)]
            PA = [pst(st, f"PA{i}", [128, 512], F32) for i in range(4)]
            PO = [pst(st, f"PO{i}", [128, 512], F32) for i in range(2)]
            PL = [pst(st, f"PL{i}", [128, 512], F32) for i in range(1)]
            dma("sp", SEL4[:], sel4_in, (), ["SEL4"])
            dma("sp", CM[:], cmask_in.rearrange("p (j n) -> p j n", j=4), (), ["CM"])

            LA = 3
            blocks = [(h, qb) for h in range(4) for qb in range(NBA)]
            tiles = []
            for n_, (h, qb) in enumerate(blocks):
                nkt = 4 * (qb + 1)
                for kt in range(nkt):
                    tiles.append((n_, h, qb, kt, nkt))

            def load_head(h):
                hi = h % 2
                return [dma("sp", KTs[hi][:], kT[h], (), [f"KT{hi}"]),
                        dma("sp", QTs[hi][:], qT[h], (), [f"QT{hi}"]),
                        dma("sp", VT[hi][:], va[h], (), [f"VT{hi}"])]

            def prologue(n_):
                h, qb = blocks[n_]
                qi = n_ % 2
                qs = slice(qb * 512, (qb + 1) * 512)
                mm(PF[:], SEL4[:, h * 128:(h + 1) * 128], FROW[:, qs], True, True, ["SEL4", "FROW"], ["PF"])
                add("act", lambda e: e.copy(out=FQ[qi][:, 0, :], in_=PF[:]), ["PF"], [f"FQ{qi}"])
                for d in range(4):
                    add("pool", lambda e, d=d: e.tensor_add(out=FQ[qi][:, 1 + d, :], in0=CM[:, d, :], in1=FQ[qi][:, 0, :]),
                        [f"FQ{qi}", "CM"], [f"FQm{qi}_{d}"])

            def front(i):
                n_, h, qb, kt, nkt = tiles[i]
                hi, qi, a = h % 2, n_ % 2, i % 4
                d = kt - 4 * qb
                mm(PA[a][:], KTs[hi][:, kt * 128:(kt + 1) * 128], QTs[hi][:, qb * 512:(qb + 1) * 512], True, True,
                   [f"KT{hi}", f"QT{hi}"], [f"PA{a}"])
                if d >= 0:
                    fq, fslot = FQ[qi][:, 1 + d, :], f"FQm{qi}_{d}"
                else:
                    fq, fslot = FQ[qi][:, 0, :], f"FQ{qi}"
                dve("tensor_tensor", [f"PA{a}", fslot], [f"TMP{a}"], out=TMP[a][:], in0=PA[a][:], in1=fq, op=ALU.add)
                act(PT[a][:], TMP[a][:], AF.Exp, [f"TMP{a}", "NEGF"], [f"PT{a}"],
                    bias=NEGF[:, kt * 4 + h:kt * 4 + h + 1], scale=1.0)

            def back(i):
                n_, h, qb, kt, nkt = tiles[i]
                hi, qi, a = h % 2, n_ % 2, i % 4
                mm(PO[qi][:], VT[hi][:, kt, :], PT[a][:], kt == 0, kt == nkt - 1, [f"VT{hi}", f"PT{a}"], [f"PO{qi}"])
                mm(PL[0][:], ONESB[:], PT[a][:], kt == 0, kt == nkt - 1, ["ONESB", f"PT{a}"], ["PL0"])
                if kt == nkt - 1:
                    dve("reciprocal", ["PL0"], ["RL"], out=RL[:], in_=PL[0][:])
                    dve("tensor_tensor", [f"PO{qi}", "RL"], [f"OA{qi}"], out=OA[qi][:], in0=PO[qi][:], in1=RL[:], op=ALU.mult)
                    for k in range(4):
                        act(OS[qi][:, k, :], OA[qi][:], AF.Identity, [f"OA{qi}", "COLS"], [f"OS{qi}"], scale=COLS[:, 11 + k:12 + k])
                    quarter, off = (qb * 512) // QT, (qb * 512) % QT
                    for k in range(4):
                        r0 = quarter * 2048 + k * 512 + h * 128
                        dma("sp", rs_a[r0:r0 + 128, off:off + 512], OS[qi][:, k, :], [f"OS{qi}"], [])

            hl = load_head(0) + load_head(1)
            rsb = add("pool", lambda e: e.collective_compute("ReduceScatter", ALU.add, replica_groups=RG, ins=[rs_b], outs=[mT_b], dma_qos="P3"),
                      (), (), kind="cc", after=hl)
            prologue(0)
            for i in range(len(tiles) + LA):
                if i < len(tiles):
                    n_, h, qb, kt, nkt = tiles[i]
                    if kt == 0 and n_ + 1 < len(blocks):
                        prologue(n_ + 1)
                    front(i)
                if i - LA >= 0:
                    back(i - LA)
                    n_, h, qb, kt, nkt = tiles[i - LA]
                    if qb == NBA - 1 and kt == nkt - 1 and h + 2 < 4:
                        load_head(h + 2)
        Sc.barrier()

        def alloc_res(st):
            c = {"XR": [sbt(st, f"XR{i}", [128, 512], F32) for i in range(2)],
                 "ER": [sbt(st, f"ER{i}", [128, 512], F32) for i in range(2)],
                 "OF": [sbt(st, f"OF{i}", [128, 512], F32) for i in range(2)],
                 "OB": [sbt(st, f"OB{i}", [128, 512], BF16) for i in range(2)], "n": 0}
            return c

        def epi_res(prev, dst):
            def pre(c, pi, tt, bi):
                i = c.setdefault("pn", 0) % 2
                c["pn"] += 1
                r0, c0 = bi * 512 + tt * 128, pi * 512
                dma("sp", c["XR"][i][:], prev[r0:r0 + 128, c0:c0 + 512], (), [f"XR{i}"])

            def epi(c, ps, pname, pi, tt, bi):
                i = c["n"] % 2
                c["n"] += 1
                r0, c0 = bi * 512 + tt * 128, pi * 512
                dve("tensor_tensor", [pname, f"XR{i}"], [f"OF{i}"], out=c["OF"][i][:], in0=ps[:], in1=c["XR"][i][:], op=ALU.add)
                dma("sp", dst[r0:r0 + 128, c0:c0 + 512], c["OF"][i][:], [f"OF{i}"], [])
            epi.pre = pre
            return epi

        def ple_side(st):
            WPP = sbt(st, "WPP", [128, 2, D], BF16)
            GB = sbt(st, "GBp", [128, D], F32)
            PQ = [sbt(st, f"PQ{i}", [128, PLE], F32) for i in range(2)]
            P16 = [sbt(st, f"P16{i}", [128, PLE], BF16) for i in range(2)]
            PTt = [sbt(st, f"PTt{i}", [128, 2, 128], BF16) for i in range(2)]
            EPs = [sbt(st, f"EP{i}", [128, D], F32) for i in range(2)]
            JK = sbt(st, "JKp", [128, D], BF16)
            SSq = [sbt(st, f"SSp{i}", [128, 4], F32) for i in range(2)]
            PSs = [pst(st, f"PSp{i}", [128, 512], F32) for i in range(2)]
            PBp = pst(st, "PBp", [128, 1024], BF16)
            dma("pool", WPP[:], w_pp.rearrange("(k p) n -> p k n", p=128), (), ["WPP"])
            dma("sp", GB[:], g_rows[3:4, :].partition_broadcast(128), (), ["GBp"])
            yield
            for t in range(QT // 128):
                i = t % 2
                EP = EPs[i]
                EPn = f"EP{i}"
                dma("act", PQ[i][:], pq[t * 128:(t + 1) * 128, :], (), [f"PQ{i}"])
                dve("tensor_copy", [f"PQ{i}"], [f"P16{i}"], out=P16[i][:], in_=PQ[i][:])
                yield
                for kc in range(2):
                    tr(PBp[:, kc * 128:(kc + 1) * 128], P16[i][:, kc * 128:(kc + 1) * 128], IDN[:], [f"P16{i}"], ["PBp"])
                add("act", lambda e, i=i: e.copy(out=PTt[i][:], in_=PBp[:, 0:256].rearrange("p (k m) -> p k m", k=2)), ["PBp"], [f"PTt{i}"])
                yield
                for nb in range(8):
                    p_ = nb % 2
                    for kc in range(2):
                        mm(PSs[p_][:], PTt[i][:, kc, :], WPP[:, kc, nb * 512:(nb + 1) * 512], kc == 0, kc == 1,
                           [f"PTt{i}", "WPP"], [f"PSp{p_}"])
                    add("act", lambda e, nb=nb, p_=p_, EP=EP: e.copy(out=EP[:, nb * 512:(nb + 1) * 512], in_=PSs[p_][:]),
                        [f"PSp{p_}"], [EPn])
                    if nb % 2 == 1:
                        yield
                act(JK[:], EP[:], AF.Square, [EPn], ["JKp", f"SSa{i}"], accum_out=SSq[i][:, 0:1])
                act(SSq[i][:, 1:2], SSq[i][:, 0:1], AF.Sqrt, [f"SSa{i}"], [f"SSb{i}"], scale=1.0 / D, bias=EPS)
                dve("reciprocal", [f"SSb{i}"], [f"SSc{i}"], out=SSq[i][:, 2:3], in_=SSq[i][:, 1:2])
                dve("scalar_tensor_tensor", [EPn, f"SSc{i}", "GBp"], [EPn], out=EP[:], in0=EP[:],
                    scalar=SSq[i][:, 2:3], in1=GB[:], op0=ALU.mult, op1=ALU.mult)
                dma("pool", eS[t * 128:(t + 1) * 128, :], EP[:], [EPn], [])
                yield

        mTb3 = mT_b.rearrange("(k p) m -> k p m", p=128)
        mTa3 = mT_a.rearrange("(k p) m -> k p m", p=128)
        rs_ops = {}

        def issue_rsa(init_ids):
            rs_ops["a"] = add("pool", lambda e: e.collective_compute("ReduceScatter", ALU.add, replica_groups=RG, ins=[rs_a],
                                                                      outs=[mT_a], dma_qos="P3"), (), (), kind="cc", after=init_ids)

        gemm_pass(mTb3, QT, w_out[2048:4096, :],
                  [(pi * 512, 512, "T", epi_res(xq, h1)) for pi in range(8)], alloc_res, KC=16, a_after=[rsb],
                  a_view=lambda bi: mTb3[:, :, bi * 512:(bi + 1) * 512].rearrange("k p m -> p k m"), post_load=issue_rsa)
        gemm_pass(mTa3, QT, w_out[0:2048, :],
                  [(pi * 512, 512, "T", epi_res(h1, h1)) for pi in range(8)], alloc_res, KC=16, a_after=[rs_ops["a"]],
                  side=ple_side, nps=5,
                  a_view=lambda bi: mTa3[:, :, bi * 512:(bi + 1) * 512].rearrange("k p m -> p k m"))

        norm_transpose(h1, umT, QT, 1, "H")

        def epi_up(c, ps, pname, pi, ch, bi):
            i = c["n"] % 2
            c["n"] += 1
            act(c["OF"][i][:], ps[:], AF.Relu, [pname], [f"OF{i}"])
            dve("tensor_tensor", [f"OF{i}"], [f"OB{i}"], out=c["OB"][i][:], in0=c["OF"][i][:], in1=c["OF"][i][:], op=ALU.mult)
            dma("sp", hidT[bi, :, pi * 4 + ch, :], c["OB"][i][:], [f"OB{i}"], [])

        gemm_pass(umT, QT, w_up, [(pi * 512, 512, "F", epi_up) for pi in range(32)], alloc_res)
        for q in range(4):
            gemm_pass(hidT, QT, w_down[q * D:(q + 1) * D, :],
                      [(pi * 512, 512, "T", epi_res(h1 if q == 0 else h2, h2)) for pi in range(8)], alloc_res,
                      a_view=lambda bi, q=q: hidT[bi, :, q * 32:(q + 1) * 32, :])

        norm_transpose(h2, upT, QT, 2, "K")

        def epi_gate(c, ps, pname, pi, tt, bi):
            i = c["n"] % 2
            c["n"] += 1
            r0, c0 = bi * 512 + tt * 128, pi * 512
            dma("sp", c["XR"][i][:], h2[r0:r0 + 128, c0:c0 + 512], (), [f"XR{i}"])
            dma("sp", c["ER"][i][:], eS[r0:r0 + 128, c0:c0 + 512], (), [f"ER{i}"])
            act(c["OF"][i][:], ps[:], AF.Sigmoid, [pname], [f"OF{i}"])
            dve("tensor_tensor", [f"OF{i}", f"ER{i}"], [f"OF{i}"], out=c["OF"][i][:], in0=c["OF"][i][:], in1=c["ER"][i][:], op=ALU.mult)
            dve("tensor_tensor", [f"OF{i}", f"XR{i}"], [f"OF{i}"], out=c["OF"][i][:], in0=c["OF"][i][:], in1=c["XR"][i][:], op=ALU.add)
            dma("sp", out[r0:r0 + 128, c0:c0 + 512], c["OF"][i][:], [f"OF{i}"], [])

        gemm_pass(upT, QT, w_gate, [(pi * 512, 512, "T", epi_gate) for pi in range(8)], alloc_res)

        for name, dst in dbg_out.items():
            src = dbg_src[name]
            dma("sp", dst, src, (), [])
        Sc.emit(top)
        print("ops per engine:", Sc.stats, flush=True)
    return nc


def host_consts():
    ident = np.eye(128, dtype=np.float32).astype(ml_dtypes.bfloat16)
    p = np.arange(128)[:, None]
    j = np.arange(512)[None, :]
    cm = np.zeros((128, 4, 512), np.float32)
    for d in range(4):
        cm[:, d, :] = np.where(j >= 128 * d + p, 0.0, NEG)
    s = np.arange(128)[:, None]
    t = np.arange(128)[None, :]
    hm = ((s // 64 == t // 64) & (s <= t)).astype(np.float32)
    rm = np.ones((128, 512), np.float32)
    rm[:, ::64] = 0.0
    sel4 = np.zeros((4, 4, 128), np.float32)
    for h in range(4):
        sel4[h, h, :] = 1.0
    return {"ident": ident, "cmask": cm.reshape(128, 2048), "hmask": hm, "rmask": rm,
            "sel4": sel4.reshape(4, 512), "id4": np.eye(4, dtype=np.float32)}


def make_in_maps(inp, S):
    QT = S // 4
    f = lambda a: np.ascontiguousarray(np.asarray(a, dtype=np.float32))
    x, p = f(inp["x"]), f(inp["p"])
    w_in = f(inp["w_in"])[0]
    consts = host_consts()
    g_rows = np.stack([f(inp["norm_mix_g"])[0], f(inp["norm_mlp_g"])[0], f(inp["ple_norm_g"])[0], f(inp["ple_post_g"])[0]])
    lbl = f(inp["hgrn_lb_logits"])
    shared = {"w_out": f(inp["w_out"])[0], "w_up": f(inp["w_up"])[0], "w_down": f(inp["w_down"])[0],
              "w_gate": f(inp["w_ple_gate"])[0], "w_pp": f(inp["w_ple_proj"])[0], "g_rows": g_rows}
    shared.update(consts)
    w_in_r = []
    for r in range(4):
        sl = lambda o: w_in[:, o + r * 512:o + (r + 1) * 512]
        w_in_r.append(np.ascontiguousarray(np.concatenate(
            [sl(0), w_in[:, 6144 + 4 * r:6144 + 4 * r + 4], sl(2048), sl(4096), sl(6160), sl(8208), sl(10256), sl(12304)], axis=1)))
    maps = []
    for c in range(8):
        b, r = c // 4, c % 4
        cols = np.zeros((128, 16), np.float32)
        cols[:, 0] = f(inp["fox_q_norm_g"])[0]
        cols[:, 1] = f(inp["fox_k_norm_g"])[0]
        cols[:, 2] = f(inp["hgrn_norm_g"])[0]
        cols[:, 3:7] = lbl[0].reshape(16, 128)[4 * r:4 * r + 4].T
        cols[:, 7:11] = lbl[1].reshape(16, 128)[4 * r:4 * r + 4].T
        cols[:, 11 + r] = 1.0
        m = dict(shared)
        m.update({"xb": x[b], "xq": np.ascontiguousarray(x[b, r * QT:(r + 1) * QT]),
                  "pq": np.ascontiguousarray(p[0, b, r * QT:(r + 1) * QT]), "w_in": w_in_r[r], "cols": cols,
                  "fbias": np.ascontiguousarray(f(inp["fox_f_bias"])[0, 4 * r:4 * r + 4].reshape(4, 1))})
        maps.append(m)
    return maps


_NC_CACHE = {}


def kernel(**inputs):
    S = int(np.asarray(inputs["x"]).shape[1])
    QT = S // 4
    if S not in _NC_CACHE:
        _NC_CACHE[S] = build_nc(S)
    nc = _NC_CACHE[S]
    maps = make_in_maps(inputs, S)
    res = run_bass_kernel_spmd(nc, maps, core_ids=list(range(8)))
    outp = np.empty((2, S, D), np.float32)
    for c in range(8):
        b, r = c // 4, c % 4
        outp[b, r * QT:(r + 1) * QT] = res.results[c]["out"]
    return outp
```

```python
import math
from contextlib import ExitStack

import numpy as np
import ml_dtypes
import concourse.bass as bass
import concourse.mybir as mybir
from concourse.bass_utils import run_bass_kernel_spmd

F32 = mybir.dt.float32
BF16 = mybir.dt.bfloat16
AF = mybir.ActivationFunctionType
ALU = mybir.AluOpType

D = 4096
HD = 128
DFF = 16384
PLE = 256
EPS = 1e-6
NEG = -1.0e30
HG_OFFSET = 16


class Sched:
    ENGS = ("pe", "act", "dve", "pool", "sp")

    def __init__(self, nc, nsem=4, ndsem=8):
        self.nc = nc
        self.ops = []
        self.deps = []
        self.last_writer = {}
        self.readers = {}
        self.nsem = nsem
        self.ndsem = ndsem
        self.last_c = {}
        self.last_d = {e: [] for e in self.ENGS}
        self.excl = set()

    def add(self, eng, fn, reads=(), writes=(), kind="c", after=()):
        i = len(self.ops)
        d = set(after)
        if self.excl:
            ex = [s for s in reads if s in self.excl]
            if ex:
                reads = [s for s in reads if s not in self.excl]
                writes = list(writes) + ex
        for s in reads:
            w = self.last_writer.get(s)
            if w is not None:
                d.add(w)
        for s in writes:
            w = self.last_writer.get(s)
            if w is not None:
                d.add(w)
            d.update(self.readers.get(s, ()))
        for s in reads:
            self.readers.setdefault(s, []).append(i)
        for s in writes:
            self.last_writer[s] = i
            self.readers[s] = []
        d.discard(i)
        self.ops.append((eng, fn, kind))
        self.deps.append(d)
        if kind == "d":
            self.last_d[eng] = (self.last_d[eng] + [i])[-self.ndsem:]
        elif kind != "cc":
            self.last_c[eng] = i
        return i

    def dma(self, eng, out, in_, reads=(), writes=(), after=()):
        return self.add(eng, lambda e: e.dma_start(out=out, in_=in_), reads, writes, kind="d", after=after)

    def barrier(self):
        L = set(self.last_c.values())
        for e in self.ENGS:
            L.update(self.last_d[e])
        for e in self.ENGS:
            i = self.add(e, None, (), (), kind="n")
            self.deps[i] = set(x for x in L if x != i)
        self.last_writer = {}
        self.readers = {}

    def emit(self, stack):
        nc = self.nc
        ops, deps = self.ops, self.deps
        n = len(ops)
        has_dep = [False] * n
        for i in range(n):
            for d in deps[i]:
                if ops[d][0] == "pe" and ops[i][0] == "pe" and ops[d][2] == "c" and ops[i][2] == "c":
                    continue
                has_dep[d] = True
        csem = {e: [stack.enter_context(nc.semaphore(f"c_{e}_{k}")) for k in range(self.nsem)] for e in self.ENGS}
        dsem = {e: [stack.enter_context(nc.semaphore(f"d_{e}_{k}")) for k in range(self.ndsem)] for e in self.ENGS}
        sig = [None] * n
        ccount = {e: 0 for e in self.ENGS}
        dcount = {e: 0 for e in self.ENGS}
        throttle = [None] * n
        for i, (eng, fn, kind) in enumerate(ops):
            if kind == "d":
                j = dcount[eng]
                dcount[eng] += 1
                slot, rnd = j % self.ndsem, j // self.ndsem
                sig[i] = (dsem[eng][slot], 16 * (rnd + 1), "d", eng, j)
                if rnd > 0:
                    throttle[i] = (dsem[eng][slot], 16 * rnd)
            elif kind == "cc":
                ccsem = stack.enter_context(nc.semaphore(f"cc_{i}"))
                sig[i] = (ccsem, 1, "x", eng, 0)
            elif kind == "c" and has_dep[i]:
                k = ccount[eng]
                ccount[eng] += 1
                sig[i] = (csem[eng][k % self.nsem], k // self.nsem + 1, "c", eng, k)
        per_eng = {e: [] for e in self.ENGS}
        for i, (eng, fn, kind) in enumerate(ops):
            per_eng[eng].append(i)
        self.stats = {e: len(per_eng[e]) for e in self.ENGS}
        final_d = dict(dcount)
        cc_final = []

        def make_body(eng):
            idxs = per_eng[eng]

            def body(e):
                waited = {}
                known = {}

                def wait(sem, val):
                    key = id(sem)
                    if waited.get(key, 0) >= val:
                        return
                    e.wait_ge(sem, val)
                    waited[key] = val

                for i in idxs:
                    _, fn, kind = ops[i]
                    if throttle[i] is not None:
                        wait(*throttle[i])
                    cmax = {}
                    dmax = {}
                    for d in deps[i]:
                        s = sig[d]
                        if s is None:
                            continue
                        sem, val, skind, peng, k = s
                        if skind == "c":
                            if peng == "pe" and eng == "pe" and kind == "c":
                                continue
                            if k > cmax.get(peng, (-1,))[0]:
                                cmax[peng] = (k, sem, val)
                        else:
                            key = id(sem)
                            if val > dmax.get(key, (0,))[0]:
                                dmax[key] = (val, sem)
                    for peng, (k, sem, val) in cmax.items():
                        if known.get(peng, -1) >= k:
                            continue
                        wait(sem, val)
                        known[peng] = k
                    for key, (val, sem) in dmax.items():
                        wait(sem, val)
                    if fn is None:
                        continue
                    ins = fn(e)
                    s = sig[i]
                    if s is not None:
                        ins.then_inc(s[0], 16 if s[2] == "d" else 1)
                    if kind == "cc":
                        cc_final.append(s)
                if eng == "sp":
                    for q in self.ENGS:
                        cnt = final_d[q]
                        for slot in range(min(cnt, self.ndsem)):
                            uses = (cnt - slot + self.ndsem - 1) // self.ndsem
                            wait(dsem[q][slot], 16 * uses)

            return body

        block = stack.enter_context(nc.Block())
        block.tensor(make_body("pe"))
        block.scalar(make_body("act"))
        block.vector(make_body("dve"))
        block.gpsimd(make_body("pool"))
        block.sync(make_body("sp"))


def build_nc(S, dbg=()):
    QT = S // 4
    NBA = S // 512
    NBQ = QT // 512
    NKT = S // 128
    nc = bass.Bass("TRN2", target_bir_lowering=False)
    dt_in = lambda name, shape, dt=F32: nc.dram_tensor(name, list(shape), dt, kind="ExternalInput").ap()
    dt_sc = lambda name, shape, dt: nc.dram_tensor(name, list(shape), dt, kind="Internal").ap()

    xb = dt_in("xb", [S, D])
    xq = dt_in("xq", [QT, D])
    pq = dt_in("pq", [QT, PLE])
    w_in = dt_in("w_in", [D, 3588])
    w_out = dt_in("w_out", [D, D])
    w_up = dt_in("w_up", [D, DFF])
    w_down = dt_in("w_down", [DFF, D])
    w_gate = dt_in("w_gate", [D, D])
    w_pp = dt_in("w_pp", [PLE, D])
    g_rows = dt_in("g_rows", [4, D])
    cols = dt_in("cols", [128, 16])
    fbias = dt_in("fbias", [4, 1])
    ident_in = dt_in("ident", [128, 128], BF16)
    cmask_in = dt_in("cmask", [128, 4 * 512])
    hmask_in = dt_in("hmask", [128, 128])
    rmask_in = dt_in("rmask", [128, 512])
    sel4_in = dt_in("sel4", [4, 4 * 128])
    id4_in = dt_in("id4", [4, 4])
    out = nc.dram_tensor("out", [QT, D], F32, kind="ExternalOutput").ap()

    uT = dt_sc("uT", [S // 512, 128, 32, 512], BF16)
    qa_raw = dt_sc("qa_raw", [4, 128, S], F32)
    ka_raw = dt_sc("ka_raw", [4, 128, S], F32)
    qT = dt_sc("qT", [4, 128, S], BF16)
    kT = dt_sc("kT", [4, 128, S], BF16)
    va = dt_sc("va", [4, 128, S // 128, 128], BF16)
    fa_raw = dt_sc("fa_raw", [4, S], F32)
    qbT = dt_sc("qbT", [4, 128, S], F32)
    sgT = dt_sc("sgT", [4, 128, S], F32)
    gbT = dt_sc("gbT", [4, 128, S], F32)
    vb = dt_sc("vb", [4, 128, S // 128, 128], BF16)
    rs_a = dt_sc("rs_a", [4 * 2048, QT], BF16)
    rs_b = dt_sc("rs_b", [4 * 2048, QT], BF16)
    mT_a = dt_sc("mT_a", [2048, QT], BF16)
    mT_b = dt_sc("mT_b", [2048, QT], BF16)
    h1 = dt_sc("h1", [QT, D], F32)
    umT = dt_sc("umT", [QT // 512, 128, 32, 512], BF16)
    hidT = dt_sc("hidT", [QT // 512, 128, 128, 512], BF16)
    h2 = dt_sc("h2", [QT, D], F32)
    upT = dt_sc("upT", [QT // 512, 128, 32, 512], BF16)
    eS = dt_sc("eS", [QT, D], F32)
    dbg_out = {}
    dbg_src = {}
    for name in dbg:
        src = locals()[name]
        dbg_src[name] = src
        dbg_out[name] = nc.dram_tensor("dbg_" + name, list(src.shape), src.dtype, kind="ExternalOutput").ap()

    with ExitStack() as top:
        Sc = Sched(nc, ndsem=12)
        add, dma = Sc.add, Sc.dma

        def mm(o, lhsT, rhs, start, stop, r, w):
            add("pe", lambda e: e.matmul(o, lhsT=lhsT, rhs=rhs, start=start, stop=stop), r, w)

        def tr(o, in_, ident, r, w):
            add("pe", lambda e: e.transpose(out=o, in_=in_, identity=ident), r, w)

        def act(o, in_, func, r, w, **kw):
            add("act", lambda e: e.activation(out=o, in_=in_, func=func, **kw), r, w)

        def dve(method, r, w, **kw):
            add("dve", lambda e: getattr(e, method)(**kw), r, w)

        uid = [0]

        def sbt(st, name, shape, dt):
            uid[0] += 1
            return st.enter_context(nc.sbuf_tensor(f"{name}_{uid[0]}", list(shape), dt))

        def pst(st, name, shape, dt):
            uid[0] += 1
            Sc.excl.add(name)
            return st.enter_context(nc.psum_tensor(f"{name}_{uid[0]}", list(shape), dt))
        IDN = sbt(top, "IDN", [128, 128], BF16)
        COLS = sbt(top, "COLS", [128, 16], F32)
        CD = sbt(top, "CD", [128, 24], F32)
        ONESF = sbt(top, "ONESF", [128, 128], F32)
        ONESB = sbt(top, "ONESB", [128, 128], BF16)
        zst = ExitStack()
        ZT = sbt(zst, "ZT", [128, 2048], BF16)
        dma("sp", IDN[:], ident_in, (), ["IDN"])
        dma("sp", COLS[:], cols, (), ["COLS"])
        add("dve", lambda e: e.memset(ONESF[:], 1.0 / 128), (), ["ONESF"])
        add("dve", lambda e: e.memset(ONESB[:], 1.0), (), ["ONESB"])
        add("dve", lambda e: e.memset(ZT[:], 0.0), (), ["ZT"])
        add("dve", lambda e: e.tensor_scalar_mul(out=CD[:, 0:1], in0=COLS[:, 0:1], scalar1=1.0 / math.sqrt(HD)), ["COLS"], ["CD0"])
        add("dve", lambda e: e.tensor_sub(out=CD[:, 13:17], in0=COLS[:, 3:7], in1=COLS[:, 7:11]), ["COLS"], ["CD13"])
        act(CD[:, 1:5], CD[:, 13:17], AF.Sigmoid, ["CD13"], ["CD1"])
        act(CD[:, 5:9], CD[:, 13:17], AF.Sigmoid, ["CD13"], ["CD5"], scale=-1.0)
        add("dve", lambda e: e.tensor_scalar_mul(out=CD[:, 9:13], in0=CD[:, 5:9], scalar1=-1.0), ["CD5"], ["CD9"])
        zw = min(QT, 2048)
        Sc.barrier()
        zst.close()

        def norm_transpose(src, dstT, M, grow, tag):
            with ExitStack() as st:
                X = [sbt(st, f"X{i}", [128, D], F32) for i in range(4)]
                GB = sbt(st, "GB", [128, D], F32)
                JK = sbt(st, "JK", [128, D], BF16)
                U = [sbt(st, f"U{i}", [128, D], BF16) for i in range(4)]
                UT = [sbt(st, f"UT{i}", [128, 32, 512], BF16) for i in range(2)]
                SSq = [sbt(st, f"SS{i}", [128, 4], F32) for i in range(4)]
                PB = [pst(st, f"PB{i}", [128, 1024], BF16) for i in range(4)]
                dma("sp", GB[:], g_rows[grow:grow + 1, :].partition_broadcast(128), (), ["GB"])
                nt = M // 128
                for t_ in range(min(3, nt)):
                    dma("sp", X[t_][:], src[t_ * 128:(t_ + 1) * 128, :], (), [f"X{t_}"])

                def stats(t):
                    i = t % 4
                    act(JK[:], X[i][:], AF.Square, [f"X{i}"], ["JK", f"SSa{i}"], accum_out=SSq[i][:, 0:1])
                    act(SSq[i][:, 1:2], SSq[i][:, 0:1], AF.Sqrt, [f"SSa{i}"], [f"SSb{i}"], scale=1.0 / D, bias=EPS)

                def recip(t):
                    i = t % 4
                    dve("reciprocal", [f"SSb{i}"], [f"SSc{i}"], out=SSq[i][:, 2:3], in_=SSq[i][:, 1:2])

                stats(0)
                recip(0)
                for t in range(nt):
                    i = t % 4
                    g, tt = t // 4, t % 4
                    gi = g % 2
                    if t + 3 < nt:
                        dma("sp", X[(t + 3) % 4][:], src[(t + 3) * 128:(t + 4) * 128, :], (), [f"X{(t + 3) % 4}"])
                    if t + 1 < nt:
                        stats(t + 1)
                    dve("scalar_tensor_tensor", [f"X{i}", f"SSc{i}", "GB"], [f"U{i}"], out=U[i][:], in0=X[i][:],
                        scalar=SSq[i][:, 2:3], in1=GB[:], op0=ALU.mult, op1=ALU.mult)
                    if t + 1 < nt:
                        recip(t + 1)
                    for q in range(4):
                        pb = (t * 4 + q) % 4
                        for j in range(8):
                            kc = q * 8 + j
                            tr(PB[pb][:, j * 128:(j + 1) * 128], U[i][:, kc * 128:(kc + 1) * 128], IDN[:],
                               [f"U{i}", "IDN"], [f"PB{pb}"])
                        o = UT[gi][:, q * 8:(q + 1) * 8, tt * 128:(tt + 1) * 128]
                        src_ps = PB[pb][:].rearrange("p (j m) -> p j m", j=8)
                        if q % 2 == 0:
                            add("act", lambda e, o=o, s_=src_ps: e.copy(out=o, in_=s_), [f"PB{pb}"], [f"UT{gi}"])
                        else:
                            add("dve", lambda e, o=o, s_=src_ps: e.tensor_copy(out=o, in_=s_), [f"PB{pb}"], [f"UT{gi}"])
                    if tt == 3:
                        dma("pool", dstT[g], UT[gi][:], [f"UT{gi}"], [])
            Sc.barrier()

        def gemm_pass(aT, M, w, panels, extra_alloc=None, KC=32, side=None, a_after=(), nps=6, side_from=0,
                      a_view=None, post_load=None):
            with ExitStack() as st:
                nblk = M // 512
                resident = nblk <= 4
                wmax = max(p_[1] for p_ in panels)
                WP = [sbt(st, f"WP{i}", [128, KC, (wmax + 7) // 8 * 8], BF16) for i in range(2)]
                AB = [sbt(st, f"AB{i}", [128, KC, 512], BF16) for i in range(nblk if resident else 3)]
                PS = [pst(st, f"PS{i}", [128, 512], F32) for i in range(nps)]
                ctx = extra_alloc(st) if extra_alloc else None
                sgen = side(st) if side else None
                psi = 0
                seq = [(pi, bi) for pi in range(len(panels)) for bi in range(nblk)]

                def load_w(pi):
                    c0, ncol, _, _ = panels[pi]
                    wi = pi % 2
                    ids = []
                    for kg in range(KC // 8):
                        ids.append(dma("pool", WP[wi][:, kg * 8:(kg + 1) * 8, 0:ncol],
                                       w[kg * 1024:(kg + 1) * 1024, c0:c0 + ncol].rearrange("(k p) n -> p k n", p=128),
                                       (), [f"WP{wi}_{kg}"]))
                    return ids

                def load_a(n):
                    pi, bi = seq[n]
                    ai = bi if resident else n % 3
                    src = a_view(bi) if a_view else aT[bi]
                    ids = []
                    for kg in range(KC // 8):
                        ids.append(dma("sp", AB[ai][:, kg * 8:(kg + 1) * 8, :], src[:, kg * 8:(kg + 1) * 8, :], (), [f"AB{ai}_{kg}"],
                                       after=a_after))
                    return ids

                def side_step(pi):
                    if sgen is not None and pi >= side_from:
                        next(sgen, None)

                init_ids = load_w(0)
                if resident:
                    for b_ in range(nblk):
                        init_ids += load_a(b_)
                else:
                    init_ids += load_a(0)
                    init_ids += load_a(1)
                if post_load:
                    post_load(init_ids)
                if hasattr(panels[0][3], "pre"):
                    panels[0][3].pre(ctx, 0, 0, 0)
                for n, (pi, bi) in enumerate(seq):
                    c0, ncol, orient, epi = panels[pi]
                    wi = pi % 2
                    ai = bi if resident else n % 3
                    if bi == 0 and pi + 1 < len(panels):
                        load_w(pi + 1)
                    if not resident and n + 2 < len(seq):
                        load_a(n + 2)
                    nsub = (ncol + 127) // 128 if orient == "F" else 4
                    for sub in range(nsub):
                        ps = PS[psi % nps]
                        pname = f"PS{psi % nps}"
                        psi += 1
                        if orient == "F":
                            cw = min(128, ncol - sub * 128)
                            for kc in range(KC):
                                mm(ps[0:cw, :], WP[wi][:, kc, sub * 128:sub * 128 + cw], AB[ai][:, kc, :], kc == 0, kc == KC - 1,
                                   [f"WP{wi}_{kc // 8}", f"AB{ai}_{kc // 8}"], [pname])
                        else:
                            for kc in range(KC):
                                mm(ps[:, 0:ncol], AB[ai][:, kc, sub * 128:(sub + 1) * 128], WP[wi][:, kc, 0:ncol], kc == 0, kc == KC - 1,
                                   [f"WP{wi}_{kc // 8}", f"AB{ai}_{kc // 8}"], [pname])
                        nxt = None
                        if sub + 1 < nsub:
                            nxt = (pi, sub + 1, bi)
                        elif n + 1 < len(seq):
                            nxt = (seq[n + 1][0], 0, seq[n + 1][1])
                        if nxt is not None and hasattr(panels[nxt[0]][3], "pre"):
                            panels[nxt[0]][3].pre(ctx, *nxt)
                        epi(ctx, ps, pname, pi, sub, bi)
                        side_step(pi)
                if sgen is not None:
                    for _ in sgen:
                        pass
            Sc.barrier()

        norm_transpose(xb, uT, S, 0, "A")

        def alloc_B(st):
            c = {}
            c["OF"] = [sbt(st, f"OF{i}", [128, 512], F32) for i in range(3)]
            c["OB"] = [sbt(st, f"OB{i}", [128, 512], BF16) for i in range(2)]
            c["n"] = 0
            return c

        def epi_F(dst, func, tag=None):
            def epi(c, ps, pname, pi, ch, bi):
                i = c["n"] % 3
                c["n"] += 1
                o = c["OF"][i]
                act(o[:], ps[:], func, [pname], [f"OF{i}"])
                dma("sp", dst[ch, :, bi * 512:(bi + 1) * 512], o[:], [f"OF{i}"], [f"{tag}_{ch}_{bi}"] if tag else [])
            return epi

        def epi_T16(dst):
            def epi(c, ps, pname, pi, tt, bi):
                i = c["n"] % 2
                c["n"] += 1
                o = c["OB"][i]
                dve("tensor_copy", [pname], [f"OB{i}"], out=o[:], in_=ps[:])
                kt = bi * 4 + tt
                dma("sp", dst[:, :, kt, :].rearrange("h p d -> p h d"), o[:].rearrange("p (h d) -> p h d", h=4), [f"OB{i}"], [])
            return epi

        def epi_fa(c, ps, pname, pi, ch, bi):
            i = c["n"] % 3
            c["n"] += 1
            o = c["OF"][i]
            act(o[0:4, :], ps[0:4, :], AF.Copy, [pname], [f"OF{i}"])
            dma("sp", fa_raw[:, bi * 512:(bi + 1) * 512], o[0:4, :], [f"OF{i}"], [])

        epi_qa = epi_F(qa_raw, AF.Copy, "qa")

        def epi_qa_fa(c, ps, pname, pi, ch, bi):
            (epi_fa if ch == 4 else epi_qa)(c, ps, pname, pi, ch, bi)

        panels_B = [
            (0, 516, "F", epi_qa_fa),
            (516, 512, "F", epi_F(ka_raw, AF.Copy, "ka")),
            (1028, 512, "T", epi_T16(va)),
            (2564, 512, "T", epi_T16(vb)),
            (1540, 512, "F", epi_F(qbT, AF.Silu)),
            (3076, 512, "F", epi_F(gbT, AF.Silu)),
            (2052, 512, "F", epi_F(sgT, AF.Sigmoid)),
        ]
        def c_side(st):
            RAW = [sbt(st, f"RAW{i}", [128, 512], F32) for i in range(3)]
            SQ = [sbt(st, f"SQ{i}", [128, 512], F32) for i in range(3)]
            SD = [sbt(st, f"SD{i}", [128, 512], F32) for i in range(3)]
            O16 = [sbt(st, f"O16{i}", [128, 512], BF16) for i in range(3)]
            PM = [pst(st, f"PM{i}", [128, 512], F32) for i in range(2)]
            items = [(src, dst, gcol, tag, h, bi) for (src, dst, gcol, tag) in
                     ((qa_raw, qT, CD[:, 0:1], "qa"), (ka_raw, kT, COLS[:, 1:2], "ka")) for h in range(4) for bi in range(NBA)]
            N = len(items)

            def sA(n):
                src, dst, gcol, tag, h, bi = items[n]
                i = n % 3
                dma("act", RAW[i][:], src[h, :, bi * 512:(bi + 1) * 512], [f"{tag}_{h}_{bi}"], [f"RAW{i}"])
                act(SQ[i][:], RAW[i][:], AF.Square, [f"RAW{i}"], [f"SQ{i}"])

            def sB(n):
                i, j = n % 3, n % 2
                mm(PM[j][:], ONESF[:], SQ[i][:], True, True, [f"SQ{i}"], [f"PM{j}"])

            def sC(n):
                src, dst, gcol, tag, h, bi = items[n]
                i, j = n % 3, n % 2
                act(SD[i][:], PM[j][:], AF.Sqrt, [f"PM{j}"], [f"SD{i}"], bias=EPS, scale=1.0)
                dve("reciprocal", [f"SD{i}"], [f"SD{i}"], out=SD[i][:], in_=SD[i][:])
                dve("scalar_tensor_tensor", [f"RAW{i}", f"SD{i}"], [f"O16{i}"], out=O16[i][:],
                    in0=RAW[i][:], scalar=gcol, in1=SD[i][:], op0=ALU.mult, op1=ALU.mult)
                dma("pool", dst[h, :, bi * 512:(bi + 1) * 512], O16[i][:], [f"O16{i}"], [])

            for k in range(N + 2):
                if k < N:
                    sA(k)
                if 0 <= k - 1 < N:
                    sB(k - 1)
                if 0 <= k - 2 < N:
                    sC(k - 2)
                yield

        gemm_pass(uT, S, w_in, panels_B, alloc_B, side=c_side, nps=5, side_from=2)

        with ExitStack() as st:
            HM = sbt(st, "HM", [128, 128], F32)
            RM = sbt(st, "RM", [128, 512], F32)
            per = lambda name, shape, dt: [sbt(st, f"{name}{i}", shape, dt) for i in range(4)]
            QB, SG, GG = per("QB", [128, 512], F32), per("SG", [128, 512], F32), per("GG", [128, 512], F32)
            VB = per("VB", [128, 4, 128], BF16)
            LOGF, KBt, Bt, EB, ENB, KE = (per(nm, [128, 512], F32) for nm in ("LOGF", "KBt", "Bt", "EB", "ENB", "KE"))
            QE16, KE16, KD16 = (per(nm, [128, 512], BF16) for nm in ("QE16", "KE16", "KD16"))
            KDT = per("KDT", [128, 4, 128], BF16)
            SQh, SDh, Yh = (per(nm, [128, 512], F32) for nm in ("SQh", "SDh", "Yh"))
            OSh = per("OSh", [128, 4, 512], BF16)
            ST = [[sbt(st, f"ST{h}_{j}", [128, 128], F32) for j in range(2)] for h in range(4)]
            S16 = [[sbt(st, f"S16{h}_{j}", [128, 128], BF16) for j in range(2)] for h in range(4)]
            ATM = per("ATM", [128, 128], BF16)
            PAT = pst(st, "PAT", [128, 512], F32)
            PUs = [pst(st, f"PU{i}", [128, 512], F32) for i in range(2)]
            PMS = pst(st, "PKM", [128, 512], F32)
            PKD = PMS[:].bitcast(BF16)
            POT = [pst(st, f"POT{i}", [128, 512], F32) for i in range(4)]
            dma("sp", HM[:], hmask_in, (), ["HM"])
            dma("sp", RM[:], rmask_in, (), ["RM"])
            for h in range(4):
                add("dve", lambda e, h=h: e.memset(ST[h][0][:], 0.0), (), [f"ST{h}_0"])
                add("dve", lambda e, h=h: e.memset(S16[h][0][:], 0.0), (), [f"S16{h}_0"])

            def group_gen(heads):
                cur = {h: 0 for h in heads}
                for bi in range(NBA):
                    sl = slice(bi * 512, (bi + 1) * 512)
                    for h in heads:
                        i = h
                        dma("sp", QB[i][:], qbT[h, :, sl], (), [f"QB{i}"])
                        dma("sp", SG[i][:], sgT[h, :, sl], (), [f"SG{i}"])
                        dma("sp", GG[i][:], gbT[h, :, sl], (), [f"GG{i}"])
                        dma("sp", VB[i][:], vb[h, :, bi * 4:(bi + 1) * 4, :], (), [f"VB{i}"])
                    yield
                    for h in heads:
                        i = h
                        act(LOGF[i][:], SG[i][:], AF.Ln, [f"SG{i}"], [f"LOGF{i}"], scale=CD[:, 5 + h:6 + h], bias=CD[:, 1 + h:2 + h])
                        act(KBt[i][:], SG[i][:], AF.Identity, [f"SG{i}"], [f"KBt{i}"], scale=CD[:, 9 + h:10 + h], bias=CD[:, 5 + h:6 + h])
                        yield
                        add("dve", lambda e, i=i: e.tensor_tensor_scan(out=Bt[i][:], data0=RM[:], data1=LOGF[i][:], initial=0.0,
                                                                       op0=ALU.mult, op1=ALU.add), ["RM", f"LOGF{i}"], [f"Bt{i}"])
                        act(EB[i][:], Bt[i][:], AF.Exp, [f"Bt{i}"], [f"EB{i}"])
                        act(ENB[i][:], Bt[i][:], AF.Exp, [f"Bt{i}"], [f"ENB{i}"], scale=-1.0)
                        yield
                        dve("tensor_tensor", [f"QB{i}", f"EB{i}"], [f"QE16{i}"], out=QE16[i][:], in0=QB[i][:], in1=EB[i][:], op=ALU.mult)
                        dve("tensor_tensor", [f"KBt{i}", f"ENB{i}"], [f"KE{i}"], out=KE[i][:], in0=KBt[i][:], in1=ENB[i][:], op=ALU.mult)
                        add("act", lambda e, i=i: e.copy(out=KE16[i][:], in_=KE[i][:]), [f"KE{i}"], [f"KE16{i}"])
                        yield
                        for c in range(8):
                            cs = slice(c * 64, (c + 1) * 64)
                            dve("tensor_scalar_mul", [f"KE{i}", f"EB{i}"], [f"KD16{i}_{c}"], out=KD16[i][:, cs], in0=KE[i][:, cs],
                                scalar1=EB[i][:, c * 64 + 63:c * 64 + 64])
                            if c == 3:
                                yield
                        yield
                        hp = h % 2
                        for tt in range(4):
                            tr(PKD[:, hp * 512 + tt * 128:hp * 512 + (tt + 1) * 128], KD16[i][:, tt * 128:(tt + 1) * 128], IDN[:],
                               [f"KD16{i}_{2 * tt}", f"KD16{i}_{2 * tt + 1}", "IDN"], ["PKM"])
                        add("act", lambda e, i=i, hp=hp: e.copy(out=KDT[i][:], in_=PKD[:, hp * 512:(hp + 1) * 512].rearrange("p (t d) -> p t d", t=4)),
                            ["PKM"], [f"KDT{i}"])
                        yield
                    for tt in range(4):
                        ts_ = slice(tt * 128, (tt + 1) * 128)
                        for h in heads:
                            i = h
                            pas = slice(h * 128, (h + 1) * 128)
                            mm(PAT[:, pas], KE16[i][:, ts_], QE16[i][:, ts_], True, True, [f"KE16{i}", f"QE16{i}"], ["PAT"])
                            dve("tensor_tensor", ["PAT", "HM"], [f"ATM{h}"], out=ATM[h][:], in0=PAT[:, pas], in1=HM[:], op=ALU.mult)
                        yield
                        for c in range(2):
                            tok = tt * 128 + c * 64
                            tk = slice(tok, tok + 64)
                            for h in heads:
                                i = h
                                pus = slice(h * 128, (h + 1) * 128)
                                cu = cur[h]
                                nx = 1 - cu
                                mm(POT[h][:, tk], VB[i][:, tt, :], ATM[h][:, c * 64:(c + 1) * 64], True, False,
                                   [f"VB{i}", f"ATM{h}"], [f"POT{h}"])
                                mm(POT[h][:, tk], S16[h][cu][:], QE16[i][:, tk], False, True, [f"S16{h}_{cu}", f"QE16{i}"], [f"POT{h}"])
                                mm(PUs[h % 2][:, pus], KDT[i][c * 64:(c + 1) * 64, tt, :], VB[i][c * 64:(c + 1) * 64, tt, :], True, True,
                                   [f"KDT{i}", f"VB{i}"], [f"PU{h % 2}"])
                                dve("scalar_tensor_tensor", [f"ST{h}_{cu}", f"EB{i}", f"PU{h % 2}"], [f"ST{h}_{nx}"], out=ST[h][nx][:],
                                    in0=ST[h][cu][:], scalar=EB[i][:, tok + 63:tok + 64], in1=PUs[h % 2][:, pus], op0=ALU.mult, op1=ALU.add)
                                add("act", lambda e, h=h, nx=nx: e.copy(out=S16[h][nx][:], in_=ST[h][nx][:]), [f"ST{h}_{nx}"], [f"S16{h}_{nx}"])
                                cur[h] = nx
                            yield
                    for h in heads:
                        i = h
                        act(SQh[i][:], POT[h][:], AF.Square, [f"POT{h}"], [f"SQh{i}"])
                        mm(PMS[:], ONESF[:], SQh[i][:], True, True, ["ONESF", f"SQh{i}"], ["PKM"])
                        act(SDh[i][:], PMS[:], AF.Sqrt, ["PKM"], [f"SDh{i}"], bias=EPS, scale=1.0)
                        yield
                        dve("reciprocal", [f"SDh{i}"], [f"SDh{i}"], out=SDh[i][:], in_=SDh[i][:])
                        dve("scalar_tensor_tensor", [f"POT{h}", f"SDh{i}"], [f"Yh{i}"], out=Yh[i][:], in0=POT[h][:],
                            scalar=COLS[:, 2:3], in1=SDh[i][:], op0=ALU.mult, op1=ALU.mult)
                        dve("tensor_tensor", [f"Yh{i}", f"GG{i}"], [f"Yh{i}"], out=Yh[i][:], in0=Yh[i][:], in1=GG[i][:], op=ALU.mult)
                        yield
                        for k in range(4):
                            act(OSh[i][:, k, :], Yh[i][:], AF.Identity, [f"Yh{i}"], [f"OSh{i}"], scale=COLS[:, 11 + k:12 + k])
                        quarter, off = (bi * 512) // QT, (bi * 512) % QT
                        for k in range(4):
                            r0 = quarter * 2048 + k * 512 + h * 128
                            dma("sp", rs_b[r0:r0 + 128, off:off + 512], OSh[i][:, k, :], [f"OSh{i}"], [])
                        yield

            gA, gB = group_gen((0, 1)), group_gen((2, 3))
            for _ in range(HG_OFFSET):
                next(gA, None)
            doneA = doneB = False
            while not (doneA and doneB):
                if not doneA:
                    try:
                        next(gA)
                    except StopIteration:
                        doneA = True
                if not doneB:
                    try:
                        next(gB)
                    except StopIteration:
                        doneB = True
        Sc.barrier()

        RG = [[0, 1, 2, 3], [4, 5, 6, 7]]

        with ExitStack() as st:
            FROW = sbt(st, "FROW", [4, S], F32)
            NEGF = sbt(st, "NEGF", [128, NKT * 4], F32)
            SEL4 = sbt(st, "SEL4", [4, 512], F32)
            PF = pst(st, "PF", [128, 512], F32)
            with ExitStack() as st2:
                FTMP = sbt(st2, "FTMP", [4, S], F32)
                ONE4 = sbt(st2, "ONE4", [4, S], F32)
                NB4 = sbt(st2, "NB4", [4, 2], F32)
                ID4 = sbt(st2, "ID4", [4, 4], F32)
                dma("sp", FROW[:], fa_raw, (), ["FROW"])
                dma("sp", NB4[:, 0:1], fbias, (), ["NB4"])
                dma("sp", ID4[:], id4_in, (), ["ID4"])
                add("dve", lambda e: e.memset(ONE4[:], 1.0), (), ["ONE4"])
                add("dve", lambda e: e.tensor_scalar_mul(out=NB4[:, 1:2], in0=NB4[:, 0:1], scalar1=-1.0), ["NB4"], ["NB4b"])
                act(FTMP[:], FROW[:], AF.Exp, ["FROW", "NB4b"], ["FTMP"], bias=NB4[:, 1:2], scale=-1.0)
                act(FTMP[:], FTMP[:], AF.Ln, ["FTMP"], ["FTMP"], bias=1.0, scale=1.0)
                add("dve", lambda e: e.tensor_tensor_scan(out=FROW[:], data0=ONE4[:], data1=FTMP[:], initial=0.0,
                                                          op0=ALU.mult, op1=ALU.subtract), ["ONE4", "FTMP"], ["FROW"])
                for kt0 in range(0, NKT, 128):
                    nk = min(NKT, kt0 + 128) - kt0
                    for kt in range(kt0, kt0 + nk):
                        mm(PF[:, (kt - kt0) * 4:(kt - kt0) * 4 + 4], FROW[:, kt * 128:(kt + 1) * 128], ID4[:], True, True,
                           ["FROW", "ID4"], ["PF"])
                    add("act", lambda e, kt0=kt0, nk=nk: e.mul(out=NEGF[:, kt0 * 4:(kt0 + nk) * 4], in_=PF[:, 0:nk * 4], mul=-1.0),
                        ["PF"], ["NEGF"])
            Sc.barrier()
            CM = sbt(st, "CM", [128, 4, 512], F32)
            KTs = [sbt(st, f"KT{i}", [128, S], BF16) for i in range(2)]
            QTs = [sbt(st, f"QT{i}", [128, S], BF16) for i in range(2)]
            VT = [sbt(st, f"VT{i}", [128, NKT, 128], BF16) for i in range(2)]
            FQ = [sbt(st, f"FQ{i}", [128, 5, 512], F32) for i in range(2)]
            TMP = [sbt(st, f"TMP{i}", [128, 512], F32) for i in range(4)]
            PT = [sbt(st, f"PT{i}", [128, 512], BF16) for i in range(4)]
            RL = sbt(st, "RL", [128, 512], F32)
            OA = [sbt(st, f"OA{i}", [128, 512], F32) for i in range(2)]
            OS = [sbt(st, f"OS{i}", [128, 4, 512], BF16) for i in range(2)]
            PA = [pst(st, f"PA{i}", [128, 512], F32) for i in range(4)]
            PO = [pst(st, f"PO{i}", [128, 512], F32) for i in range(2)]
            PL = [pst(st, f"PL{i}", [128, 512], F32) for i in range(1)]
            dma("sp", SEL4[:], sel4_in, (), ["SEL4"])
            dma("sp", CM[:], cmask_in.rearrange("p (j n) -> p j n", j=4), (), ["CM"])

            LA = 3
            blocks = [(h, qb) for h in range(4) for qb in range(NBA)]
            tiles = []
            for n_, (h, qb) in enumerate(blocks):
                nkt = 4 * (qb + 1)
                for kt in range(nkt):
                    tiles.append((n_, h, qb, kt, nkt))

            def load_head(h):
                hi = h % 2
                return [dma("sp", KTs[hi][:], kT[h], (), [f"KT{hi}"]),
                        dma("sp", QTs[hi][:], qT[h], (), [f"QT{hi}"]),
                        dma("sp", VT[hi][:], va[h], (), [f"VT{hi}"])]

            def prologue(n_):
                h, qb = blocks[n_]
                qi = n_ % 2
                qs = slice(qb * 512, (qb + 1) * 512)
                mm(PF[:], SEL4[:, h * 128:(h + 1) * 128], FROW[:, qs], True, True, ["SEL4", "FROW"], ["PF"])
                add("act", lambda e: e.copy(out=FQ[qi][:, 0, :], in_=PF[:]), ["PF"], [f"FQ{qi}"])
                for d in range(4):
                    add("pool", lambda e, d=d: e.tensor_add(out=FQ[qi][:, 1 + d, :], in0=CM[:, d, :], in1=FQ[qi][:, 0, :]),
                        [f"FQ{qi}", "CM"], [f"FQm{qi}_{d}"])

            def front(i):
                n_, h, qb, kt, nkt = tiles[i]
                hi, qi, a = h % 2, n_ % 2, i % 4
                d = kt - 4 * qb
                mm(PA[a][:], KTs[hi][:, kt * 128:(kt + 1) * 128], QTs[hi][:, qb * 512:(qb + 1) * 512], True, True,
                   [f"KT{hi}", f"QT{hi}"], [f"PA{a}"])
                if d >= 0:
                    fq, fslot = FQ[qi][:, 1 + d, :], f"FQm{qi}_{d}"
                else:
                    fq, fslot = FQ[qi][:, 0, :], f"FQ{qi}"
                dve("tensor_tensor", [f"PA{a}", fslot], [f"TMP{a}"], out=TMP[a][:], in0=PA[a][:], in1=fq, op=ALU.add)
                act(PT[a][:], TMP[a][:], AF.Exp, [f"TMP{a}", "NEGF"], [f"PT{a}"],
                    bias=NEGF[:, kt * 4 + h:kt * 4 + h + 1], scale=1.0)

            def back(i):
                n_, h, qb, kt, nkt = tiles[i]
                hi, qi, a = h % 2, n_ % 2, i % 4
                mm(PO[qi][:], VT[hi][:, kt, :], PT[a][:], kt == 0, kt == nkt - 1, [f"VT{hi}", f"PT{a}"], [f"PO{qi}"])
                mm(PL[0][:], ONESB[:], PT[a][:], kt == 0, kt == nkt - 1, ["ONESB", f"PT{a}"], ["PL0"])
                if kt == nkt - 1:
                    dve("reciprocal", ["PL0"], ["RL"], out=RL[:], in_=PL[0][:])
                    dve("tensor_tensor", [f"PO{qi}", "RL"], [f"OA{qi}"], out=OA[qi][:], in0=PO[qi][:], in1=RL[:], op=ALU.mult)
                    for k in range(4):
                        act(OS[qi][:, k, :], OA[qi][:], AF.Identity, [f"OA{qi}", "COLS"], [f"OS{qi}"], scale=COLS[:, 11 + k:12 + k])
                    quarter, off = (qb * 512) // QT, (qb * 512) % QT
                    for k in range(4):
                        r0 = quarter * 2048 + k * 512 + h * 128
                        dma("sp", rs_a[r0:r0 + 128, off:off + 512], OS[qi][:, k, :], [f"OS{qi}"], [])

            hl = load_head(0) + load_head(1)
            rsb = add("pool", lambda e: e.collective_compute("ReduceScatter", ALU.add, replica_groups=RG, ins=[rs_b], outs=[mT_b], dma_qos="P3"),
                      (), (), kind="cc", after=hl)
            prologue(0)
            for i in range(len(tiles) + LA):
                if i < len(tiles):
                    n_, h, qb, kt, nkt = tiles[i]
                    if kt == 0 and n_ + 1 < len(blocks):
                        prologue(n_ + 1)
                    front(i)
                if i - LA >= 0:
                    back(i - LA)
                    n_, h, qb, kt, nkt = tiles[i - LA]
                    if qb == NBA - 1 and kt == nkt - 1 and h + 2 < 4:
                        load_head(h + 2)
        Sc.barrier()

        def alloc_res(st):
            c = {"XR": [sbt(st, f"XR{i}", [128, 512], F32) for i in range(2)],
                 "ER": [sbt(st, f"ER{i}", [128, 512], F32) for i in range(2)],
                 "OF": [sbt(st, f"OF{i}", [128, 512], F32) for i in range(2)],
                 "OB": [sbt(st, f"OB{i}", [128, 512], BF16) for i in range(2)], "n": 0}
            return c

        def epi_res(prev, dst):
            def pre(c, pi, tt, bi):
                i = c.setdefault("pn", 0) % 2
                c["pn"] += 1
                r0, c0 = bi * 512 + tt * 128, pi * 512
                dma("sp", c["XR"][i][:], prev[r0:r0 + 128, c0:c0 + 512], (), [f"XR{i}"])

            def epi(c, ps, pname, pi, tt, bi):
                i = c["n"] % 2
                c["n"] += 1
                r0, c0 = bi * 512 + tt * 128, pi * 512
                dve("tensor_tensor", [pname, f"XR{i}"], [f"OF{i}"], out=c["OF"][i][:], in0=ps[:], in1=c["XR"][i][:], op=ALU.add)
                dma("sp", dst[r0:r0 + 128, c0:c0 + 512], c["OF"][i][:], [f"OF{i}"], [])
            epi.pre = pre
            return epi

        def ple_side(st):
            WPP = sbt(st, "WPP", [128, 2, D], BF16)
            GB = sbt(st, "GBp", [128, D], F32)
            PQ = [sbt(st, f"PQ{i}", [128, PLE], F32) for i in range(2)]
            P16 = [sbt(st, f"P16{i}", [128, PLE], BF16) for i in range(2)]
            PTt = [sbt(st, f"PTt{i}", [128, 2, 128], BF16) for i in range(2)]
            EPs = [sbt(st, f"EP{i}", [128, D], F32) for i in range(2)]
            JK = sbt(st, "JKp", [128, D], BF16)
            SSq = [sbt(st, f"SSp{i}", [128, 4], F32) for i in range(2)]
            PSs = [pst(st, f"PSp{i}", [128, 512], F32) for i in range(2)]
            PBp = pst(st, "PBp", [128, 1024], BF16)
            dma("pool", WPP[:], w_pp.rearrange("(k p) n -> p k n", p=128), (), ["WPP"])
            dma("sp", GB[:], g_rows[3:4, :].partition_broadcast(128), (), ["GBp"])
            yield
            for t in range(QT // 128):
                i = t % 2
                EP = EPs[i]
                EPn = f"EP{i}"
                dma("act", PQ[i][:], pq[t * 128:(t + 1) * 128, :], (), [f"PQ{i}"])
                dve("tensor_copy", [f"PQ{i}"], [f"P16{i}"], out=P16[i][:], in_=PQ[i][:])
                yield
                for kc in range(2):
                    tr(PBp[:, kc * 128:(kc + 1) * 128], P16[i][:, kc * 128:(kc + 1) * 128], IDN[:], [f"P16{i}"], ["PBp"])
                add("act", lambda e, i=i: e.copy(out=PTt[i][:], in_=PBp[:, 0:256].rearrange("p (k m) -> p k m", k=2)), ["PBp"], [f"PTt{i}"])
                yield
                for nb in range(8):
                    p_ = nb % 2
                    for kc in range(2):
                        mm(PSs[p_][:], PTt[i][:, kc, :], WPP[:, kc, nb * 512:(nb + 1) * 512], kc == 0, kc == 1,
                           [f"PTt{i}", "WPP"], [f"PSp{p_}"])
                    add("act", lambda e, nb=nb, p_=p_, EP=EP: e.copy(out=EP[:, nb * 512:(nb + 1) * 512], in_=PSs[p_][:]),
                        [f"PSp{p_}"], [EPn])
                    if nb % 2 == 1:
                        yield
                act(JK[:], EP[:], AF.Square, [EPn], ["JKp", f"SSa{i}"], accum_out=SSq[i][:, 0:1])
                act(SSq[i][:, 1:2], SSq[i][:, 0:1], AF.Sqrt, [f"SSa{i}"], [f"SSb{i}"], scale=1.0 / D, bias=EPS)
                dve("reciprocal", [f"SSb{i}"], [f"SSc{i}"], out=SSq[i][:, 2:3], in_=SSq[i][:, 1:2])
                dve("scalar_tensor_tensor", [EPn, f"SSc{i}", "GBp"], [EPn], out=EP[:], in0=EP[:],
                    scalar=SSq[i][:, 2:3], in1=GB[:], op0=ALU.mult, op1=ALU.mult)
                dma("pool", eS[t * 128:(t + 1) * 128, :], EP[:], [EPn], [])
                yield

        mTb3 = mT_b.rearrange("(k p) m -> k p m", p=128)
        mTa3 = mT_a.rearrange("(k p) m -> k p m", p=128)
        rs_ops = {}

        def issue_rsa(init_ids):
            rs_ops["a"] = add("pool", lambda e: e.collective_compute("ReduceScatter", ALU.add, replica_groups=RG, ins=[rs_a],
                                                                      outs=[mT_a], dma_qos="P3"), (), (), kind="cc", after=init_ids)

        gemm_pass(mTb3, QT, w_out[2048:4096, :],
                  [(pi * 512, 512, "T", epi_res(xq, h1)) for pi in range(8)], alloc_res, KC=16, a_after=[rsb],
                  a_view=lambda bi: mTb3[:, :, bi * 512:(bi + 1) * 512].rearrange("k p m -> p k m"), post_load=issue_rsa)
        gemm_pass(mTa3, QT, w_out[0:2048, :],
                  [(pi * 512, 512, "T", epi_res(h1, h1)) for pi in range(8)], alloc_res, KC=16, a_after=[rs_ops["a"]],
                  side=ple_side, nps=5,
                  a_view=lambda bi: mTa3[:, :, bi * 512:(bi + 1) * 512].rearrange("k p m -> p k m"))

        norm_transpose(h1, umT, QT, 1, "H")

        def epi_up(c, ps, pname, pi, ch, bi):
            i = c["n"] % 2
            c["n"] += 1
            act(c["OF"][i][:], ps[:], AF.Relu, [pname], [f"OF{i}"])
            dve("tensor_tensor", [f"OF{i}"], [f"OB{i}"], out=c["OB"][i][:], in0=c["OF"][i][:], in1=c["OF"][i][:], op=ALU.mult)
            dma("sp", hidT[bi, :, pi * 4 + ch, :], c["OB"][i][:], [f"OB{i}"], [])

        gemm_pass(umT, QT, w_up, [(pi * 512, 512, "F", epi_up) for pi in range(32)], alloc_res)
        for q in range(4):
            gemm_pass(hidT, QT, w_down[q * D:(q + 1) * D, :],
                      [(pi * 512, 512, "T", epi_res(h1 if q == 0 else h2, h2)) for pi in range(8)], alloc_res,
                      a_view=lambda bi, q=q: hidT[bi, :, q * 32:(q + 1) * 32, :])

        norm_transpose(h2, upT, QT, 2, "K")

        def epi_gate(c, ps, pname, pi, tt, bi):
            i = c["n"] % 2
            c["n"] += 1
            r0, c0 = bi * 512 + tt * 128, pi * 512
            dma("sp", c["XR"][i][:], h2[r0:r0 + 128, c0:c0 + 512], (), [f"XR{i}"])
            dma("sp", c["ER"][i][:], eS[r0:r0 + 128, c0:c0 + 512], (), [f"ER{i}"])
            act(c["OF"][i][:], ps[:], AF.Sigmoid, [pname], [f"OF{i}"])
            dve("tensor_tensor", [f"OF{i}", f"ER{i}"], [f"OF{i}"], out=c["OF"][i][:], in0=c["OF"][i][:], in1=c["ER"][i][:], op=ALU.mult)
            dve("tensor_tensor", [f"OF{i}", f"XR{i}"], [f"OF{i}"], out=c["OF"][i][:], in0=c["OF"][i][:], in1=c["XR"][i][:], op=ALU.add)
            dma("sp", out[r0:r0 + 128, c0:c0 + 512], c["OF"][i][:], [f"OF{i}"], [])

        gemm_pass(upT, QT, w_gate, [(pi * 512, 512, "T", epi_gate) for pi in range(8)], alloc_res)

        for name, dst in dbg_out.items():
            src = dbg_src[name]
            dma("sp", dst, src, (), [])
        Sc.emit(top)
        print("ops per engine:", Sc.stats, flush=True)
    return nc


def host_consts():
    ident = np.eye(128, dtype=np.float32).astype(ml_dtypes.bfloat16)
    p = np.arange(128)[:, None]
    j = np.arange(512)[None, :]
    cm = np.zeros((128, 4, 512), np.float32)
    for d in range(4):
        cm[:, d, :] = np.where(j >= 128 * d + p, 0.0, NEG)
    s = np.arange(128)[:, None]
    t = np.arange(128)[None, :]
    hm = ((s // 64 == t // 64) & (s <= t)).astype(np.float32)
    rm = np.ones((128, 512), np.float32)
    rm[:, ::64] = 0.0
    sel4 = np.zeros((4, 4, 128), np.float32)
    for h in range(4):
        sel4[h, h, :] = 1.0
    return {"ident": ident, "cmask": cm.reshape(128, 2048), "hmask": hm, "rmask": rm,
            "sel4": sel4.reshape(4, 512), "id4": np.eye(4, dtype=np.float32)}


def make_in_maps(inp, S):
    QT = S // 4
    f = lambda a: np.ascontiguousarray(np.asarray(a, dtype=np.float32))
    x, p = f(inp["x"]), f(inp["p"])
    w_in = f(inp["w_in"])[0]
    consts = host_consts()
    g_rows = np.stack([f(inp["norm_mix_g"])[0], f(inp["norm_mlp_g"])[0], f(inp["ple_norm_g"])[0], f(inp["ple_post_g"])[0]])
    lbl = f(inp["hgrn_lb_logits"])
    shared = {"w_out": f(inp["w_out"])[0], "w_up": f(inp["w_up"])[0], "w_down": f(inp["w_down"])[0],
              "w_gate": f(inp["w_ple_gate"])[0], "w_pp": f(inp["w_ple_proj"])[0], "g_rows": g_rows}
    shared.update(consts)
    w_in_r = []
    for r in range(4):
        sl = lambda o: w_in[:, o + r * 512:o + (r + 1) * 512]
        w_in_r.append(np.ascontiguousarray(np.concatenate(
            [sl(0), w_in[:, 6144 + 4 * r:6144 + 4 * r + 4], sl(2048), sl(4096), sl(6160), sl(8208), sl(10256), sl(12304)], axis=1)))
    maps = []
    for c in range(8):
        b, r = c // 4, c % 4
        cols = np.zeros((128, 16), np.float32)
        cols[:, 0] = f(inp["fox_q_norm_g"])[0]
        cols[:, 1] = f(inp["fox_k_norm_g"])[0]
        cols[:, 2] = f(inp["hgrn_norm_g"])[0]
        cols[:, 3:7] = lbl[0].reshape(16, 128)[4 * r:4 * r + 4].T
        cols[:, 7:11] = lbl[1].reshape(16, 128)[4 * r:4 * r + 4].T
        cols[:, 11 + r] = 1.0
        m = dict(shared)
        m.update({"xb": x[b], "xq": np.ascontiguousarray(x[b, r * QT:(r + 1) * QT]),
                  "pq": np.ascontiguousarray(p[0, b, r * QT:(r + 1) * QT]), "w_in": w_in_r[r], "cols": cols,
                  "fbias": np.ascontiguousarray(f(inp["fox_f_bias"])[0, 4 * r:4 * r + 4].reshape(4, 1))})
        maps.append(m)
    return maps


_NC_CACHE = {}


def kernel(**inputs):
    S = int(np.asarray(inputs["x"]).shape[1])
    QT = S // 4
    if S not in _NC_CACHE:
        _NC_CACHE[S] = build_nc(S)
    nc = _NC_CACHE[S]
    maps = make_in_maps(inputs, S)
    res = run_bass_kernel_spmd(nc, maps, core_ids=list(range(8)))
    outp = np.empty((2, S, D), np.float32)
    for c in range(8):
        b, r = c // 4, c % 4
        outp[b, r * QT:(r + 1) * QT] = res.results[c]["out"]
    return outp
```
